# Optimizing a Trainium2 kernel written in Bass

```python
import math
import jax, jax.numpy as jnp
from jax import lax
import numpy as np

D_MODEL = 1024
BATCH = 16
SEQ = 4096
DEPTH = 2
DEC_BATCH = 2
DEC_SEQ = 16384
PAST_LEN = 128

GRID_W = 64
WIN_R = 8
WIN_C = 16
QCB = 16
KCB = QCB + WIN_C
D_MIX = D_MODEL
N_MEM = 256
CROSS_HEADS = 4
CROSS_DH = 64
D_CROSS = CROSS_HEADS * CROSS_DH
D_PRIM = D_MIX - D_CROSS
NA_HEADS = 12
NA_DH = D_PRIM // NA_HEADS
HG_HEADS = 6
HG_DK = 128
HG_DV = D_PRIM // HG_HEADS
HG_F = HG_HEADS * HG_DK
HG_CHUNK = 32
D_FF = 4 * D_MODEL
N_MIXERS = 2
N_A = (DEPTH + 1) // 2
N_B = DEPTH // 2
A_IN = 3 * D_PRIM + D_CROSS
B_IN = 3 * HG_F + 2 * D_PRIM + D_CROSS
ALPHA = (2 * DEPTH) ** 0.25
BETA = (8 * DEPTH) ** -0.25
LN_EPS = 1e-5
RMS_EPS = 1e-6

kernel_name = "hybrid_natten_hgrn2_memory_encoder"


def layer_norm(x, g, b):
    xf = x.astype(jnp.float32)
    mu = jnp.mean(xf, axis=-1, keepdims=True)
    var = jnp.mean(jnp.square(xf - mu), axis=-1, keepdims=True)
    return ((xf - mu) * lax.rsqrt(var + LN_EPS) * g + b).astype(x.dtype)


def neighborhood_attention(q, k, v, rpb):
    B, T, H, Dh = q.shape
    rows = T // GRID_W
    kr = min(WIN_R, rows)
    n_cb = GRID_W // QCB
    q = q.reshape(B, rows, n_cb, QCB, H, Dh) * (Dh ** -0.5)
    k = k.reshape(B, rows, GRID_W, H, Dh)
    v = v.reshape(B, rows, GRID_W, H, Dh)
    qcol = jnp.arange(GRID_W).reshape(n_cb, QCB)
    cstart = jnp.clip(qcol - WIN_C // 2, 0, GRID_W - WIN_C)
    kcol = jnp.clip(jnp.arange(n_cb) * QCB - WIN_C // 2, 0, GRID_W - KCB)[:, None] + jnp.arange(KCB)
    kc = kcol[:, None, :]
    col_in = (kc >= cstart[..., None]) & (kc < cstart[..., None] + WIN_C)
    dcol = jnp.clip(kc - qcol[..., None] + WIN_C - 1, 0, 2 * WIN_C - 2)
    rpb_c = rpb[:, :, dcol]

    def one_row(r):
        rs = jnp.clip(r - kr // 2, 0, rows - kr)
        q_r = lax.dynamic_index_in_dim(q, r, axis=1, keepdims=False)
        k_r = lax.dynamic_slice_in_dim(k, rs, kr, axis=1)[:, :, kcol]
        v_r = lax.dynamic_slice_in_dim(v, rs, kr, axis=1)[:, :, kcol]
        drow = rs + jnp.arange(kr) - r + WIN_R - 1
        bias = jnp.transpose(jnp.take(rpb_c, drow, axis=1), (0, 2, 3, 1, 4))
        s = jnp.einsum('bnqhd,bknchd->bhnqkc', q_r, k_r).astype(jnp.float32) + bias
        s = jnp.where(col_in[:, :, None, :], s, -jnp.inf)
        p = jax.nn.softmax(s.reshape(s.shape[:4] + (kr * KCB,)), axis=-1).reshape(s.shape).astype(v.dtype)
        return jnp.einsum('bhnqkc,bknchd->bnqhd', p, v_r)

    out = lax.map(one_row, jnp.arange(rows))
    return jnp.moveaxis(out, 0, 1).reshape(B, T, H, Dh)


def hgrn2_chunk_scan(q, k, v, log_f):
    B, T, H, DK = q.shape
    DV = v.shape[-1]
    n = T // HG_CHUNK

    def to_chunks(a):
        return jnp.transpose(a.reshape(B, n, HG_CHUNK, H, a.shape[-1]), (1, 0, 3, 2, 4))

    lower = jnp.tril(jnp.ones((HG_CHUNK, HG_CHUNK), dtype=bool))

    def step(S, xs):
        qc, kc, vc, gc = xs
        b = jnp.cumsum(gc, axis=2)
        b_last = b[:, :, -1:]
        o_inter = jnp.einsum('bhtk,bhkv->bhtv', qc * jnp.exp(b), S)
        diff = jnp.where(lower[:, :, None], b[:, :, :, None, :] - b[:, :, None, :, :], -jnp.inf)
        A = jnp.einsum('bhtk,bhsk,bhtsk->bhts', qc, kc, jnp.exp(diff))
        o_intra = jnp.einsum('bhts,bhsv->bhtv', A, vc)
        S = jnp.exp(b_last)[..., 0, :, None] * S + jnp.einsum('bhsk,bhsv->bhkv', kc * jnp.exp(b_last - b), vc)
        return S, o_inter + o_intra

    S0 = jnp.zeros((B, H, DK, DV), jnp.float32)
    _, outs = lax.scan(step, S0, (to_chunks(q), to_chunks(k), to_chunks(v), to_chunks(log_f)))
    return jnp.transpose(outs, (1, 0, 3, 2, 4)).reshape(B, T, H, DV)


def hgrn2_bidirectional(q, i, f_fwd, f_bwd, gate, lb, norm_w):
    B, T, _ = q.shape
    f32 = jnp.float32
    qh = jax.nn.silu(q.astype(f32)).reshape(B, T, HG_HEADS, HG_DK)
    vh = i.astype(f32).reshape(B, T, HG_HEADS, HG_DV)
    lbh = lb.reshape(HG_HEADS, HG_DK)

    def key_and_logf(f_logit):
        f = lbh + (1.0 - lbh) * jax.nn.sigmoid(f_logit.astype(f32).reshape(B, T, HG_HEADS, HG_DK))
        return 1.0 - f, jnp.log(f)

    k_f, g_f = key_and_logf(f_fwd)
    k_b, g_b = key_and_logf(f_bwd)
    o_fwd = hgrn2_chunk_scan(qh, k_f, vh, g_f)
    rev = lambda a: jnp.flip(a, axis=1)
    o_bwd = rev(hgrn2_chunk_scan(rev(qh), rev(k_b), rev(vh), rev(g_b)))
    o = o_fwd + o_bwd
    o = o * lax.rsqrt(jnp.mean(o * o, axis=-1, keepdims=True) + RMS_EPS) * norm_w
    return o.reshape(B, T, D_PRIM) * jax.nn.silu(gate.astype(f32))


def memory_attention(q, mem_k, mem_v):
    B, T, _ = q.shape
    qh = q.reshape(B, T, CROSS_HEADS, CROSS_DH) * (CROSS_DH ** -0.5)
    kh = mem_k.reshape(B, N_MEM, CROSS_HEADS, CROSS_DH)
    vh = mem_v.reshape(B, N_MEM, CROSS_HEADS, CROSS_DH)
    s = jnp.einsum('bthd,bmhd->bhtm', qh, kh).astype(jnp.float32)
    p = jax.nn.softmax(s, axis=-1).astype(vh.dtype)
    return jnp.einsum('bhtm,bmhd->bthd', p, vh).reshape(B, T, D_CROSS)


def trunk(x, mem, lower_bounds, w_mem_kv, w_in_a, rpb, w_in_b, hg_norm_w,
          w_out, ln1_g, ln1_b, w_ff1, w_ff2, ln2_g, ln2_b):
    B, T, _ = x.shape
    mem_kv = mem @ w_mem_kv
    mem_k, mem_v = mem_kv[..., :D_CROSS], mem_kv[..., D_CROSS:]
    for layer in range(DEPTH):
        j = layer // N_MIXERS
        if layer % N_MIXERS == 0:
            h = x @ w_in_a[j]
            qa, ka, va, qm = jnp.split(h, [D_PRIM, 2 * D_PRIM, 3 * D_PRIM], axis=-1)
            heads = lambda a: a.reshape(B, T, NA_HEADS, NA_DH)
            prim = neighborhood_attention(heads(qa), heads(ka), heads(va), rpb[j]).reshape(B, T, D_PRIM)
        else:
            h = x @ w_in_b[j]
            qb, ffw, fbw, ib, gb, qm = jnp.split(
                h, [HG_F, 2 * HG_F, 3 * HG_F, 3 * HG_F + D_PRIM, 3 * HG_F + 2 * D_PRIM], axis=-1)
            prim = hgrn2_bidirectional(qb, ib, ffw, fbw, gb, lower_bounds[layer], hg_norm_w[j])
        mix = jnp.concatenate([prim.astype(x.dtype), memory_attention(qm, mem_k, mem_v)], axis=-1) @ w_out[layer]
        x = layer_norm(ALPHA * x + mix, ln1_g[layer], ln1_b[layer])
        hid = jnp.square(jax.nn.relu(x @ w_ff1[layer]))
        x = layer_norm(ALPHA * x + hid @ w_ff2[layer], ln2_g[layer], ln2_b[layer])
    return x


def setup_inputs(seed: int = 0) -> dict:
    key = jax.random.key(seed)
    ks = jax.random.split(key, 20)
    nrm = lambda k, shape, scale: jax.random.normal(k, shape, jnp.float32) * scale
    return {
        "x_prompt": nrm(ks[0], (BATCH, SEQ, D_MODEL), 1.0),
        "x_sample": nrm(ks[1], (DEC_BATCH, DEC_SEQ, D_MODEL), 1.0),
        "mem_prompt": nrm(ks[2], (BATCH, N_MEM, D_MODEL), 1.0),
        "mem_sample": nrm(ks[3], (DEC_BATCH, N_MEM, D_MODEL), 1.0),
        "w_mem_kv": nrm(ks[4], (D_MODEL, 2 * D_CROSS), D_MODEL ** -0.5),
        "w_in_a": nrm(ks[5], (N_A, D_MODEL, A_IN), D_MODEL ** -0.5),
        "rpb": nrm(ks[6], (N_A, NA_HEADS, 2 * WIN_R - 1, 2 * WIN_C - 1), 0.02),
        "w_in_b": nrm(ks[7], (N_B, D_MODEL, B_IN), D_MODEL ** -0.5),
        "lb_logits": nrm(ks[8], (DEPTH, HG_F), 0.1),
        "hg_norm_w": 1.0 + nrm(ks[9], (N_B, HG_DV), 0.02),
        "w_out": nrm(ks[10], (DEPTH, D_MIX, D_MODEL), BETA * D_MIX ** -0.5),
        "ln1_g": 1.0 + nrm(ks[11], (DEPTH, D_MODEL), 0.02),
        "ln1_b": nrm(ks[12], (DEPTH, D_MODEL), 0.02),
        "w_ff1": nrm(ks[13], (DEPTH, D_MODEL, D_FF), D_MODEL ** -0.5),
        "w_ff2": nrm(ks[14], (DEPTH, D_FF, D_MODEL), BETA * D_FF ** -0.5),
        "ln2_g": 1.0 + nrm(ks[15], (DEPTH, D_MODEL), 0.02),
        "ln2_b": nrm(ks[16], (DEPTH, D_MODEL), 0.02),
    }


def reference(x_prompt, x_sample, mem_prompt, mem_sample, w_mem_kv, w_in_a, rpb, w_in_b,
              lb_logits, hg_norm_w, w_out, ln1_g, ln1_b, w_ff1, w_ff2, ln2_g, ln2_b):
    p = jax.nn.softmax(lb_logits.astype(jnp.float32), axis=0)
    lower_bounds = jnp.cumsum(p, axis=0) - p[0]
    y_prompt = trunk(x_prompt, mem_prompt, lower_bounds, w_mem_kv, w_in_a, rpb, w_in_b, hg_norm_w,
                     w_out, ln1_g, ln1_b, w_ff1, w_ff2, ln2_g, ln2_b)
    y_sample = trunk(x_sample, mem_sample, lower_bounds, w_mem_kv, w_in_a, rpb, w_in_b, hg_norm_w,
                     w_out, ln1_g, ln1_b, w_ff1, w_ff2, ln2_g, ln2_b)
    return (y_prompt, y_sample)
```

```python
import numpy as np
from contextlib import ExitStack
import concourse.bass as bass
import concourse.mybir as mybir
from concourse.bass_utils import run_bass_kernel_spmd

F32 = mybir.dt.float32
BF16 = mybir.dt.bfloat16
AF = mybir.ActivationFunctionType
ALU = mybir.AluOpType

D = 1024
DFF = 4096
NCORES = 8
UNITS = 4
UT = 4096
HALO = 256
UTE = UT + 2 * HALO
ALPHA = 4.0 ** 0.25
LN_EPS = 1e-5
RMS_EPS = 1e-6


class Buf:
    __slots__ = ("name", "w", "r")

    def __init__(self, name):
        self.name = name
        self.w = None
        self.r = {}


class Sched:
    def __init__(self, nc, es, same_eng_sync=True):
        self.nc = nc
        self.es = es
        self.eng = {"pe": nc.tensor, "act": nc.scalar, "dve": nc.vector,
                    "pool": nc.gpsimd, "sp": nc.sync}
        self.same = same_eng_sync
        self.semobj = {}
        self.cnt = {}
        self.seen = {e: {} for e in self.eng}
        self.nops = 0

    def sem(self, name):
        if name not in self.semobj:
            self.semobj[name] = self.es.enter_context(self.nc.semaphore("s_" + name))
            self.cnt[name] = 0
        return self.semobj[name]

    def buf(self, name="b"):
        return Buf(name)

    def _waits(self, e, reads, writes, is_dma, extra=()):
        need = {}

        def add(tok, raw):
            s, v, te = tok
            if te == e and not is_dma:
                if e == "pe" or not self.same:
                    return
            if need.get(s, 0) < v:
                need[s] = v
        for b in reads:
            if b.w is not None:
                add(b.w, True)
        for b in writes:
            if b.w is not None:
                add(b.w, False)
            for r in b.r.values():
                add(r, False)
        for tok in extra:
            add(tok, True)
        for s, v in need.items():
            if self.seen[e].get(s, 0) < v:
                self.eng[e].wait_ge(self.semobj[s], v)
                self.seen[e][s] = v

    def _update(self, rk, tok, reads, writes):
        for b in reads:
            b.r[rk] = tok
        for b in writes:
            b.w = tok
            b.r = {}

    def op(self, eng, fn, reads=(), writes=()):
        self._waits(eng, reads, writes, False)
        inst = fn()
        so = self.sem(eng)
        self.cnt[eng] += 1
        inst.then_inc(so, 1)
        tok = (eng, self.cnt[eng], eng)
        self._update(eng, tok, reads, writes)
        self.nops += 1
        return tok

    def dma(self, q, fns, key, reads=(), writes=()):
        if not isinstance(fns, (list, tuple)):
            fns = [fns]
        s = "dma_" + key
        so = self.sem(s)
        prev = ((s, self.cnt[s], None),) if self.cnt[s] > 0 else ()
        self._waits(q, reads, writes, True, extra=prev)
        for f in fns:
            f().then_inc(so, 16)
        self.cnt[s] += 16 * len(fns)
        tok = (s, self.cnt[s], None)
        self._update(s, tok, reads, writes)
        self.nops += 1
        return tok

    def cc(self, fn, key, reads=(), writes=()):
        s = "cc_" + key
        so = self.sem(s)
        prev = ((s, self.cnt[s], None),) if self.cnt[s] > 0 else ()
        self._waits("pool", reads, writes, True, extra=prev)
        fn().then_inc(so, 1)
        self.cnt[s] += 1
        tok = (s, self.cnt[s], None)
        self._update(s, tok, reads, writes)
        return tok

    def barrier(self):
        for e in self.eng:
            for s, v in self.cnt.items():
                if v > 0 and s != e and self.seen[e].get(s, 0) < v:
                    self.eng[e].wait_ge(self.semobj[s], v)
                    self.seen[e][s] = v

    def emit(self, es=None):
        self.nsem = len(self.semobj)


class Ctx:
    pass


_UNIQ = [0]


def sb(es, nc, name, shape, dt):
    _UNIQ[0] += 1
    return es.enter_context(nc.sbuf_tensor(f"sb_{name}_{_UNIQ[0]}", list(shape), dt))


def ps(es, nc, name, shape, dt=F32):
    _UNIQ[0] += 1
    return es.enter_context(nc.psum_tensor(f"ps_{name}_{_UNIQ[0]}", list(shape), dt))


def load_weight_bf16(C, es_outer, wdram, kparts, ncols, name, piece_cols=2048):
    nc, S = C.nc, C.S
    wt = sb(es_outer, nc, name, [128, kparts, ncols], BF16)
    wb = S.buf(name)
    piece_cols = min(piece_cols, ncols)
    npc = ncols // piece_cols
    with ExitStack() as es:
        stg = [sb(es, nc, f"{name}_stg{i}", [128, piece_cols], F32) for i in range(2)]
        stb = [S.buf(f"{name}_stg{i}") for i in range(2)]
        engs = ["dve", "act", "pool"]
        n = 0
        for k in range(kparts):
            for p in range(npc):
                j = n % 2
                src = wdram[k * 128:(k + 1) * 128, p * piece_cols:(p + 1) * piece_cols]
                S.dma("sp", (lambda d=stg[j], s=src: nc.sync.dma_start(out=d[:], in_=s)),
                      f"{name}_stg{j}", reads=(), writes=(stb[j],))
                e = engs[n % 3]
                dst = wt[:, k, p * piece_cols:(p + 1) * piece_cols]
                if e == "dve":
                    S.op("dve", (lambda d=dst, s=stg[j]: nc.vector.tensor_copy(d, s[:])),
                         reads=(stb[j],), writes=())
                elif e == "act":
                    S.op("act", (lambda d=dst, s=stg[j]: nc.scalar.copy(d, s[:])),
                         reads=(stb[j],), writes=())
                else:
                    S.op("pool", (lambda d=dst, s=stg[j]: nc.gpsimd.tensor_copy(d, s[:])),
                         reads=(stb[j],), writes=())
                n += 1
        S.barrier()
    return wt, wb


def layernorm_store(C, T, y, yb, out_tile, outb):
    nc, S = C.nc, C.S
    st, stb, mv, mvb, rs, rsb = T["st"], T["stb"], T["mv"], T["mvb"], T["rs"], T["rsb"]
    for h in range(2):
        S.op("dve", (lambda h=h: nc.vector.bn_stats(st[:, h, :], y[:, h * 512:(h + 1) * 512])),
             reads=(yb,), writes=(stb,))
    S.op("dve", (lambda: nc.vector.bn_aggr(mv[:], st[:].rearrange("p a b -> p (a b)"))),
         reads=(stb,), writes=(mvb,))
    S.op("act", (lambda: nc.scalar.activation(out=rs[:, 0:1], in_=mv[:, 1:2], func=AF.Sqrt,
                                              bias=C.eps_ln[:, 0:1], scale=1.0)),
         reads=(mvb,), writes=(rsb,))
    S.op("dve", (lambda: nc.vector.reciprocal(rs[:, 1:2], rs[:, 0:1])),
         reads=(rsb,), writes=(rsb,))
    S.op("dve", (lambda: nc.vector.tensor_scalar(y[:], y[:], mv[:, 0:1], rs[:, 1:2],
                                                 op0=ALU.subtract, op1=ALU.mult)),
         reads=(yb, mvb, rsb), writes=(yb,))
    S.op("pool", (lambda: nc.gpsimd.tensor_tensor(out=y[:], in0=y[:], in1=T["gb"][:, 0, :], op=ALU.mult)),
         reads=(yb,), writes=(yb,))
    S.op("pool", (lambda: nc.gpsimd.tensor_tensor(out=out_tile, in0=y[:], in1=T["gb"][:, 1, :], op=ALU.add)),
         reads=(yb,), writes=(outb,))


def load_gb(C, es, gi, name):
    nc, S = C.nc, C.S
    t = sb(es, nc, name, [128, 2, D], F32)
    S.dma("sp", (lambda: nc.sync.dma_start(out=t[:], in_=C.d_vecs[:, gi:gi + 2, :])), name, writes=(S.buf(),))
    S.barrier()
    return t


def stage_ffn(C, layer, src, dst):
    nc, S = C.nc, C.S
    TT = 256
    ntiles = (UNITS * UT) // TT
    ntiles = min(ntiles, C.max_tiles) if C.max_tiles else ntiles
    with ExitStack() as es:
        w1, _ = load_weight_bf16(C, es, C.w_ff1[layer], 8, DFF, "w1")
        w2, _ = load_weight_bf16(C, es, C.w_ff2[layer], 32, D, "w2", piece_cols=1024)
        xin = [sb(es, nc, f"f_x{i}", [128, 2, D], F32) for i in range(2)]
        xinb = [S.buf() for _ in range(2)]
        xT = sb(es, nc, "f_xT", [128, 8, TT], BF16)
        xTb = S.buf()
        h1 = sb(es, nc, "f_h1", [128, 32, TT], BF16)
        h1b = [S.buf() for _ in range(32)]
        rl = [sb(es, nc, f"f_rl{i}", [128, TT], F32) for i in range(2)]
        rlb = [S.buf() for _ in range(2)]
        yt = [sb(es, nc, f"f_y{i}", [128, D], F32) for i in range(2)]
        ytb = [S.buf() for _ in range(2)]
        T = dict(st=sb(es, nc, "f_st", [128, 2, 6], F32), stb=S.buf(),
                 mv=sb(es, nc, "f_mv", [128, 2], F32), mvb=S.buf(),
                 rs=sb(es, nc, "f_rs", [128, 2], F32), rsb=S.buf(),
                 gb=load_gb(C, es, 4 * layer + 2, "f_gb"))
        pT = ps(es, nc, "f_pT", [128, 1024])
        pTb = S.buf()
        pH = [ps(es, nc, f"f_pH{i}", [128, 512]) for i in range(2)]
        pHb = [S.buf() for _ in range(2)]
        pY = [ps(es, nc, f"f_pY{i}", [128, 1024]) for i in range(2)]
        pYb = [S.buf() for _ in range(2)]

        def load(t):
            j = t % 2
            s = src[t * TT:(t + 1) * TT, :].rearrange("(s p) d -> p s d", p=128)
            S.dma("sp", (lambda: nc.sync.dma_start(out=xin[j][:], in_=s)), f"f_x{j}",
                  writes=(xinb[j],))

        stop = getattr(C, "stop", 99)
        if stop <= 0:
            S.barrier(); return
        load(0)
        for t in range(ntiles):
            j = t % 2
            if t + 1 < ntiles:
                load(t + 1)
            for s in range(2):
                for k in range(8):
                    S.op("pe", (lambda s=s, k=k: nc.tensor.transpose(
                        pT[:, k * 128:(k + 1) * 128], xin[j][:, s, k * 128:(k + 1) * 128], C.ident_f[:])),
                        reads=(xinb[j],), writes=(pTb,))
                S.op("act" if s == 0 else "dve",
                     (lambda s=s: (nc.scalar.copy if s == 0 else nc.vector.tensor_copy)(
                         xT[:, :, s * 128:(s + 1) * 128],
                         pT[:].rearrange("p (k t) -> p k t", k=8))),
                     reads=(pTb,), writes=(xTb,))
            if stop <= 1:
                continue
            for c in range(32):
                q = c % 2
                pv = pH[q][:, 0:256]
                for k in range(8):
                    S.op("pe", (lambda c=c, k=k, pv=pv: nc.tensor.matmul(
                        pv, w1[:, k, c * 128:(c + 1) * 128], xT[:, k, :], start=(k == 0), stop=(k == 7))),
                        reads=(xTb,), writes=(pHb[q],))
                S.op("act", (lambda pv=pv, q=q: nc.scalar.activation(out=rl[q][:], in_=pv, func=AF.Relu)),
                     reads=(pHb[q],), writes=(rlb[q],))
                S.op("pool", (lambda c=c, q=q: nc.gpsimd.tensor_tensor(
                    out=h1[:, c, :], in0=rl[q][:], in1=rl[q][:], op=ALU.mult)),
                    reads=(rlb[q],), writes=(h1b[c],))
            if stop <= 2:
                continue
            for s in range(2):
                for hf in range(2):
                    for c in range(32):
                        S.op("pe", (lambda s=s, hf=hf, c=c: nc.tensor.matmul(
                            pY[s][:, hf * 512:(hf + 1) * 512], h1[:, c, s * 128:(s + 1) * 128],
                            w2[:, c, hf * 512:(hf + 1) * 512], start=(c == 0), stop=(c == 31))),
                            reads=(h1b[c],), writes=(pYb[s],))
                if stop <= 3:
                    continue
                S.op("dve", (lambda s=s: nc.vector.scalar_tensor_tensor(
                    out=yt[s][:], in0=xin[j][:, s, :], scalar=ALPHA, in1=pY[s][:],
                    op0=ALU.mult, op1=ALU.add)),
                    reads=(xinb[j], pYb[s]), writes=(ytb[s],))
                if stop <= 4:
                    continue
                layernorm_store(C, T, yt[s], ytb[s], yt[s][:], ytb[s])
                if stop <= 5:
                    continue
                d = dst[t * TT + s * 128: t * TT + (s + 1) * 128, :]
                S.dma("sp", (lambda s=s, d=d: nc.sync.dma_start(out=d, in_=yt[s][:])), f"f_o{s}",
                      reads=(ytb[s],))
        S.barrier()


def setup_consts(C, es):
    nc, S = C.nc, C.S
    C.ident_f = sb(es, nc, "ident_f", [128, 128], F32)
    C.ident_b = sb(es, nc, "ident_b", [128, 128], BF16)
    C.eps_ln = sb(es, nc, "eps_ln", [128, 1], F32)
    b = S.buf()
    S.dma("sp", [lambda: nc.sync.dma_start(out=C.ident_f[:], in_=C.d_ident[:, :]),
                 ], "consts", writes=(b,))
    S.op("dve", lambda: nc.vector.tensor_copy(C.ident_b[:], C.ident_f[:]), reads=(b,), writes=(b,))
    S.op("dve", lambda: nc.vector.memset(C.eps_ln[:], LN_EPS), writes=(b,))
    S.barrier()


def setup_persistent(C, es):
    nc, S = C.nc, C.S
    setup_consts(C, es)
    C.memk = sb(es, nc, "memk", [128, UNITS, 2, 256], BF16)
    C.memv = sb(es, nc, "memv", [128, UNITS, 2, 4, 65], BF16)
    C.ones_b = sb(es, nc, "ones_b", [128, 1], BF16)
    C.memb = S.buf("mem")


def stage_memkv(C):
    nc, S = C.nc, C.S
    with ExitStack() as es:
        w, _ = load_weight_bf16(C, es, C.w_mem_kv, 8, 512, "wkv", piece_cols=512)
        mx = sb(es, nc, "m_x", [128, 2, D], F32)
        mxb = S.buf()
        mT = sb(es, nc, "m_T", [128, 8, 256], BF16)
        mTb = S.buf()
        pT = ps(es, nc, "m_pT", [128, 1024])
        pTb = S.buf()
        pA = ps(es, nc, "m_pA", [128, 512])
        pAb = S.buf()
        S.op("dve", lambda: nc.vector.memset(C.memv[:], 1.0), writes=(C.memb,))
        S.op("dve", lambda: nc.vector.memset(C.ones_b[:], 1.0), writes=(C.memb,))
        for u in range(UNITS):
            S.dma("sp", lambda: nc.sync.dma_start(out=mx[:], in_=C.d_mem[u].rearrange("(s p) d -> p s d", p=128)),
                  "m_x", writes=(mxb,))
            for s in range(2):
                def tr():
                    for k in range(8):
                        i = nc.tensor.transpose(pT[:, k * 128:(k + 1) * 128], mx[:, s, k * 128:(k + 1) * 128], C.ident_f[:])
                    return i
                S.op("pe", tr, reads=(mxb,), writes=(pTb,))
                S.op("act", lambda: nc.scalar.copy(mT[:, :, s * 128:(s + 1) * 128], pT[:].rearrange("p (k t) -> p k t", k=8)),
                     reads=(pTb,), writes=(mTb,))
            for c in range(2):
                def mm():
                    for k in range(8):
                        i = nc.tensor.matmul(pA[:, 0:256], w[:, k, c * 128:(c + 1) * 128], mT[:, k, :],
                                             start=(k == 0), stop=(k == 7))
                    return i
                S.op("pe", mm, reads=(mTb,), writes=(pAb,))
                S.op("act", lambda: nc.scalar.copy(C.memk[:, u, c, :], pA[:, 0:256]), reads=(pAb,), writes=(C.memb,))
            for mc in range(2):
                def mm():
                    for k in range(8):
                        i = nc.tensor.matmul(pA[:, 0:256], mT[:, k, mc * 128:(mc + 1) * 128], w[:, k, 256:512],
                                             start=(k == 0), stop=(k == 7))
                    return i
                S.op("pe", mm, reads=(mTb,), writes=(pAb,))
                S.op("dve", lambda: nc.vector.tensor_copy(C.memv[:, u, mc, :, 0:64],
                                                          pA[:, 0:256].rearrange("p (h d) -> p h d", h=4)),
                     reads=(pAb,), writes=(C.memb,))
        S.barrier()


def mem_attn_tile(C, T, u, qm, qmb, tok0, ntok=512):
    nc, S = C.nc, C.S
    PT, PTb, pS, pSb, pO, pOb = T["PT"], T["PTb"], T["pS"], T["pSb"], T["pO"], T["pOb"]
    rd, rdb, cr, crb = T["rd"], T["rdb"], T["cr"], T["crb"]
    n = 0
    for h in range(4):
        p0 = (h % 2) * 64
        for mc in range(2):
            q = n % 2
            S.op("pe", lambda: nc.tensor.matmul(pS[q][:, 0:ntok], C.memk[p0:p0 + 64, u, h // 2, mc * 128:(mc + 1) * 128],
                                                qm[p0:p0 + 64, h // 2, 0:ntok], start=True, stop=True),
                 reads=(qmb,), writes=(pSb[q],))
            S.op("act", lambda: nc.scalar.activation(out=PT[:, h, mc, 0:ntok], in_=pS[q][:, 0:ntok], func=AF.Exp),
                 reads=(pSb[q],), writes=(PTb[h],))
            n += 1
    for s in range(ntok // 128):
        def pv():
            for h in range(4):
                for mc in range(2):
                    i = nc.tensor.matmul(pO[:, h * 65:(h + 1) * 65], PT[:, h, mc, s * 128:(s + 1) * 128],
                                         C.memv[:, u, mc, h, :], start=(mc == 0), stop=(mc == 1))
            return i
        S.op("pe", pv, reads=tuple(PTb), writes=(pOb,))
        pov = pO[:, 0:260].rearrange("p (h d) -> p h d", h=4)
        S.op("dve", lambda: nc.vector.reciprocal(rd[:], pov[:, :, 64]), reads=(pOb,), writes=(rdb,))
        j = s % 2
        S.op("dve", lambda: nc.vector.tensor_tensor(out=cr[j][:], in0=pov[:, :, 0:64],
                                                    in1=rd[:].unsqueeze(2).to_broadcast([128, 4, 64]), op=ALU.mult),
             reads=(pOb, rdb), writes=(crb[j],))
        d = C.mix_s[u, tok0 + s * 128: tok0 + (s + 1) * 128, 768:1024]
        S.dma("sp", lambda: nc.sync.dma_start(out=d, in_=cr[j][:].rearrange("p h d -> p (h d)")), f"cr{j}",
              reads=(crb[j],))


def mem_attn_tiles(C, es, pS, pSb):
    nc, S = C.nc, C.S
    return dict(PT=sb(es, nc, "ma_PT", [128, 4, 2, 512], BF16), PTb=[S.buf() for _ in range(4)],
                pS=pS, pSb=pSb,
                pO=ps(es, nc, "ma_pO", [128, 512]), pOb=S.buf(),
                rd=sb(es, nc, "ma_rd", [128, 4], F32), rdb=S.buf(),
                cr=[sb(es, nc, f"ma_cr{i}", [128, 4, 64], BF16) for i in range(2)], crb=[S.buf() for _ in range(2)])


def stage_inproj_a(C):
    nc, S = C.nc, C.S
    units = range(UNITS) if not getattr(C, "units", None) else C.units
    with ExitStack() as es:
        w, _ = load_weight_bf16(C, es, C.w_in_a, 8, 2560, "wa", piece_cols=1280)
        xin = [sb(es, nc, f"a_x{i}", [128, 4, D], F32) for i in range(2)]
        xinb = [S.buf() for _ in range(2)]
        xT = sb(es, nc, "a_xT", [128, 8, 512], BF16)
        xTb = S.buf()
        qf = sb(es, nc, "a_qf", [128, 6, 512], BF16); qfb = S.buf()
        kf = sb(es, nc, "a_kf", [128, 6, 512], BF16); kfb = S.buf()
        qm = sb(es, nc, "a_qm", [128, 2, 512], BF16); qmb = S.buf()
        va = sb(es, nc, "a_va", [128, 4, 12, 65], BF16); vab = S.buf()
        pT = ps(es, nc, "a_pT", [128, 1024]); pTb = S.buf()
        pA = [ps(es, nc, f"a_pA{i}", [128, 512]) for i in range(2)]; pAb = [S.buf() for _ in range(2)]
        pV = ps(es, nc, "a_pV", [128, 1024]); pVb = S.buf()
        MA = mem_attn_tiles(C, es, pA, pAb)
        S.op("dve", lambda: nc.vector.memset(va[:], 1.0), writes=(vab,))
        tiles = [(u, t) for u in units for t in range(9)]

        def load(i):
            u, t = tiles[i]
            j = i % 2
            if t < 8:
                srcs = [(xin[j][:], C.xe[u, HALO + t * 512: HALO + (t + 1) * 512, :].rearrange("(s p) d -> p s d", p=128))]
            else:
                srcs = [(xin[j][:, 0:2, :], C.xe[u, 0:HALO, :].rearrange("(s p) d -> p s d", p=128)),
                        (xin[j][:, 2:4, :], C.xe[u, HALO + UT:UTE, :].rearrange("(s p) d -> p s d", p=128))]
            S.dma("sp", [(lambda o=o, i_=i_: nc.sync.dma_start(out=o, in_=i_)) for o, i_ in srcs], f"a_x{j}",
                  writes=(xinb[j],))

        load(0)
        na = 0
        for i, (u, t) in enumerate(tiles):
            j = i % 2
            if i + 1 < len(tiles):
                load(i + 1)
            real = t < 8
            for s in range(4):
                def tr():
                    for k in range(8):
                        ins = nc.tensor.transpose(pT[:, k * 128:(k + 1) * 128], xin[j][:, s, k * 128:(k + 1) * 128], C.ident_f[:])
                    return ins
                S.op("pe", tr, reads=(xinb[j],), writes=(pTb,))
                if s % 2 == 0:
                    S.op("act", lambda: nc.scalar.copy(xT[:, :, s * 128:(s + 1) * 128], pT[:].rearrange("p (k t) -> p k t", k=8)),
                         reads=(pTb,), writes=(xTb,))
                else:
                    S.op("dve", lambda: nc.vector.tensor_copy(xT[:, :, s * 128:(s + 1) * 128], pT[:].rearrange("p (k t) -> p k t", k=8)),
                         reads=(pTb,), writes=(xTb,))
            jobs = []
            if real:
                jobs += [(c * 128, qf, qfb, c, 0.125) for c in range(6)]
                jobs += [(2304 + c * 128, qm, qmb, c, 0.125) for c in range(2)]
            jobs += [(768 + c * 128, kf, kfb, c, 1.0) for c in range(6)]
            for (col, dt_, db, c, sc) in jobs:
                q = na % 2
                na += 1

                def mm():
                    for k in range(8):
                        ins = nc.tensor.matmul(pA[q][:, :], w[:, k, col:col + 128], xT[:, k, :], start=(k == 0), stop=(k == 7))
                    return ins
                S.op("pe", mm, reads=(xTb,), writes=(pAb[q],))
                S.op("act", lambda: nc.scalar.activation(out=dt_[:, c, :], in_=pA[q][:, :], func=AF.Copy, scale=sc),
                     reads=(pAb[q],), writes=(db,))
            for s in range(4):
                def mm():
                    for hf, (c0, c1) in enumerate(((0, 512), (512, 768))):
                        for k in range(8):
                            ins = nc.tensor.matmul(pV[:, c0:c1], xT[:, k, s * 128:(s + 1) * 128], w[:, k, 1536 + c0:1536 + c1],
                                                   start=(k == 0), stop=(k == 7))
                    return ins
                S.op("pe", mm, reads=(xTb,), writes=(pVb,))
                S.op("dve", lambda: nc.vector.tensor_copy(va[:, s, :, 0:64], pV[:, 0:768].rearrange("p (h d) -> p h d", h=12)),
                     reads=(pVb,), writes=(vab,))
            if real:
                k0 = HALO + t * 512
                S.dma("sp", lambda: nc.sync.dma_start(
                    out=C.qT_s[u].rearrange("(c p) t -> p c t", p=128)[:, :, t * 512:(t + 1) * 512], in_=qf[:]), "a_qf", reads=(qfb,))
                S.dma("sp", lambda: nc.sync.dma_start(
                    out=C.kT_s[u].rearrange("(c p) t -> p c t", p=128)[:, :, k0:k0 + 512], in_=kf[:]), "a_kf", reads=(kfb,))
                S.dma("sp", lambda: nc.sync.dma_start(
                    out=C.v_s[u, k0:k0 + 512, :].rearrange("(s p) f -> p s f", p=128), in_=va[:].rearrange("p s h d -> p s (h d)")),
                    "a_va", reads=(vab,))
                mem_attn_tile(C, MA, u, qm, qmb, t * 512)
            else:
                kv = C.kT_s[u].rearrange("(c p) t -> p c t", p=128)
                S.dma("sp", [lambda: nc.sync.dma_start(out=kv[:, :, 0:HALO], in_=kf[:, :, 0:256]),
                               lambda: nc.sync.dma_start(out=kv[:, :, HALO + UT:UTE], in_=kf[:, :, 256:512])], "a_kf", reads=(kfb,))
                vv = va[:].rearrange("p s h d -> p s (h d)")
                S.dma("sp", [lambda: nc.sync.dma_start(out=C.v_s[u, 0:HALO, :].rearrange("(s p) f -> p s f", p=128), in_=vv[:, 0:2, :]),
                               lambda: nc.sync.dma_start(out=C.v_s[u, HALO + UT:UTE, :].rearrange("(s p) f -> p s f", p=128), in_=vv[:, 2:4, :])],
                      "a_va", reads=(vab,))
        S.barrier()


MASKV = -30000.0
SPECIAL_PAIRS = (0, 1, 30, 31)


def pair_chunks(p):
    if p == 0:
        return list(range(0, 6))
    if p == 31:
        return list(range(30, 36))
    return list(range(p, p + 5))


def na_table(rpb0, row_offset, rows_total, p):
    tbl = np.full((2, 64, 12, 6, 2, 64), MASKV, np.float32)
    qc = np.arange(64)
    cstart = np.clip(qc - 8, 0, 48)
    kc = np.arange(64)
    colok = (kc[:, None] >= cstart[None, :]) & (kc[:, None] < cstart[None, :] + 16)
    dcol = np.clip(kc[:, None] - qc[None, :] + 15, 0, 30)
    for j, m in enumerate(pair_chunks(p)):
        for e in range(2):
            KR = row_offset + 2 * m + e - 4
            if KR < 0 or KR >= rows_total:
                continue
            for a in range(2):
                R = row_offset + 2 * p + a
                rs = min(max(R - 4, 0), rows_total - 8)
                if not (rs <= KR < rs + 8):
                    continue
                drow = KR - R + 7
                blk = rpb0[:, drow, :][:, dcol]
                blk = np.where(colok[None], blk, np.float32(MASKV))
                tbl[e, :, :, j, a, :] = np.transpose(blk, (1, 0, 2))
    return tbl.reshape(128, 12, 6, 128)


def stage_na(C):
    nc, S = C.nc, C.S
    units = range(UNITS) if not getattr(C, "units", None) else C.units
    pairs = getattr(C, "pairs", None)
    with ExitStack() as es:
        wout, _ = load_weight_bf16(C, es, C.w_out[0], 8, D, "wo", piece_cols=1024)
        T = dict(st=sb(es, nc, "n_st", [128, 2, 6], F32), stb=S.buf(),
                 mv=sb(es, nc, "n_mv", [128, 2], F32), mvb=S.buf(),
                 rs=sb(es, nc, "n_rs", [128, 2], F32), rsb=S.buf(),
                 gb=load_gb(C, es, 0, "n_gb"))
        tbI = sb(es, nc, "n_tbI", [128, 12, 6, 128], BF16); tbIb = S.buf()
        tbS = sb(es, nc, "n_tbS", [128, 12, 6, 128], BF16); tbSb = S.buf()
        stg = [sb(es, nc, f"n_stg{i}", [128, 768], F32) for i in range(2)]; stgb = [S.buf() for _ in range(2)]
        kh = sb(es, nc, "n_kh", [128, 6, 2560], BF16); khb = S.buf()
        vh = sb(es, nc, "n_vh", [128, 20, 780], BF16); vhb = S.buf()
        qh = [sb(es, nc, f"n_qh{i}", [128, 6, 512], BF16) for i in range(2)]; qhb = [S.buf() for _ in range(2)]
        PT = sb(es, nc, "n_PT", [128, 12, 768], BF16); PTb = [S.buf() for _ in range(12)]
        x0 = [sb(es, nc, f"n_x0{i}", [128, D], F32) for i in range(2)]; x0b = [S.buf() for _ in range(2)]
        mix = [sb(es, nc, f"n_mix{i}", [128, D], BF16) for i in range(2)]; mixb = [S.buf() for _ in range(2)]
        mixT = sb(es, nc, "n_mixT", [128, 8, 128], BF16); mixTb = S.buf()
        yt = [sb(es, nc, f"n_y{i}", [128, D], F32) for i in range(2)]; ytb = [S.buf() for _ in range(2)]
        rd = sb(es, nc, "n_rd", [128, 12], F32); rdb = S.buf()
        pS = [ps(es, nc, f"n_pS{i}", [128, 1024]) for i in range(2)]; pSb = [S.buf() for _ in range(2)]
        pO = ps(es, nc, "n_pO", [128, 1024]); pOb = S.buf()
        pX = ps(es, nc, "n_pX", [128, 512]); pXb = S.buf()
        pZ = ps(es, nc, "n_pZ", [128, 512]); pZb = S.buf()
        pXh = pX[:].bitcast(BF16)

        ncast = [0]

        def load_table(dst, dstb, idx):
            for h in range(12):
                j = ncast[0] % 2
                ncast[0] += 1
                S.dma("sp", lambda: nc.sync.dma_start(out=stg[j][:], in_=C.natab[idx, :, h * 768:(h + 1) * 768]),
                      f"n_stg{j}", writes=(stgb[j],))
                S.op("pool", lambda: nc.gpsimd.tensor_copy(dst[:, h, :, :], stg[j][:].rearrange("p (j q) -> p j q", j=6)),
                     reads=(stgb[j],), writes=(dstb,))

        load_table(tbI, tbIb, 0)
        pend = [None]
        npair = [0]

        def outproj(u, p, jj):
            def f():
                def tr():
                    for k in range(8):
                        ins = nc.tensor.transpose(pXh[:, k * 128:(k + 1) * 128], mix[jj][:, k * 128:(k + 1) * 128], C.ident_b[:])
                    return ins
                S.op("pe", tr, reads=(mixb[jj],), writes=(pXb,))
                S.op("act", lambda: nc.scalar.copy(mixT[:], pXh.rearrange("p (k t) -> p k t", k=8)), reads=(pXb,), writes=(mixTb,))
                for hf, (pz, pzb) in enumerate(((pZ, pZb), (pX, pXb))):
                    def mm():
                        for k in range(8):
                            ins = nc.tensor.matmul(pz[:, :], mixT[:, k, :], wout[:, k, hf * 512:(hf + 1) * 512],
                                                   start=(k == 0), stop=(k == 7))
                        return ins
                    S.op("pe", mm, reads=(mixTb,), writes=(pzb,))
                    S.op("dve", lambda: nc.vector.scalar_tensor_tensor(
                        out=yt[jj][:, hf * 512:(hf + 1) * 512], in0=x0[jj][:, hf * 512:(hf + 1) * 512], scalar=ALPHA,
                        in1=pz[:, :], op0=ALU.mult, op1=ALU.add), reads=(x0b[jj], pzb), writes=(ytb[jj],))
                layernorm_store(C, T, yt[jj], ytb[jj], yt[jj][:], ytb[jj])
                S.dma("sp", lambda: nc.sync.dma_start(out=C.x1_s[u * UT + p * 128: u * UT + (p + 1) * 128, :], in_=yt[jj][:]),
                      f"n_y{jj}", reads=(ytb[jj],))
            return f

        for u in units:
            for half in range(2):
                plist = [p for p in range(half * 16, half * 16 + 16) if pairs is None or p in pairs]
                if not plist:
                    continue
                tok0 = half * 2048
                kv = C.kT_s[u].rearrange("(c p) t -> p c t", p=128)
                S.dma("sp", [(lambda c=c: nc.sync.dma_start(out=kh[:, c, :], in_=kv[:, c, tok0:tok0 + 2560])) for c in range(6)],
                      "n_kh", writes=(khb,))
                vv = C.v_s[u, tok0:tok0 + 2560, :].rearrange("(s p) f -> p s f", p=128)
                S.dma("sp", [(lambda g=g: nc.sync.dma_start(out=vh[:, g * 5:(g + 1) * 5, :], in_=vv[:, g * 5:(g + 1) * 5, :])) for g in range(4)],
                      "n_vh", writes=(vhb,))
                qcur = None
                for p in plist:
                    jj = npair[0] % 2
                    npair[0] += 1
                    pp = p - half * 16
                    if qcur is None or p // 4 != qcur[0]:
                        qi = (p // 4) % 2
                        qv = C.qT_s[u].rearrange("(c p) t -> p c t", p=128)
                        S.dma("sp", lambda: nc.sync.dma_start(out=qh[qi][:], in_=qv[:, :, (p // 4) * 512:(p // 4) * 512 + 512]),
                              f"n_qh{qi}", writes=(qhb[qi],))
                        qcur = (p // 4, qi)
                    qi = qcur[1]
                    if p in SPECIAL_PAIRS:
                        idx = 1 + 4 * u + SPECIAL_PAIRS.index(p)
                        load_table(tbS, tbSb, idx)
                        tb, tbb = tbS, tbSb
                    else:
                        tb, tbb = tbI, tbIb
                    S.dma("sp", [lambda: nc.sync.dma_start(out=x0[jj][:], in_=C.xe[u, HALO + p * 128:HALO + (p + 1) * 128, :]),
                                 lambda: nc.sync.dma_start(out=mix[jj][:, 768:1024], in_=C.mix_s[u, p * 128:(p + 1) * 128, 768:1024])],
                          f"n_in{jj}", writes=(x0b[jj], mixb[jj]))
                    chunks = pair_chunks(p)
                    nj = len(chunks)
                    qoff = (p % 4) * 128

                    def qk(h):
                        hp = (h % 2) * 64
                        hc = h // 2
                        psS = pS[h % 2]

                        def mm():
                            for j, m in enumerate(chunks):
                                lm = m - half * 16
                                nc.tensor.matmul(psS[:, j * 128:(j + 1) * 128], kh[hp:hp + 64, hc, lm * 128:(lm + 1) * 128],
                                                 qh[qi][hp:hp + 64, hc, qoff:qoff + 128], start=True, stop=False)
                                ins = nc.tensor.matmul(psS[:, j * 128:(j + 1) * 128], C.ident_b[:], tb[:, h, j, :],
                                                       start=False, stop=True)
                            return ins
                        S.op("pe", mm, reads=(khb, qhb[qi], tbb), writes=(pSb[h % 2],))
                        S.op("act", lambda: nc.scalar.activation(out=PT[:, h, 0:nj * 128], in_=psS[:, 0:nj * 128], func=AF.Exp),
                             reads=(pSb[h % 2],), writes=(PTb[h],))

                    def pv(h):
                        def mm():
                            for j, m in enumerate(chunks):
                                lm = m - half * 16
                                oc = (h // 6) * 512 + (h % 6) * 65
                                ins = nc.tensor.matmul(pO[:, oc:oc + 65], PT[:, h, j * 128:(j + 1) * 128],
                                                       vh[:, lm, h * 65:(h + 1) * 65], start=(j == 0), stop=(j == nj - 1))
                            return ins
                        S.op("pe", mm, reads=(PTb[h], vhb), writes=(pOb,))

                    qk(0)
                    qk(1)
                    for h in range(12):
                        if h + 2 < 12:
                            qk(h + 2)
                        pv(h)
                        if h == 4 and pend[0] is not None:
                            pend[0]()
                            pend[0] = None
                    pov = pO[:].rearrange("p (g x) -> p g x", g=2)[:, :, 0:390].rearrange("p g (h d) -> p g h d", h=6)
                    rdv = rd[:].rearrange("p (g h) -> p g h", g=2)
                    S.op("dve", lambda: nc.vector.reciprocal(rdv, pov[:, :, :, 64]), reads=(pOb,), writes=(rdb,))
                    S.op("dve", lambda: nc.vector.tensor_tensor(
                        out=mix[jj][:, 0:768].rearrange("p (g h d) -> p g h d", g=2, h=6), in0=pov[:, :, :, 0:64],
                        in1=rdv.unsqueeze(3).to_broadcast([128, 2, 6, 64]), op=ALU.mult),
                        reads=(pOb, rdb), writes=(mixb[jj],))
                    pend[0] = outproj(u, p, jj)
        if pend[0] is not None:
            pend[0]()
        S.barrier()


HC = 64
NCH = UT // HC


def stage_inproj_b(C):
    nc, S = C.nc, C.S
    units = range(UNITS) if not getattr(C, "units", None) else C.units
    TT = 256
    NS = TT // 128
    NC4 = TT // HC
    src = C.x2_s
    with ExitStack() as es:
        w, _ = load_weight_bf16(C, es, C.w_in_b, 8, 4096, "wb")
        lg = sb(es, nc, "b_lg", [128, 2, 6], F32)
        lb = sb(es, nc, "b_lb", [128, 6], F32)
        oml = sb(es, nc, "b_oml", [128, 6], F32)
        noml = sb(es, nc, "b_noml", [128, 6], F32)
        nwb = sb(es, nc, "b_nwb", [128, 768], F32)
        mask = sb(es, nc, "b_mask", [128, TT], F32)
        cb = S.buf()
        S.dma("sp", [lambda: nc.sync.dma_start(out=lg[:], in_=C.lb_logits[:, :, :]),
                     lambda: nc.sync.dma_start(out=nwb[:], in_=C.d_nwb[:, :])], "b_c", writes=(cb,))
        S.op("dve", lambda: nc.vector.tensor_tensor(out=lb[:], in0=lg[:, 1, :], in1=lg[:, 0, :], op=ALU.subtract), reads=(cb,), writes=(cb,))
        S.op("act", lambda: nc.scalar.activation(out=lb[:], in_=lb[:], func=AF.Sigmoid), reads=(cb,), writes=(cb,))
        S.op("dve", lambda: nc.vector.tensor_scalar(oml[:], lb[:], -1.0, 1.0, op0=ALU.mult, op1=ALU.add), reads=(cb,), writes=(cb,))
        S.op("dve", lambda: nc.vector.tensor_scalar(noml[:], oml[:], -1.0, None, op0=ALU.mult), reads=(cb,), writes=(cb,))
        S.op("dve", lambda: nc.vector.memset(mask[:], 1.0), writes=(cb,))
        S.op("dve", lambda: nc.vector.memset(mask[:].rearrange("p (c t) -> p c t", t=HC)[:, :, 0:1], 0.0), writes=(cb,))
        S.barrier()

        xin = [sb(es, nc, f"b_x{i}", [128, NS, D], F32) for i in range(2)]; xinb = [S.buf() for _ in range(2)]
        xT = sb(es, nc, "b_xT", [128, 8, TT], BF16); xTb = S.buf()
        sq = sb(es, nc, "b_sq", [128, 6, TT], F32); sqb = [S.buf() for _ in range(6)]
        sg = sb(es, nc, "b_sg", [128, 12, TT], F32); sgb = [S.buf() for _ in range(12)]
        wk = {n: sb(es, nc, f"b_{n}", [128, TT], F32) for n in ("g", "k", "b", "t3", "t4", "e1", "e2", "e3")}
        wkb = {n: S.buf() for n in wk}
        outs = {n: [sb(es, nc, f"b_o{n}{i}", [128, TT], BF16) for i in range(2)] for n in ("QF", "KF", "KHF", "QB", "KB", "KHB")}
        outb = {n: [S.buf() for _ in range(2)] for n in outs}
        ll = sb(es, nc, "b_ll", [128, 2, 6, NC4], F32); llb = S.buf()
        qm = sb(es, nc, "b_qm", [128, 2, TT], BF16); qmb = S.buf()
        vo = sb(es, nc, "b_vo", [128, NS, 768], BF16); vob = S.buf()
        go = sb(es, nc, "b_go", [128, NS, 768], F32); gob = S.buf()
        gs = sb(es, nc, "b_gs", [128, 768], F32); gsb = S.buf()
        pT = ps(es, nc, "b_pT", [128, 1024]); pTb = S.buf()
        pA = [ps(es, nc, f"b_pA{i}", [128, 512]) for i in range(2)]; pAb = [S.buf() for _ in range(2)]
        pV = ps(es, nc, "b_pV", [128, 1024]); pVb = S.buf()
        MA = mem_attn_tiles(C, es, pA, pAb)
        ntile = UT // TT
        tiles = [(u, t) for u in units for t in range(ntile)]
        if getattr(C, "max_tiles", None):
            tiles = tiles[:C.max_tiles]
        dst = dict(QF=C.QF, KF=C.KF, KHF=C.KHF, QB=C.QB, KB=C.KB, KHB=C.KHB)

        def load(i):
            u, t = tiles[i]
            j = i % 2
            S.dma("sp", lambda: nc.sync.dma_start(
                out=xin[j][:], in_=src[u * UT + t * TT: u * UT + (t + 1) * TT, :].rearrange("(s p) d -> p s d", p=128)),
                f"b_x{j}", writes=(xinb[j],))

        load(0)
        na = 0
        no = 0
        for i, (u, t) in enumerate(tiles):
            j = i % 2
            if i + 1 < len(tiles):
                load(i + 1)
            for s in range(NS):
                def tr():
                    for k in range(8):
                        ins = nc.tensor.transpose(pT[:, k * 128:(k + 1) * 128], xin[j][:, s, k * 128:(k + 1) * 128], C.ident_f[:])
                    return ins
                S.op("pe", tr, reads=(xinb[j],), writes=(pTb,))
                S.op("dve", lambda: nc.vector.tensor_copy(xT[:, :, s * 128:(s + 1) * 128], pT[:].rearrange("p (k t) -> p k t", k=8)),
                     reads=(pTb,), writes=(xTb,))

            def fm(col, fn):
                nonlocal na
                q = na % 2
                na += 1

                def mm():
                    for k in range(8):
                        ins = nc.tensor.matmul(pA[q][:, 0:TT], w[:, k, col:col + 128], xT[:, k, :], start=(k == 0), stop=(k == 7))
                    return ins
                S.op("pe", mm, reads=(xTb,), writes=(pAb[q],))
                fn(pA[q][:, 0:TT], pAb[q])
            for c in range(6):
                fm(c * 128, lambda pv, pb: S.op("act", lambda: nc.scalar.activation(out=sq[:, c, :], in_=pv, func=AF.Silu),
                                                reads=(pb,), writes=(sqb[c],)))
                for d in range(2):
                    fm(768 * (d + 1) + c * 128, lambda pv, pb: S.op(
                        "act", lambda: nc.scalar.activation(out=sg[:, d * 6 + c, :], in_=pv, func=AF.Sigmoid),
                        reads=(pb,), writes=(sgb[d * 6 + c],)))
            for c in range(2):
                fm(3840 + c * 128, lambda pv, pb: S.op(
                    "act", lambda: nc.scalar.activation(out=qm[:, c, :], in_=pv, func=AF.Copy, scale=0.125),
                    reads=(pb,), writes=(qmb,)))
            for s in range(NS):
                for which in range(2):
                    base = 2304 + which * 768

                    def mm():
                        for (c0, c1) in ((0, 512), (512, 768)):
                            for k in range(8):
                                ins = nc.tensor.matmul(pV[:, c0:c1], xT[:, k, s * 128:(s + 1) * 128], w[:, k, base + c0:base + c1],
                                                       start=(k == 0), stop=(k == 7))
                        return ins
                    S.op("pe", mm, reads=(xTb,), writes=(pVb,))
                    if which == 0:
                        S.op("dve", lambda: nc.vector.tensor_copy(vo[:, s, :], pV[:, 0:768]), reads=(pVb,), writes=(vob,))
                    else:
                        S.op("act", lambda: nc.scalar.activation(out=gs[:], in_=pV[:, 0:768], func=AF.Silu), reads=(pVb,), writes=(gsb,))
                        S.op("pool", lambda: nc.gpsimd.tensor_tensor(out=go[:, s, :], in0=gs[:], in1=nwb[:], op=ALU.mult),
                             reads=(gsb,), writes=(gob,))
            r0 = u * UT + t * TT
            S.dma("sp", lambda: nc.sync.dma_start(out=C.V1[r0:r0 + TT, :].rearrange("(s p) f -> p s f", p=128), in_=vo[:]),
                  "b_vo", reads=(vob,))
            S.dma("sp", lambda: nc.sync.dma_start(out=C.G1[r0:r0 + TT, :].rearrange("(s p) f -> p s f", p=128), in_=go[:]),
                  "b_go", reads=(gob,))
            b3 = wk["b"][:].rearrange("p (c t) -> p c t", t=HC)
            import os as _os
            for c in range(0 if not _os.environ.get("SKIP_P2") else 6, 6):
                for d in range(2):
                    sgi = sg[:, d * 6 + c, :]
                    sgib = sgb[d * 6 + c]
                    S.op("act", lambda: nc.scalar.activation(out=wk["g"][:], in_=sgi, func=AF.Ln, bias=lb[:, c:c + 1], scale=oml[:, c:c + 1]),
                         reads=(sgib,), writes=(wkb["g"],))
                    S.op("dve", lambda: nc.vector.tensor_scalar(wk["k"][:], sgi, noml[:, c:c + 1], oml[:, c:c + 1], op0=ALU.mult, op1=ALU.add),
                         reads=(sgib,), writes=(wkb["k"],))
                    S.op("dve", lambda: nc.vector.tensor_tensor_scan(wk["b"][:], mask[:], wk["g"][:], 0.0, ALU.mult, ALU.add),
                         reads=(wkb["g"],), writes=(wkb["b"],))
                    S.op("dve", lambda: nc.vector.tensor_copy(ll[:, d, c, :], b3[:, :, HC - 1]), reads=(wkb["b"],), writes=(llb,))
                    S.op("dve", lambda: nc.vector.tensor_tensor(
                        out=wk["t3"][:].rearrange("p (c t) -> p c t", t=HC), in0=b3[:, :, HC - 1:HC].to_broadcast([128, NC4, HC]),
                        in1=b3, op=ALU.subtract), reads=(wkb["b"],), writes=(wkb["t3"],))
                    if d == 0:
                        a1, a2, a3 = wk["b"], wk["b"], wk["t3"]
                        a1b, a2b, a3b = wkb["b"], wkb["b"], wkb["t3"]
                        names = ("QF", "KF", "KHF")
                    else:
                        S.op("dve", lambda: nc.vector.tensor_tensor(out=wk["t4"][:], in0=wk["b"][:], in1=wk["g"][:], op=ALU.subtract),
                             reads=(wkb["b"], wkb["g"]), writes=(wkb["t4"],))
                        S.op("dve", lambda: nc.vector.tensor_tensor(out=wk["t3"][:], in0=wk["t3"][:], in1=wk["g"][:], op=ALU.add),
                             reads=(wkb["t3"], wkb["g"]), writes=(wkb["t3"],))
                        a1, a2, a3 = wk["t3"], wk["t3"], wk["t4"]
                        a1b, a2b, a3b = wkb["t3"], wkb["t3"], wkb["t4"]
                        names = ("QB", "KB", "KHB")
                    S.op("act", lambda: nc.scalar.activation(out=wk["e1"][:], in_=a1[:], func=AF.Exp), reads=(a1b,), writes=(wkb["e1"],))
                    S.op("act", lambda: nc.scalar.activation(out=wk["e2"][:], in_=a2[:], func=AF.Exp, scale=-1.0), reads=(a2b,), writes=(wkb["e2"],))
                    S.op("act", lambda: nc.scalar.activation(out=wk["e3"][:], in_=a3[:], func=AF.Exp), reads=(a3b,), writes=(wkb["e3"],))
                    oj = no % 2
                    no += 1
                    for nm, e_, other, ob_ in ((names[0], "e1", sq[:, c, :], sqb[c]), (names[1], "e2", wk["k"][:], wkb["k"]),
                                               (names[2], "e3", wk["k"][:], wkb["k"])):
                        S.op("pool", lambda: nc.gpsimd.tensor_tensor(out=outs[nm][oj][:], in0=other, in1=wk[e_][:], op=ALU.mult),
                             reads=(ob_, wkb[e_]), writes=(outb[nm][oj],))
                        S.dma("sp", lambda: nc.sync.dma_start(out=dst[nm][u, c * 128:(c + 1) * 128, t * TT:(t + 1) * TT], in_=outs[nm][oj][:]),
                              f"b_o{nm}{oj}", reads=(outb[nm][oj],))
            for d, LD in enumerate((C.LF, C.LB)):
                S.dma("sp", lambda: nc.sync.dma_start(
                    out=LD[u].rearrange("(c p) n -> p c n", p=128)[:, :, t * NC4:(t + 1) * NC4], in_=ll[:, d, :, :]),
                    "b_ll", reads=(llb,))
            if not _os.environ.get("SKIP_MA"):
                mem_attn_tile(C, MA, u, qm, qmb, t * TT, ntok=TT)
        S.barrier()


def hg_tiles(C, es):
    nc, S = C.nc, C.S
    H = Ctx()
    H.Sf = sb(es, nc, "h_Sf", [128, 6, 128], F32); H.Sfb = S.buf()
    H.Sb = sb(es, nc, "h_Sb", [128, 6, 128], F32); H.Sbb = S.buf()
    H.Sf16 = sb(es, nc, "h_Sf16", [128, 6, 128], BF16); H.Sf16b = S.buf()
    H.Sb16 = sb(es, nc, "h_Sb16", [128, 6, 128], BF16); H.Sb16b = S.buf()
    H.LF = sb(es, nc, "h_LF", [128, 6, NCH], F32); H.LB = sb(es, nc, "h_LB", [128, 6, NCH], F32)
    H.DF = sb(es, nc, "h_DF", [128, 6, NCH], F32); H.DB = sb(es, nc, "h_DB", [128, 6, NCH], F32)
    H.Lb = S.buf()
    H.khT = sb(es, nc, "h_khT", [64, 768], BF16); H.khTb = S.buf()
    H.pB = ps(es, nc, "h_pB", [128, 1024], BF16); H.pBb = S.buf()
    H.pD = ps(es, nc, "h_pD", [128, 1024]); H.pDb = S.buf()
    H.sbsd = S.buf()
    return H


def hg_load_decays(C, H, u):
    nc, S = C.nc, C.S
    nv = (getattr(C, "hg_ng", None) or (UT // 256)) * (256 // HC)
    S.dma("sp", [lambda: nc.sync.dma_start(out=H.LF[:, :, 0:nv], in_=C.LF[u].rearrange("(c p) n -> p c n", p=128)[:, :, 0:nv]),
                 lambda: nc.sync.dma_start(out=H.LB[:, :, 0:nv], in_=C.LB[u].rearrange("(c p) n -> p c n", p=128)[:, :, 0:nv])], "h_L", writes=(H.Lb,))
    S.op("act", lambda: nc.scalar.activation(out=H.DF[:, :, 0:nv], in_=H.LF[:, :, 0:nv], func=AF.Exp), reads=(H.Lb,), writes=(H.Lb,))
    S.op("act", lambda: nc.scalar.activation(out=H.DB[:, :, 0:nv], in_=H.LB[:, :, 0:nv], func=AF.Exp), reads=(H.Lb,), writes=(H.Lb,))


def hg_state_step(C, H, khsrc, khsrcb, cols, vch, vb, St, Stb, S16, S16b, Dt, n):
    nc, S = C.nc, C.S

    def tr():
        for h in range(6):
            ins = nc.tensor.transpose(H.pB[0:64, h * 128:(h + 1) * 128], khsrc[:, h, cols], C.ident_b[:])
        return ins
    S.op("pe", tr, reads=(khsrcb,), writes=(H.pBb,))
    S.op("act", lambda: nc.scalar.copy(H.khT[:], H.pB[0:64, 0:768]), reads=(H.pBb,), writes=(H.khTb,))

    def ds():
        for h in range(6):
            ins = nc.tensor.matmul(H.pD[:, h * 128:(h + 1) * 128], H.khT[:, h * 128:(h + 1) * 128], vch[:, h * 128:(h + 1) * 128],
                                   start=True, stop=True)
        return ins
    S.op("pe", ds, reads=(H.khTb, vb), writes=(H.pDb,))

    def upd():
        for h in range(6):
            ins = nc.vector.scalar_tensor_tensor(out=St[:, h, :], in0=St[:, h, :], scalar=Dt[:, h, n:n + 1],
                                                 in1=H.pD[:, h * 128:(h + 1) * 128], op0=ALU.mult, op1=ALU.add)
        return ins
    S.op("dve", upd, reads=(H.pDb, H.Lb), writes=(Stb,))
    S.op("act", lambda: nc.scalar.copy(S16[:], St[:]), reads=(Stb,), writes=(S16b,))


def stage_hgrn(C):
    nc, S = C.nc, C.S
    units = range(UNITS) if not getattr(C, "units", None) else C.units
    GT = 256
    GC = GT // HC
    NG = getattr(C, "hg_ng", None) or UT // GT
    with ExitStack() as es:
        wout, _ = load_weight_bf16(C, es, C.w_out[1], 8, D, "wo1", piece_cols=1024)
        T = dict(st=sb(es, nc, "h_st", [128, 2, 6], F32), stb=S.buf(),
                 mv=sb(es, nc, "h_mv", [128, 2], F32), mvb=S.buf(),
                 rs=sb(es, nc, "h_rs", [128, 2], F32), rsb=S.buf(),
                 gb=load_gb(C, es, 4, "h_gb"))
        H = hg_tiles(C, es)
        tri = sb(es, nc, "h_tri", [64, 768], F32)
        S.dma("sp", lambda: nc.sync.dma_start(out=tri[:], in_=C.d_tri[:, :]), "h_tri", writes=(S.buf(),))
        S.barrier()
        names = ("QF", "KF", "QB", "KB", "KHF")
        G = []
        for i in range(2):
            g = Ctx()
            g.t = {n: sb(es, nc, f"h_{n}{i}", [128, 6, GT], BF16) for n in names}
            g.v = sb(es, nc, f"h_v{i}", [64, GC, 768], BF16)
            g.g1 = sb(es, nc, f"h_g1{i}", [64, GC, 768], F32)
            g.sbs = sb(es, nc, f"h_sbs{i}", [128, GC, 768], BF16)
            g.x = sb(es, nc, f"h_x{i}", [128, GT // 128, D], F32)
            g.b = S.buf()
            G.append(g)
        khb_t = [sb(es, nc, f"h_khb{i}", [128, 6, GT], BF16) for i in range(2)]; khb_b = [S.buf() for _ in range(2)]
        vb_t = [sb(es, nc, f"h_vb{i}", [64, GC, 768], BF16) for i in range(2)]; vb_b = [S.buf() for _ in range(2)]
        at = sb(es, nc, "h_at", [64, 768], BF16); atb = S.buf()
        osb = sb(es, nc, "h_osb", [64, 768], F32); osbb = S.buf()
        sqr = sb(es, nc, "h_sqr", [64, 768], F32); sqrb = S.buf()
        ss = sb(es, nc, "h_ss", [64, 3, 6], F32); ssb = S.buf()
        mixc = [sb(es, nc, f"h_mixc{i}", [64, D], BF16) for i in range(2)]; mixcb = [S.buf() for _ in range(2)]
        mixT = sb(es, nc, "h_mixT", [128, 8, 128], BF16); mixTb = S.buf()
        yt = [sb(es, nc, f"h_y{i}", [128, D], F32) for i in range(2)]; ytb = [S.buf() for _ in range(2)]
        eps_r = sb(es, nc, "h_epsr", [64, 1], F32)
        S.op("dve", lambda: nc.vector.memset(eps_r[:], RMS_EPS), writes=(S.buf(),))
        pAT = ps(es, nc, "h_pAT", [64, 1024]); pATb = S.buf()
        pOo = ps(es, nc, "h_pOo", [64, 1024]); pOob = S.buf()
        S.barrier()
        ny = 0
        cfl = sb(es, nc, "h_cfl", [128, 1], F32); cflb = S.buf()
        S.dma("sp", lambda: nc.sync.dma_start(out=cfl[:], in_=C.d_cflag[:, :]), "h_cfl", writes=(cflb,))
        ulist = list(units)
        for u in reversed(ulist):
            hg_load_decays(C, H, u)
            if u == ulist[-1]:
                S.op("dve", lambda: nc.vector.memset(H.Sb[:], 0.0), writes=(H.Sbb,))
            else:
                S.op("dve", lambda: nc.vector.tensor_scalar(H.Sb[:], H.Sb[:], cfl[:, 0:1], None, op0=ALU.mult),
                     reads=(H.Sbb, cflb), writes=(H.Sbb,))
            S.op("act", lambda: nc.scalar.copy(H.Sb16[:], H.Sb[:]), reads=(H.Sbb,), writes=(H.Sb16b,))

            def loadb(gi):
                j = gi % 2
                r0 = u * UT + gi * GT
                S.dma("sp", lambda: nc.sync.dma_start(out=khb_t[j][:], in_=C.KHB[u].rearrange("(c p) t -> p c t", p=128)[:, :, gi * GT:(gi + 1) * GT]),
                      f"h_khb{j}", writes=(khb_b[j],))
                S.dma("sp", lambda: nc.sync.dma_start(out=vb_t[j][:], in_=C.V1[r0:r0 + GT, :].rearrange("(c p) f -> p c f", p=64)),
                      f"h_vb{j}", writes=(vb_b[j],))
            glist = list(range(NG - 1, -1, -1))
            loadb(glist[0])
            for ii, gi in enumerate(glist):
                j = gi % 2
                if ii + 1 < len(glist):
                    loadb(glist[ii + 1])
                for cc in range(GC - 1, -1, -1):
                    n = gi * GC + cc
                    S.dma("sp", lambda: nc.sync.dma_start(out=C.SBs[u, n], in_=H.Sb16[:].rearrange("p h d -> p (h d)")), "h_sbst",
                          reads=(H.Sb16b,), writes=(H.sbsd,))
                    hg_state_step(C, H, khb_t[j], khb_b[j], slice(cc * HC, (cc + 1) * HC), vb_t[j][:, cc, :], vb_b[j],
                                  H.Sb, H.Sbb, H.Sb16, H.Sb16b, H.DB, n)
        for u in ulist:
            hg_load_decays(C, H, u)
            if u == ulist[0]:
                S.op("dve", lambda: nc.vector.memset(H.Sf[:], 0.0), writes=(H.Sfb,))
            else:
                S.op("dve", lambda: nc.vector.tensor_scalar(H.Sf[:], H.Sf[:], cfl[:, 0:1], None, op0=ALU.mult),
                     reads=(H.Sfb, cflb), writes=(H.Sfb,))
            S.op("act", lambda: nc.scalar.copy(H.Sf16[:], H.Sf[:]), reads=(H.Sfb,), writes=(H.Sf16b,))

            def loadf(gi):
                g = G[gi % 2]
                r0 = u * UT + gi * GT
                fns = [(lambda nm=nm: nc.sync.dma_start(out=g.t[nm][:], in_=getattr(C, nm)[u].rearrange("(c p) t -> p c t", p=128)[:, :, gi * GT:(gi + 1) * GT]))
                       for nm in names]
                fns.append(lambda: nc.sync.dma_start(out=g.v[:], in_=C.V1[r0:r0 + GT, :].rearrange("(c p) f -> p c f", p=64)))
                fns.append(lambda: nc.sync.dma_start(out=g.g1[:], in_=C.G1[r0:r0 + GT, :].rearrange("(c p) f -> p c f", p=64)))
                fns.append(lambda: nc.sync.dma_start(out=g.sbs[:], in_=C.SBs[u, gi * GC:(gi + 1) * GC].rearrange("n p f -> p n f")))
                fns.append(lambda: nc.sync.dma_start(out=g.x[:], in_=C.x2_s[r0:r0 + GT, :].rearrange("(s p) d -> p s d", p=128)))
                S.dma("sp", fns, f"h_g{gi % 2}", reads=(H.sbsd,), writes=(g.b,))
            loadf(0)
            for gi in range(NG):
                g = G[gi % 2]
                if gi + 1 < NG:
                    loadf(gi + 1)
                for cc in range(GC):
                    n = gi * GC + cc
                    cols = slice(cc * HC, (cc + 1) * HC)
                    mj = n % 2
                    S.dma("sp", lambda: nc.sync.dma_start(out=mixc[mj][:, 768:1024], in_=C.mix_s[u, n * HC:(n + 1) * HC, 768:1024]),
                          f"h_mixc{mj}", writes=(mixcb[mj],))

                    def amm():
                        for d, (kn, qn) in enumerate((("KF", "QF"), ("KB", "QB"))):
                            for h in range(6):
                                o = (d * 6 + h) * 64
                                ins = nc.tensor.matmul(pAT[:, o:o + 64], g.t[kn][:, h, cols], g.t[qn][:, h, cols], start=True, stop=True)
                        return ins
                    S.op("pe", amm, reads=(g.b,), writes=(pATb,))
                    S.op("dve", lambda: nc.vector.tensor_tensor(out=at[:], in0=pAT[:, 0:768], in1=tri[:], op=ALU.mult),
                         reads=(pATb,), writes=(atb,))

                    def omm():
                        for h in range(6):
                            o = pOo[:, h * 128:(h + 1) * 128]
                            vv = g.v[:, cc, h * 128:(h + 1) * 128]
                            nc.tensor.matmul(o, at[:, h * 64:(h + 1) * 64], vv, start=True, stop=False)
                            nc.tensor.matmul(o, at[:, (6 + h) * 64:(7 + h) * 64], vv, start=False, stop=False)
                            nc.tensor.matmul(o, g.t["QF"][:, h, cols], H.Sf16[:, h, :], start=False, stop=False)
                            ins = nc.tensor.matmul(o, g.t["QB"][:, h, cols], g.sbs[:, cc, h * 128:(h + 1) * 128], start=False, stop=True)
                        return ins
                    S.op("pe", omm, reads=(atb, g.b, H.Sf16b), writes=(pOob,))
                    hg_state_step(C, H, g.t["KHF"], g.b, cols, g.v[:, cc, :], g.b, H.Sf, H.Sfb, H.Sf16, H.Sf16b, H.DF, n)
                    S.op("act", lambda: nc.scalar.copy(osb[:], pOo[:, 0:768]), reads=(pOob,), writes=(osbb,))
                    S.op("pool", lambda: nc.gpsimd.tensor_tensor(out=sqr[:], in0=osb[:], in1=osb[:], op=ALU.mult), reads=(osbb,), writes=(sqrb,))
                    S.op("dve", lambda: nc.vector.tensor_reduce(out=ss[:, 0, :], in_=sqr[:].rearrange("p (h d) -> p h d", h=6),
                                                                axis=mybir.AxisListType.X, op=ALU.add), reads=(sqrb,), writes=(ssb,))
                    S.op("act", lambda: nc.scalar.activation(out=ss[:, 1, :], in_=ss[:, 0, :], func=AF.Sqrt, bias=eps_r[:, 0:1], scale=1.0 / 128),
                         reads=(ssb,), writes=(ssb,))
                    S.op("dve", lambda: nc.vector.reciprocal(ss[:, 2, :], ss[:, 1, :]), reads=(ssb,), writes=(ssb,))
                    S.op("dve", lambda: nc.vector.tensor_tensor(out=sqr[:].rearrange("p (h d) -> p h d", h=6), in0=osb[:].rearrange("p (h d) -> p h d", h=6),
                                                                in1=ss[:, 2, :].unsqueeze(2).to_broadcast([64, 6, 128]), op=ALU.mult),
                         reads=(osbb, ssb), writes=(sqrb,))
                    S.op("pool", lambda: nc.gpsimd.tensor_tensor(out=mixc[mj][:, 0:768], in0=sqr[:], in1=g.g1[:, cc, :], op=ALU.mult),
                         reads=(sqrb, g.b), writes=(mixcb[mj],))
                    hh = n % 2

                    def tr():
                        for k in range(8):
                            ins = nc.tensor.transpose(H.pB[:, k * 64:(k + 1) * 64], mixc[mj][:, k * 128:(k + 1) * 128], C.ident_b[0:64, 0:64])
                        return ins
                    S.op("pe", tr, reads=(mixcb[mj],), writes=(H.pBb,))
                    S.op("act", lambda: nc.scalar.copy(mixT[:, :, hh * 64:(hh + 1) * 64], H.pB[:, 0:512].rearrange("p (k t) -> p k t", k=8)),
                         reads=(H.pBb,), writes=(mixTb,))
                    if hh == 1:
                        yj = ny % 2
                        ny += 1
                        sx = (n // 2) % (GT // 128)

                        def mm():
                            for hf in range(2):
                                for k in range(8):
                                    ins = nc.tensor.matmul(H.pD[:, hf * 512:(hf + 1) * 512], mixT[:, k, :], wout[:, k, hf * 512:(hf + 1) * 512],
                                                           start=(k == 0), stop=(k == 7))
                            return ins
                        S.op("pe", mm, reads=(mixTb,), writes=(H.pDb,))
                        S.op("dve", lambda: nc.vector.scalar_tensor_tensor(out=yt[yj][:], in0=g.x[:, sx, :], scalar=ALPHA, in1=H.pD[:, :],
                                                                          op0=ALU.mult, op1=ALU.add), reads=(g.b, H.pDb), writes=(ytb[yj],))
                        layernorm_store(C, T, yt[yj], ytb[yj], yt[yj][:], ytb[yj])
                        r = u * UT + (n // 2) * 128
                        S.dma("sp", lambda: nc.sync.dma_start(out=C.x3_s[r:r + 128, :], in_=yt[yj][:]), f"h_y{yj}", reads=(ytb[yj],))
        S.barrier()


def build_program(exchange=True):
    nc = bass.Bass("TRN2", target_bir_lowering=False)
    C = Ctx()
    C.nc = nc
    C.max_tiles = None
    NT = UNITS * UT

    def din(name, shape, dt=F32):
        return nc.dram_tensor(name, list(shape), dt, kind="ExternalInput").ap()

    def dscr(name, shape, dt):
        return nc.dram_tensor(name, list(shape), dt, kind="Internal").ap()

    C.xe = din("xe", [UNITS, UTE, D])
    C.d_mem = din("mem", [UNITS, 256, D])
    C.w_mem_kv = din("w_mem_kv", [D, 512])
    C.w_in_a = din("w_in_a", [D, 2560])
    C.w_in_b = din("w_in_b", [D, 4096])
    C.w_out = din("w_out", [2, D, D])
    C.w_ff1 = din("w_ff1", [2, D, DFF])
    C.w_ff2 = din("w_ff2", [2, DFF, D])
    C.d_ident = din("ident", [128, 128])
    C.d_vecs = din("vecs", [128, 8, D])
    C.natab = din("natab", [1 + 4 * UNITS, 128, 9216])
    C.d_tri = din("tri", [64, 768])
    C.lb_logits = din("lbl", [128, 2, 6])
    C.d_nwb = din("nwb", [128, 768])
    C.y = nc.dram_tensor("y", [NT, D], F32, kind="ExternalOutput").ap()
    C.qT_s = dscr("qT_s", [UNITS, 768, UT], BF16)
    C.kT_s = dscr("kT_s", [UNITS, 768, UTE], BF16)
    C.v_s = dscr("v_s", [UNITS, UTE, 780], BF16)
    C.mix_s = dscr("mix_s", [UNITS, UT, D], BF16)
    C.x1_s = dscr("x1_s", [NT, D], F32)
    C.x2_s = dscr("x2_s", [NT, D], F32)
    C.x3_s = C.x1_s
    for n in ("QF", "KF", "KHF", "QB", "KB", "KHB"):
        setattr(C, n, dscr(n, [UNITS, 768, UT], BF16))
    C.LF = dscr("LF", [UNITS, 768, NCH], F32)
    C.LB = dscr("LB", [UNITS, 768, NCH], F32)
    C.V1 = dscr("V1", [NT, 768], BF16)
    C.G1 = dscr("G1", [NT, 768], F32)
    C.SBs = dscr("SBs", [UNITS, NCH, 128, 768], BF16)
    C.d_cflag = din("cflag", [128, 1])
    with ExitStack() as es:
        C.S = Sched(nc, es)
        setup_persistent(C, es)
        stage_memkv(C)
        stage_inproj_a(C)
        stage_na(C)
        stage_ffn(C, 0, C.x1_s, C.x2_s)
        stage_inproj_b(C)
        stage_hgrn(C)
        stage_ffn(C, 1, C.x3_s, C.y)
        C.S.barrier()
    return nc, C


def tri_mask():
    s = np.arange(64)[:, None]
    t = np.arange(64)[None, :]
    f = (s <= t).astype(np.float32)
    b = (s >= t).astype(np.float32)
    return np.concatenate([np.tile(f, (1, 6)), np.tile(b, (1, 6))], axis=1)


def kernel(x_prompt, x_sample, mem_prompt, mem_sample, w_mem_kv, w_in_a, rpb, w_in_b, lb_logits, hg_norm_w,
           w_out, ln1_g, ln1_b, w_ff1, w_ff2, ln2_g, ln2_b):
    f = lambda a: np.ascontiguousarray(np.asarray(a, dtype=np.float32))
    x_prompt, x_sample, mem_prompt, mem_sample = f(x_prompt), f(x_sample), f(mem_prompt), f(mem_sample)
    rpb0 = f(rpb)[0]
    vec = np.stack([f(ln1_g)[0], f(ln1_b)[0], f(ln2_g)[0], f(ln2_b)[0], f(ln1_g)[1], f(ln1_b)[1], f(ln2_g)[1], f(ln2_b)[1]])
    shared = dict(
        w_mem_kv=f(w_mem_kv), w_in_a=f(w_in_a)[0], w_in_b=f(w_in_b)[0], w_out=f(w_out), w_ff1=f(w_ff1), w_ff2=f(w_ff2),
        ident=np.eye(128, dtype=np.float32), vecs=np.ascontiguousarray(np.broadcast_to(vec[None], (128, 8, D))),
        tri=tri_mask(), lbl=np.ascontiguousarray(f(lb_logits).reshape(2, 6, 128).transpose(2, 0, 1)),
        nwb=np.ascontiguousarray(np.broadcast_to(np.tile(f(hg_norm_w)[0], 6)[None], (128, 768))))
    tab_int = na_table(rpb0, 64, 256, 8)
    tab_true = [na_table(rpb0, 0, 64, p) for p in SPECIAL_PAIRS]
    tab_seg = [[na_table(rpb0, 64 * j, 256, p) for p in SPECIAL_PAIRS] for j in range(4)]
    prompt_of = {}
    pi = 0
    for c in range(2, NCORES):
        for u in range(3 if c < 6 else 2):
            prompt_of[(c, u)] = pi
            pi += 1
    assert pi == x_prompt.shape[0]
    in_maps = []
    for c in range(NCORES):
        xe = np.zeros((UNITS, UTE, D), np.float32)
        mem = np.zeros((UNITS, 256, D), np.float32)
        tabs = [tab_int]
        for u in range(UNITS):
            if c < 2:
                lo, hi = u * UT - HALO, (u + 1) * UT + HALO
                clo, chi = max(lo, 0), min(hi, x_sample.shape[1])
                xe[u, clo - lo:chi - lo] = x_sample[c, clo:chi]
                mem[u] = mem_sample[c]
                tabs += tab_seg[u]
            else:
                if (c, u) in prompt_of:
                    xe[u, HALO:HALO + UT] = x_prompt[prompt_of[(c, u)]]
                    mem[u] = mem_prompt[prompt_of[(c, u)]]
                tabs += tab_true
        natab = np.stack(tabs).reshape(1 + 4 * UNITS, 128, 9216)
        in_maps.append(dict(shared, xe=xe, mem=mem, natab=natab,
                            cflag=np.full((128, 1), 1.0 if c < 2 else 0.0, np.float32)))
    nc, _ = build_program()
    res = run_bass_kernel_spmd(nc, in_maps, core_ids=list(range(NCORES)))
    y_prompt = np.empty((16, UT, D), np.float32)
    y_sample = np.empty((2, 4 * UT, D), np.float32)
    for c in range(NCORES):
        y = np.asarray(res.results[c]["y"]).reshape(UNITS, UT, D)
        for u in range(UNITS):
            if c < 2:
                y_sample[c, u * UT:(u + 1) * UT] = y[u]
            elif (c, u) in prompt_of:
                y_prompt[prompt_of[(c, u)]] = y[u]
    return (y_prompt, y_sample)
```

```python
import numpy as np
from contextlib import ExitStack
import concourse.bass as bass
import concourse.mybir as mybir
from concourse.bass_utils import run_bass_kernel_spmd

F32 = mybir.dt.float32
BF16 = mybir.dt.bfloat16
AF = mybir.ActivationFunctionType
ALU = mybir.AluOpType

D = 1024
DFF = 4096
NCORES = 8
UNITS = 4
UT = 4096
HALO = 256
UTE = UT + 2 * HALO
ALPHA = 4.0 ** 0.25
LN_EPS = 1e-5
RMS_EPS = 1e-6


class Buf:
    __slots__ = ("name", "w", "r")

    def __init__(self, name):
        self.name = name
        self.w = None
        self.r = {}


class Sched:
    def __init__(self, nc, es, same_eng_sync=True):
        self.nc = nc
        self.es = es
        self.eng = {"pe": nc.tensor, "act": nc.scalar, "dve": nc.vector,
                    "pool": nc.gpsimd, "sp": nc.sync}
        self.same = same_eng_sync
        self.semobj = {}
        self.cnt = {}
        self.seen = {e: {} for e in self.eng}
        self.nops = 0

    def sem(self, name):
        if name not in self.semobj:
            self.semobj[name] = self.es.enter_context(self.nc.semaphore("s_" + name))
            self.cnt[name] = 0
        return self.semobj[name]

    def buf(self, name="b"):
        return Buf(name)

    def _waits(self, e, reads, writes, is_dma, extra=()):
        need = {}

        def add(tok, raw):
            s, v, te = tok
            if te == e and not is_dma:
                if e == "pe" or not self.same:
                    return
            if need.get(s, 0) < v:
                need[s] = v
        for b in reads:
            if b.w is not None:
                add(b.w, True)
        for b in writes:
            if b.w is not None:
                add(b.w, False)
            for r in b.r.values():
                add(r, False)
        for tok in extra:
            add(tok, True)
        for s, v in need.items():
            if self.seen[e].get(s, 0) < v:
                self.eng[e].wait_ge(self.semobj[s], v)
                self.seen[e][s] = v

    def _update(self, rk, tok, reads, writes):
        for b in reads:
            b.r[rk] = tok
        for b in writes:
            b.w = tok
            b.r = {}

    def op(self, eng, fn, reads=(), writes=()):
        self._waits(eng, reads, writes, False)
        inst = fn()
        so = self.sem(eng)
        self.cnt[eng] += 1
        inst.then_inc(so, 1)
        tok = (eng, self.cnt[eng], eng)
        self._update(eng, tok, reads, writes)
        self.nops += 1
        return tok

    def dma(self, q, fns, key, reads=(), writes=()):
        if not isinstance(fns, (list, tuple)):
            fns = [fns]
        s = "dma_" + key
        so = self.sem(s)
        prev = ((s, self.cnt[s], None),) if self.cnt[s] > 0 else ()
        self._waits(q, reads, writes, True, extra=prev)
        for f in fns:
            f().then_inc(so, 16)
        self.cnt[s] += 16 * len(fns)
        tok = (s, self.cnt[s], None)
        self._update(s, tok, reads, writes)
        self.nops += 1
        return tok

    def cc(self, fn, key, reads=(), writes=()):
        s = "cc_" + key
        so = self.sem(s)
        prev = ((s, self.cnt[s], None),) if self.cnt[s] > 0 else ()
        self._waits("pool", reads, writes, True, extra=prev)
        fn().then_inc(so, 1)
        self.cnt[s] += 1
        tok = (s, self.cnt[s], None)
        self._update(s, tok, reads, writes)
        return tok

    def barrier(self):
        for e in self.eng:
            for s, v in self.cnt.items():
                if v > 0 and s != e and self.seen[e].get(s, 0) < v:
                    self.eng[e].wait_ge(self.semobj[s], v)
                    self.seen[e][s] = v

    def emit(self, es=None):
        self.nsem = len(self.semobj)


class Ctx:
    pass


_UNIQ = [0]


def sb(es, nc, name, shape, dt):
    _UNIQ[0] += 1
    return es.enter_context(nc.sbuf_tensor(f"sb_{name}_{_UNIQ[0]}", list(shape), dt))


def ps(es, nc, name, shape, dt=F32):
    _UNIQ[0] += 1
    return es.enter_context(nc.psum_tensor(f"ps_{name}_{_UNIQ[0]}", list(shape), dt))


def load_weight_bf16(C, es_outer, wdram, kparts, ncols, name, piece_cols=2048):
    nc, S = C.nc, C.S
    wt = sb(es_outer, nc, name, [128, kparts, ncols], BF16)
    wb = S.buf(name)
    piece_cols = min(piece_cols, ncols)
    npc = ncols // piece_cols
    with ExitStack() as es:
        stg = [sb(es, nc, f"{name}_stg{i}", [128, piece_cols], F32) for i in range(2)]
        stb = [S.buf(f"{name}_stg{i}") for i in range(2)]
        engs = ["dve", "act", "pool"]
        n = 0
        for k in range(kparts):
            for p in range(npc):
                j = n % 2
                src = wdram[k * 128:(k + 1) * 128, p * piece_cols:(p + 1) * piece_cols]
                S.dma("sp", (lambda d=stg[j], s=src: nc.sync.dma_start(out=d[:], in_=s)),
                      f"{name}_stg{j}", reads=(), writes=(stb[j],))
                e = engs[n % 3]
                dst = wt[:, k, p * piece_cols:(p + 1) * piece_cols]
                if e == "dve":
                    S.op("dve", (lambda d=dst, s=stg[j]: nc.vector.tensor_copy(d, s[:])),
                         reads=(stb[j],), writes=())
                elif e == "act":
                    S.op("act", (lambda d=dst, s=stg[j]: nc.scalar.copy(d, s[:])),
                         reads=(stb[j],), writes=())
                else:
                    S.op("pool", (lambda d=dst, s=stg[j]: nc.gpsimd.tensor_copy(d, s[:])),
                         reads=(stb[j],), writes=())
                n += 1
        S.barrier()
    return wt, wb


def layernorm_store(C, T, y, yb, out_tile, outb):
    nc, S = C.nc, C.S
    st, stb, mv, mvb, rs, rsb = T["st"], T["stb"], T["mv"], T["mvb"], T["rs"], T["rsb"]
    for h in range(2):
        S.op("dve", (lambda h=h: nc.vector.bn_stats(st[:, h, :], y[:, h * 512:(h + 1) * 512])),
             reads=(yb,), writes=(stb,))
    S.op("dve", (lambda: nc.vector.bn_aggr(mv[:], st[:].rearrange("p a b -> p (a b)"))),
         reads=(stb,), writes=(mvb,))
    S.op("act", (lambda: nc.scalar.activation(out=rs[:, 0:1], in_=mv[:, 1:2], func=AF.Sqrt,
                                              bias=C.eps_ln[:, 0:1], scale=1.0)),
         reads=(mvb,), writes=(rsb,))
    S.op("dve", (lambda: nc.vector.reciprocal(rs[:, 1:2], rs[:, 0:1])),
         reads=(rsb,), writes=(rsb,))
    S.op("dve", (lambda: nc.vector.tensor_scalar(y[:], y[:], mv[:, 0:1], rs[:, 1:2],
                                                 op0=ALU.subtract, op1=ALU.mult)),
         reads=(yb, mvb, rsb), writes=(yb,))
    S.op("pool", (lambda: nc.gpsimd.tensor_tensor(out=y[:], in0=y[:], in1=T["gb"][:, 0, :], op=ALU.mult)),
         reads=(yb,), writes=(yb,))
    S.op("pool", (lambda: nc.gpsimd.tensor_tensor(out=out_tile, in0=y[:], in1=T["gb"][:, 1, :], op=ALU.add)),
         reads=(yb,), writes=(outb,))


def load_gb(C, es, gi, name):
    nc, S = C.nc, C.S
    t = sb(es, nc, name, [128, 2, D], F32)
    S.dma("sp", (lambda: nc.sync.dma_start(out=t[:], in_=C.d_vecs[:, gi:gi + 2, :])), name, writes=(S.buf(),))
    S.barrier()
    return t


def stage_ffn(C, layer, src, dst):
    nc, S = C.nc, C.S
    TT = 256
    ntiles = (UNITS * UT) // TT
    ntiles = min(ntiles, C.max_tiles) if C.max_tiles else ntiles
    with ExitStack() as es:
        w1, _ = load_weight_bf16(C, es, C.w_ff1[layer], 8, DFF, "w1")
        w2, _ = load_weight_bf16(C, es, C.w_ff2[layer], 32, D, "w2", piece_cols=1024)
        xin = [sb(es, nc, f"f_x{i}", [128, 2, D], F32) for i in range(2)]
        xinb = [S.buf() for _ in range(2)]
        xT = sb(es, nc, "f_xT", [128, 8, TT], BF16)
        xTb = S.buf()
        h1 = sb(es, nc, "f_h1", [128, 32, TT], BF16)
        h1b = [S.buf() for _ in range(32)]
        rl = [sb(es, nc, f"f_rl{i}", [128, TT], F32) for i in range(2)]
        rlb = [S.buf() for _ in range(2)]
        yt = [sb(es, nc, f"f_y{i}", [128, D], F32) for i in range(2)]
        ytb = [S.buf() for _ in range(2)]
        T = dict(st=sb(es, nc, "f_st", [128, 2, 6], F32), stb=S.buf(),
                 mv=sb(es, nc, "f_mv", [128, 2], F32), mvb=S.buf(),
                 rs=sb(es, nc, "f_rs", [128, 2], F32), rsb=S.buf(),
                 gb=load_gb(C, es, 4 * layer + 2, "f_gb"))
        pT = ps(es, nc, "f_pT", [128, 1024])
        pTb = S.buf()
        pH = [ps(es, nc, f"f_pH{i}", [128, 512]) for i in range(2)]
        pHb = [S.buf() for _ in range(2)]
        pY = [ps(es, nc, f"f_pY{i}", [128, 1024]) for i in range(2)]
        pYb = [S.buf() for _ in range(2)]

        def load(t):
            j = t % 2
            s = src[t * TT:(t + 1) * TT, :].rearrange("(s p) d -> p s d", p=128)
            S.dma("sp", (lambda: nc.sync.dma_start(out=xin[j][:], in_=s)), f"f_x{j}",
                  writes=(xinb[j],))

        stop = getattr(C, "stop", 99)
        if stop <= 0:
            S.barrier(); return
        load(0)
        for t in range(ntiles):
            j = t % 2
            if t + 1 < ntiles:
                load(t + 1)
            for s in range(2):
                for k in range(8):
                    S.op("pe", (lambda s=s, k=k: nc.tensor.transpose(
                        pT[:, k * 128:(k + 1) * 128], xin[j][:, s, k * 128:(k + 1) * 128], C.ident_f[:])),
                        reads=(xinb[j],), writes=(pTb,))
                S.op("act" if s == 0 else "dve",
                     (lambda s=s: (nc.scalar.copy if s == 0 else nc.vector.tensor_copy)(
                         xT[:, :, s * 128:(s + 1) * 128],
                         pT[:].rearrange("p (k t) -> p k t", k=8))),
                     reads=(pTb,), writes=(xTb,))
            if stop <= 1:
                continue
            for c in range(32):
                q = c % 2
                pv = pH[q][:, 0:256]
                for k in range(8):
                    S.op("pe", (lambda c=c, k=k, pv=pv: nc.tensor.matmul(
                        pv, w1[:, k, c * 128:(c + 1) * 128], xT[:, k, :], start=(k == 0), stop=(k == 7))),
                        reads=(xTb,), writes=(pHb[q],))
                S.op("act", (lambda pv=pv, q=q: nc.scalar.activation(out=rl[q][:], in_=pv, func=AF.Relu)),
                     reads=(pHb[q],), writes=(rlb[q],))
                S.op("pool", (lambda c=c, q=q: nc.gpsimd.tensor_tensor(
                    out=h1[:, c, :], in0=rl[q][:], in1=rl[q][:], op=ALU.mult)),
                    reads=(rlb[q],), writes=(h1b[c],))
            if stop <= 2:
                continue
            for s in range(2):
                for hf in range(2):
                    for c in range(32):
                        S.op("pe", (lambda s=s, hf=hf, c=c: nc.tensor.matmul(
                            pY[s][:, hf * 512:(hf + 1) * 512], h1[:, c, s * 128:(s + 1) * 128],
                            w2[:, c, hf * 512:(hf + 1) * 512], start=(c == 0), stop=(c == 31))),
                            reads=(h1b[c],), writes=(pYb[s],))
                if stop <= 3:
                    continue
                S.op("dve", (lambda s=s: nc.vector.scalar_tensor_tensor(
                    out=yt[s][:], in0=xin[j][:, s, :], scalar=ALPHA, in1=pY[s][:],
                    op0=ALU.mult, op1=ALU.add)),
                    reads=(xinb[j], pYb[s]), writes=(ytb[s],))
                if stop <= 4:
                    continue
                layernorm_store(C, T, yt[s], ytb[s], yt[s][:], ytb[s])
                if stop <= 5:
                    continue
                d = dst[t * TT + s * 128: t * TT + (s + 1) * 128, :]
                S.dma("sp", (lambda s=s, d=d: nc.sync.dma_start(out=d, in_=yt[s][:])), f"f_o{s}",
                      reads=(ytb[s],))
        S.barrier()


def setup_consts(C, es):
    nc, S = C.nc, C.S
    C.ident_f = sb(es, nc, "ident_f", [128, 128], F32)
    C.ident_b = sb(es, nc, "ident_b", [128, 128], BF16)
    C.eps_ln = sb(es, nc, "eps_ln", [128, 1], F32)
    b = S.buf()
    S.dma("sp", [lambda: nc.sync.dma_start(out=C.ident_f[:], in_=C.d_ident[:, :]),
                 ], "consts", writes=(b,))
    S.op("dve", lambda: nc.vector.tensor_copy(C.ident_b[:], C.ident_f[:]), reads=(b,), writes=(b,))
    S.op("dve", lambda: nc.vector.memset(C.eps_ln[:], LN_EPS), writes=(b,))
    S.barrier()


def setup_persistent(C, es):
    nc, S = C.nc, C.S
    setup_consts(C, es)
    C.memk = sb(es, nc, "memk", [128, UNITS, 2, 256], BF16)
    C.memv = sb(es, nc, "memv", [128, UNITS, 2, 4, 65], BF16)
    C.ones_b = sb(es, nc, "ones_b", [128, 1], BF16)
    C.memb = S.buf("mem")


def stage_memkv(C):
    nc, S = C.nc, C.S
    with ExitStack() as es:
        w, _ = load_weight_bf16(C, es, C.w_mem_kv, 8, 512, "wkv", piece_cols=512)
        mx = sb(es, nc, "m_x", [128, 2, D], F32)
        mxb = S.buf()
        mT = sb(es, nc, "m_T", [128, 8, 256], BF16)
        mTb = S.buf()
        pT = ps(es, nc, "m_pT", [128, 1024])
        pTb = S.buf()
        pA = ps(es, nc, "m_pA", [128, 512])
        pAb = S.buf()
        S.op("dve", lambda: nc.vector.memset(C.memv[:], 1.0), writes=(C.memb,))
        S.op("dve", lambda: nc.vector.memset(C.ones_b[:], 1.0), writes=(C.memb,))
        for u in range(UNITS):
            S.dma("sp", lambda: nc.sync.dma_start(out=mx[:], in_=C.d_mem[u].rearrange("(s p) d -> p s d", p=128)),
                  "m_x", writes=(mxb,))
            for s in range(2):
                def tr():
                    for k in range(8):
                        i = nc.tensor.transpose(pT[:, k * 128:(k + 1) * 128], mx[:, s, k * 128:(k + 1) * 128], C.ident_f[:])
                    return i
                S.op("pe", tr, reads=(mxb,), writes=(pTb,))
                S.op("act", lambda: nc.scalar.copy(mT[:, :, s * 128:(s + 1) * 128], pT[:].rearrange("p (k t) -> p k t", k=8)),
                     reads=(pTb,), writes=(mTb,))
            for c in range(2):
                def mm():
                    for k in range(8):
                        i = nc.tensor.matmul(pA[:, 0:256], w[:, k, c * 128:(c + 1) * 128], mT[:, k, :],
                                             start=(k == 0), stop=(k == 7))
                    return i
                S.op("pe", mm, reads=(mTb,), writes=(pAb,))
                S.op("act", lambda: nc.scalar.copy(C.memk[:, u, c, :], pA[:, 0:256]), reads=(pAb,), writes=(C.memb,))
            for mc in range(2):
                def mm():
                    for k in range(8):
                        i = nc.tensor.matmul(pA[:, 0:256], mT[:, k, mc * 128:(mc + 1) * 128], w[:, k, 256:512],
                                             start=(k == 0), stop=(k == 7))
                    return i
                S.op("pe", mm, reads=(mTb,), writes=(pAb,))
                S.op("dve", lambda: nc.vector.tensor_copy(C.memv[:, u, mc, :, 0:64],
                                                          pA[:, 0:256].rearrange("p (h d) -> p h d", h=4)),
                     reads=(pAb,), writes=(C.memb,))
        S.barrier()


def mem_attn_tile(C, T, u, qm, qmb, tok0, ntok=512):
    nc, S = C.nc, C.S
    PT, PTb, pS, pSb, pO, pOb = T["PT"], T["PTb"], T["pS"], T["pSb"], T["pO"], T["pOb"]
    rd, rdb, cr, crb = T["rd"], T["rdb"], T["cr"], T["crb"]
    n = 0
    for h in range(4):
        p0 = (h % 2) * 64
        for mc in range(2):
            q = n % 2
            S.op("pe", lambda: nc.tensor.matmul(pS[q][:, 0:ntok], C.memk[p0:p0 + 64, u, h // 2, mc * 128:(mc + 1) * 128],
                                                qm[p0:p0 + 64, h // 2, 0:ntok], start=True, stop=True),
                 reads=(qmb,), writes=(pSb[q],))
            S.op("act", lambda: nc.scalar.activation(out=PT[:, h, mc, 0:ntok], in_=pS[q][:, 0:ntok], func=AF.Exp),
                 reads=(pSb[q],), writes=(PTb[h],))
            n += 1
    for s in range(ntok // 128):
        def pv():
            for h in range(4):
                for mc in range(2):
                    i = nc.tensor.matmul(pO[:, h * 65:(h + 1) * 65], PT[:, h, mc, s * 128:(s + 1) * 128],
                                         C.memv[:, u, mc, h, :], start=(mc == 0), stop=(mc == 1))
            return i
        S.op("pe", pv, reads=tuple(PTb), writes=(pOb,))
        pov = pO[:, 0:260].rearrange("p (h d) -> p h d", h=4)
        S.op("dve", lambda: nc.vector.reciprocal(rd[:], pov[:, :, 64]), reads=(pOb,), writes=(rdb,))
        j = s % 2
        S.op("dve", lambda: nc.vector.tensor_tensor(out=cr[j][:], in0=pov[:, :, 0:64],
                                                    in1=rd[:].unsqueeze(2).to_broadcast([128, 4, 64]), op=ALU.mult),
             reads=(pOb, rdb), writes=(crb[j],))
        d = C.mix_s[u, tok0 + s * 128: tok0 + (s + 1) * 128, 768:1024]
        S.dma("sp", lambda: nc.sync.dma_start(out=d, in_=cr[j][:].rearrange("p h d -> p (h d)")), f"cr{j}",
              reads=(crb[j],))


def mem_attn_tiles(C, es, pS, pSb):
    nc, S = C.nc, C.S
    return dict(PT=sb(es, nc, "ma_PT", [128, 4, 2, 512], BF16), PTb=[S.buf() for _ in range(4)],
                pS=pS, pSb=pSb,
                pO=ps(es, nc, "ma_pO", [128, 512]), pOb=S.buf(),
                rd=sb(es, nc, "ma_rd", [128, 4], F32), rdb=S.buf(),
                cr=[sb(es, nc, f"ma_cr{i}", [128, 4, 64], BF16) for i in range(2)], crb=[S.buf() for _ in range(2)])


def stage_inproj_a(C):
    nc, S = C.nc, C.S
    units = range(UNITS) if not getattr(C, "units", None) else C.units
    with ExitStack() as es:
        w, _ = load_weight_bf16(C, es, C.w_in_a, 8, 2560, "wa", piece_cols=1280)
        xin = [sb(es, nc, f"a_x{i}", [128, 4, D], F32) for i in range(2)]
        xinb = [S.buf() for _ in range(2)]
        xT = sb(es, nc, "a_xT", [128, 8, 512], BF16)
        xTb = S.buf()
        qf = sb(es, nc, "a_qf", [128, 6, 512], BF16); qfb = S.buf()
        kf = sb(es, nc, "a_kf", [128, 6, 512], BF16); kfb = S.buf()
        qm = sb(es, nc, "a_qm", [128, 2, 512], BF16); qmb = S.buf()
        va = sb(es, nc, "a_va", [128, 4, 12, 65], BF16); vab = S.buf()
        pT = ps(es, nc, "a_pT", [128, 1024]); pTb = S.buf()
        pA = [ps(es, nc, f"a_pA{i}", [128, 512]) for i in range(2)]; pAb = [S.buf() for _ in range(2)]
        pV = ps(es, nc, "a_pV", [128, 1024]); pVb = S.buf()
        MA = mem_attn_tiles(C, es, pA, pAb)
        S.op("dve", lambda: nc.vector.memset(va[:], 1.0), writes=(vab,))
        tiles = [(u, t) for u in units for t in range(9)]

        def load(i):
            u, t = tiles[i]
            j = i % 2
            if t < 8:
                srcs = [(xin[j][:], C.xe[u, HALO + t * 512: HALO + (t + 1) * 512, :].rearrange("(s p) d -> p s d", p=128))]
            else:
                srcs = [(xin[j][:, 0:2, :], C.xe[u, 0:HALO, :].rearrange("(s p) d -> p s d", p=128)),
                        (xin[j][:, 2:4, :], C.xe[u, HALO + UT:UTE, :].rearrange("(s p) d -> p s d", p=128))]
            S.dma("sp", [(lambda o=o, i_=i_: nc.sync.dma_start(out=o, in_=i_)) for o, i_ in srcs], f"a_x{j}",
                  writes=(xinb[j],))

        load(0)
        na = 0
        for i, (u, t) in enumerate(tiles):
            j = i % 2
            if i + 1 < len(tiles):
                load(i + 1)
            real = t < 8
            for s in range(4):
                def tr():
                    for k in range(8):
                        ins = nc.tensor.transpose(pT[:, k * 128:(k + 1) * 128], xin[j][:, s, k * 128:(k + 1) * 128], C.ident_f[:])
                    return ins
                S.op("pe", tr, reads=(xinb[j],), writes=(pTb,))
                if s % 2 == 0:
                    S.op("act", lambda: nc.scalar.copy(xT[:, :, s * 128:(s + 1) * 128], pT[:].rearrange("p (k t) -> p k t", k=8)),
                         reads=(pTb,), writes=(xTb,))
                else:
                    S.op("dve", lambda: nc.vector.tensor_copy(xT[:, :, s * 128:(s + 1) * 128], pT[:].rearrange("p (k t) -> p k t", k=8)),
                         reads=(pTb,), writes=(xTb,))
            jobs = []
            if real:
                jobs += [(c * 128, qf, qfb, c, 0.125) for c in range(6)]
                jobs += [(2304 + c * 128, qm, qmb, c, 0.125) for c in range(2)]
            jobs += [(768 + c * 128, kf, kfb, c, 1.0) for c in range(6)]
            for (col, dt_, db, c, sc) in jobs:
                q = na % 2
                na += 1

                def mm():
                    for k in range(8):
                        ins = nc.tensor.matmul(pA[q][:, :], w[:, k, col:col + 128], xT[:, k, :], start=(k == 0), stop=(k == 7))
                    return ins
                S.op("pe", mm, reads=(xTb,), writes=(pAb[q],))
                S.op("act", lambda: nc.scalar.activation(out=dt_[:, c, :], in_=pA[q][:, :], func=AF.Copy, scale=sc),
                     reads=(pAb[q],), writes=(db,))
            for s in range(4):
                def mm():
                    for hf, (c0, c1) in enumerate(((0, 512), (512, 768))):
                        for k in range(8):
                            ins = nc.tensor.matmul(pV[:, c0:c1], xT[:, k, s * 128:(s + 1) * 128], w[:, k, 1536 + c0:1536 + c1],
                                                   start=(k == 0), stop=(k == 7))
                    return ins
                S.op("pe", mm, reads=(xTb,), writes=(pVb,))
                S.op("dve", lambda: nc.vector.tensor_copy(va[:, s, :, 0:64], pV[:, 0:768].rearrange("p (h d) -> p h d", h=12)),
                     reads=(pVb,), writes=(vab,))
            if real:
                k0 = HALO + t * 512
                S.dma("sp", lambda: nc.sync.dma_start(
                    out=C.qT_s[u].rearrange("(c p) t -> p c t", p=128)[:, :, t * 512:(t + 1) * 512], in_=qf[:]), "a_qf", reads=(qfb,))
                S.dma("sp", lambda: nc.sync.dma_start(
                    out=C.kT_s[u].rearrange("(c p) t -> p c t", p=128)[:, :, k0:k0 + 512], in_=kf[:]), "a_kf", reads=(kfb,))
                S.dma("sp", lambda: nc.sync.dma_start(
                    out=C.v_s[u, k0:k0 + 512, :].rearrange("(s p) f -> p s f", p=128), in_=va[:].rearrange("p s h d -> p s (h d)")),
                    "a_va", reads=(vab,))
                mem_attn_tile(C, MA, u, qm, qmb, t * 512)
            else:
                kv = C.kT_s[u].rearrange("(c p) t -> p c t", p=128)
                S.dma("sp", [lambda: nc.sync.dma_start(out=kv[:, :, 0:HALO], in_=kf[:, :, 0:256]),
                               lambda: nc.sync.dma_start(out=kv[:, :, HALO + UT:UTE], in_=kf[:, :, 256:512])], "a_kf", reads=(kfb,))
                vv = va[:].rearrange("p s h d -> p s (h d)")
                S.dma("sp", [lambda: nc.sync.dma_start(out=C.v_s[u, 0:HALO, :].rearrange("(s p) f -> p s f", p=128), in_=vv[:, 0:2, :]),
                               lambda: nc.sync.dma_start(out=C.v_s[u, HALO + UT:UTE, :].rearrange("(s p) f -> p s f", p=128), in_=vv[:, 2:4, :])],
                      "a_va", reads=(vab,))
        S.barrier()


MASKV = -30000.0
SPECIAL_PAIRS = (0, 1, 30, 31)


def pair_chunks(p):
    if p == 0:
        return list(range(0, 6))
    if p == 31:
        return list(range(30, 36))
    return list(range(p, p + 5))


def na_table(rpb0, row_offset, rows_total, p):
    tbl = np.full((2, 64, 12, 6, 2, 64), MASKV, np.float32)
    qc = np.arange(64)
    cstart = np.clip(qc - 8, 0, 48)
    kc = np.arange(64)
    colok = (kc[:, None] >= cstart[None, :]) & (kc[:, None] < cstart[None, :] + 16)
    dcol = np.clip(kc[:, None] - qc[None, :] + 15, 0, 30)
    for j, m in enumerate(pair_chunks(p)):
        for e in range(2):
            KR = row_offset + 2 * m + e - 4
            if KR < 0 or KR >= rows_total:
                continue
            for a in range(2):
                R = row_offset + 2 * p + a
                rs = min(max(R - 4, 0), rows_total - 8)
                if not (rs <= KR < rs + 8):
                    continue
                drow = KR - R + 7
                blk = rpb0[:, drow, :][:, dcol]
                blk = np.where(colok[None], blk, np.float32(MASKV))
                tbl[e, :, :, j, a, :] = np.transpose(blk, (1, 0, 2))
    return tbl.reshape(128, 12, 6, 128)


def stage_na(C):
    nc, S = C.nc, C.S
    units = range(UNITS) if not getattr(C, "units", None) else C.units
    pairs = getattr(C, "pairs", None)
    with ExitStack() as es:
        wout, _ = load_weight_bf16(C, es, C.w_out[0], 8, D, "wo", piece_cols=1024)
        T = dict(st=sb(es, nc, "n_st", [128, 2, 6], F32), stb=S.buf(),
                 mv=sb(es, nc, "n_mv", [128, 2], F32), mvb=S.buf(),
                 rs=sb(es, nc, "n_rs", [128, 2], F32), rsb=S.buf(),
                 gb=load_gb(C, es, 0, "n_gb"))
        tbI = sb(es, nc, "n_tbI", [128, 12, 6, 128], BF16); tbIb = S.buf()
        tbS = sb(es, nc, "n_tbS", [128, 12, 6, 128], BF16); tbSb = S.buf()
        stg = [sb(es, nc, f"n_stg{i}", [128, 768], F32) for i in range(2)]; stgb = [S.buf() for _ in range(2)]
        kh = sb(es, nc, "n_kh", [128, 6, 2560], BF16); khb = S.buf()
        vh = sb(es, nc, "n_vh", [128, 20, 780], BF16); vhb = S.buf()
        qh = [sb(es, nc, f"n_qh{i}", [128, 6, 512], BF16) for i in range(2)]; qhb = [S.buf() for _ in range(2)]
        PT = sb(es, nc, "n_PT", [128, 12, 768], BF16); PTb = [S.buf() for _ in range(12)]
        x0 = [sb(es, nc, f"n_x0{i}", [128, D], F32) for i in range(2)]; x0b = [S.buf() for _ in range(2)]
        mix = [sb(es, nc, f"n_mix{i}", [128, D], BF16) for i in range(2)]; mixb = [S.buf() for _ in range(2)]
        mixT = sb(es, nc, "n_mixT", [128, 8, 128], BF16); mixTb = S.buf()
        yt = [sb(es, nc, f"n_y{i}", [128, D], F32) for i in range(2)]; ytb = [S.buf() for _ in range(2)]
        rd = sb(es, nc, "n_rd", [128, 12], F32); rdb = S.buf()
        pS = [ps(es, nc, f"n_pS{i}", [128, 1024]) for i in range(2)]; pSb = [S.buf() for _ in range(2)]
        pO = ps(es, nc, "n_pO", [128, 1024]); pOb = S.buf()
        pX = ps(es, nc, "n_pX", [128, 512]); pXb = S.buf()
        pZ = ps(es, nc, "n_pZ", [128, 512]); pZb = S.buf()
        pXh = pX[:].bitcast(BF16)

        ncast = [0]

        def load_table(dst, dstb, idx):
            for h in range(12):
                j = ncast[0] % 2
                ncast[0] += 1
                S.dma("sp", lambda: nc.sync.dma_start(out=stg[j][:], in_=C.natab[idx, :, h * 768:(h + 1) * 768]),
                      f"n_stg{j}", writes=(stgb[j],))
                S.op("pool", lambda: nc.gpsimd.tensor_copy(dst[:, h, :, :], stg[j][:].rearrange("p (j q) -> p j q", j=6)),
                     reads=(stgb[j],), writes=(dstb,))

        load_table(tbI, tbIb, 0)
        pend = [None]
        npair = [0]

        def outproj(u, p, jj):
            def f():
                def tr():
                    for k in range(8):
                        ins = nc.tensor.transpose(pXh[:, k * 128:(k + 1) * 128], mix[jj][:, k * 128:(k + 1) * 128], C.ident_b[:])
                    return ins
                S.op("pe", tr, reads=(mixb[jj],), writes=(pXb,))
                S.op("act", lambda: nc.scalar.copy(mixT[:], pXh.rearrange("p (k t) -> p k t", k=8)), reads=(pXb,), writes=(mixTb,))
                for hf, (pz, pzb) in enumerate(((pZ, pZb), (pX, pXb))):
                    def mm():
                        for k in range(8):
                            ins = nc.tensor.matmul(pz[:, :], mixT[:, k, :], wout[:, k, hf * 512:(hf + 1) * 512],
                                                   start=(k == 0), stop=(k == 7))
                        return ins
                    S.op("pe", mm, reads=(mixTb,), writes=(pzb,))
                    S.op("dve", lambda: nc.vector.scalar_tensor_tensor(
                        out=yt[jj][:, hf * 512:(hf + 1) * 512], in0=x0[jj][:, hf * 512:(hf + 1) * 512], scalar=ALPHA,
                        in1=pz[:, :], op0=ALU.mult, op1=ALU.add), reads=(x0b[jj], pzb), writes=(ytb[jj],))
                layernorm_store(C, T, yt[jj], ytb[jj], yt[jj][:], ytb[jj])
                S.dma("sp", lambda: nc.sync.dma_start(out=C.x1_s[u * UT + p * 128: u * UT + (p + 1) * 128, :], in_=yt[jj][:]),
                      f"n_y{jj}", reads=(ytb[jj],))
            return f

        for u in units:
            for half in range(2):
                plist = [p for p in range(half * 16, half * 16 + 16) if pairs is None or p in pairs]
                if not plist:
                    continue
                tok0 = half * 2048
                kv = C.kT_s[u].rearrange("(c p) t -> p c t", p=128)
                S.dma("sp", [(lambda c=c: nc.sync.dma_start(out=kh[:, c, :], in_=kv[:, c, tok0:tok0 + 2560])) for c in range(6)],
                      "n_kh", writes=(khb,))
                vv = C.v_s[u, tok0:tok0 + 2560, :].rearrange("(s p) f -> p s f", p=128)
                S.dma("sp", [(lambda g=g: nc.sync.dma_start(out=vh[:, g * 5:(g + 1) * 5, :], in_=vv[:, g * 5:(g + 1) * 5, :])) for g in range(4)],
                      "n_vh", writes=(vhb,))
                qcur = None
                for p in plist:
                    jj = npair[0] % 2
                    npair[0] += 1
                    pp = p - half * 16
                    if qcur is None or p // 4 != qcur[0]:
                        qi = (p // 4) % 2
                        qv = C.qT_s[u].rearrange("(c p) t -> p c t", p=128)
                        S.dma("sp", lambda: nc.sync.dma_start(out=qh[qi][:], in_=qv[:, :, (p // 4) * 512:(p // 4) * 512 + 512]),
                              f"n_qh{qi}", writes=(qhb[qi],))
                        qcur = (p // 4, qi)
                    qi = qcur[1]
                    if p in SPECIAL_PAIRS:
                        idx = 1 + 4 * u + SPECIAL_PAIRS.index(p)
                        load_table(tbS, tbSb, idx)
                        tb, tbb = tbS, tbSb
                    else:
                        tb, tbb = tbI, tbIb
                    S.dma("sp", [lambda: nc.sync.dma_start(out=x0[jj][:], in_=C.xe[u, HALO + p * 128:HALO + (p + 1) * 128, :]),
                                 lambda: nc.sync.dma_start(out=mix[jj][:, 768:1024], in_=C.mix_s[u, p * 128:(p + 1) * 128, 768:1024])],
                          f"n_in{jj}", writes=(x0b[jj], mixb[jj]))
                    chunks = pair_chunks(p)
                    nj = len(chunks)
                    qoff = (p % 4) * 128

                    def qk(h):
                        hp = (h % 2) * 64
                        hc = h // 2
                        psS = pS[h % 2]

                        def mm():
                            for j, m in enumerate(chunks):
                                lm = m - half * 16
                                nc.tensor.matmul(psS[:, j * 128:(j + 1) * 128], kh[hp:hp + 64, hc, lm * 128:(lm + 1) * 128],
                                                 qh[qi][hp:hp + 64, hc, qoff:qoff + 128], start=True, stop=False)
                                ins = nc.tensor.matmul(psS[:, j * 128:(j + 1) * 128], C.ident_b[:], tb[:, h, j, :],
                                                       start=False, stop=True)
                            return ins
                        S.op("pe", mm, reads=(khb, qhb[qi], tbb), writes=(pSb[h % 2],))
                        S.op("act", lambda: nc.scalar.activation(out=PT[:, h, 0:nj * 128], in_=psS[:, 0:nj * 128], func=AF.Exp),
                             reads=(pSb[h % 2],), writes=(PTb[h],))

                    def pv(h):
                        def mm():
                            for j, m in enumerate(chunks):
                                lm = m - half * 16
                                oc = (h // 6) * 512 + (h % 6) * 65
                                ins = nc.tensor.matmul(pO[:, oc:oc + 65], PT[:, h, j * 128:(j + 1) * 128],
                                                       vh[:, lm, h * 65:(h + 1) * 65], start=(j == 0), stop=(j == nj - 1))
                            return ins
                        S.op("pe", mm, reads=(PTb[h], vhb), writes=(pOb,))

                    qk(0)
                    qk(1)
                    for h in range(12):
                        if h + 2 < 12:
                            qk(h + 2)
                        pv(h)
                        if h == 4 and pend[0] is not None:
                            pend[0]()
                            pend[0] = None
                    pov = pO[:].rearrange("p (g x) -> p g x", g=2)[:, :, 0:390].rearrange("p g (h d) -> p g h d", h=6)
                    rdv = rd[:].rearrange("p (g h) -> p g h", g=2)
                    S.op("dve", lambda: nc.vector.reciprocal(rdv, pov[:, :, :, 64]), reads=(pOb,), writes=(rdb,))
                    S.op("dve", lambda: nc.vector.tensor_tensor(
                        out=mix[jj][:, 0:768].rearrange("p (g h d) -> p g h d", g=2, h=6), in0=pov[:, :, :, 0:64],
                        in1=rdv.unsqueeze(3).to_broadcast([128, 2, 6, 64]), op=ALU.mult),
                        reads=(pOb, rdb), writes=(mixb[jj],))
                    pend[0] = outproj(u, p, jj)
        if pend[0] is not None:
            pend[0]()
        S.barrier()


HC = 64
NCH = UT // HC


def stage_inproj_b(C):
    nc, S = C.nc, C.S
    units = range(UNITS) if not getattr(C, "units", None) else C.units
    TT = 256
    NS = TT // 128
    NC4 = TT // HC
    src = C.x2_s
    with ExitStack() as es:
        w, _ = load_weight_bf16(C, es, C.w_in_b, 8, 4096, "wb")
        lg = sb(es, nc, "b_lg", [128, 2, 6], F32)
        lb = sb(es, nc, "b_lb", [128, 6], F32)
        oml = sb(es, nc, "b_oml", [128, 6], F32)
        noml = sb(es, nc, "b_noml", [128, 6], F32)
        nwb = sb(es, nc, "b_nwb", [128, 768], F32)
        mask = sb(es, nc, "b_mask", [128, TT], F32)
        cb = S.buf()
        S.dma("sp", [lambda: nc.sync.dma_start(out=lg[:], in_=C.lb_logits[:, :, :]),
                     lambda: nc.sync.dma_start(out=nwb[:], in_=C.d_nwb[:, :])], "b_c", writes=(cb,))
        S.op("dve", lambda: nc.vector.tensor_tensor(out=lb[:], in0=lg[:, 1, :], in1=lg[:, 0, :], op=ALU.subtract), reads=(cb,), writes=(cb,))
        S.op("act", lambda: nc.scalar.activation(out=lb[:], in_=lb[:], func=AF.Sigmoid), reads=(cb,), writes=(cb,))
        S.op("dve", lambda: nc.vector.tensor_scalar(oml[:], lb[:], -1.0, 1.0, op0=ALU.mult, op1=ALU.add), reads=(cb,), writes=(cb,))
        S.op("dve", lambda: nc.vector.tensor_scalar(noml[:], oml[:], -1.0, None, op0=ALU.mult), reads=(cb,), writes=(cb,))
        S.op("dve", lambda: nc.vector.memset(mask[:], 1.0), writes=(cb,))
        S.op("dve", lambda: nc.vector.memset(mask[:].rearrange("p (c t) -> p c t", t=HC)[:, :, 0:1], 0.0), writes=(cb,))
        S.barrier()

        xin = [sb(es, nc, f"b_x{i}", [128, NS, D], F32) for i in range(2)]; xinb = [S.buf() for _ in range(2)]
        xT = sb(es, nc, "b_xT", [128, 8, TT], BF16); xTb = S.buf()
        sq = sb(es, nc, "b_sq", [128, 6, TT], F32); sqb = [S.buf() for _ in range(6)]
        sg = sb(es, nc, "b_sg", [128, 12, TT], F32); sgb = [S.buf() for _ in range(12)]
        wk = {n: sb(es, nc, f"b_{n}", [128, TT], F32) for n in ("g", "k", "b", "t3", "t4", "e1", "e2", "e3")}
        wkb = {n: S.buf() for n in wk}
        outs = {n: [sb(es, nc, f"b_o{n}{i}", [128, TT], BF16) for i in range(2)] for n in ("QF", "KF", "KHF", "QB", "KB", "KHB")}
        outb = {n: [S.buf() for _ in range(2)] for n in outs}
        ll = sb(es, nc, "b_ll", [128, 2, 6, NC4], F32); llb = S.buf()
        qm = sb(es, nc, "b_qm", [128, 2, TT], BF16); qmb = S.buf()
        vo = sb(es, nc, "b_vo", [128, NS, 768], BF16); vob = S.buf()
        go = sb(es, nc, "b_go", [128, NS, 768], F32); gob = S.buf()
        gs = sb(es, nc, "b_gs", [128, 768], F32); gsb = S.buf()
        pT = ps(es, nc, "b_pT", [128, 1024]); pTb = S.buf()
        pA = [ps(es, nc, f"b_pA{i}", [128, 512]) for i in range(2)]; pAb = [S.buf() for _ in range(2)]
        pV = ps(es, nc, "b_pV", [128, 1024]); pVb = S.buf()
        MA = mem_attn_tiles(C, es, pA, pAb)
        ntile = UT // TT
        tiles = [(u, t) for u in units for t in range(ntile)]
        if getattr(C, "max_tiles", None):
            tiles = tiles[:C.max_tiles]
        dst = dict(QF=C.QF, KF=C.KF, KHF=C.KHF, QB=C.QB, KB=C.KB, KHB=C.KHB)

        def load(i):
            u, t = tiles[i]
            j = i % 2
            S.dma("sp", lambda: nc.sync.dma_start(
                out=xin[j][:], in_=src[u * UT + t * TT: u * UT + (t + 1) * TT, :].rearrange("(s p) d -> p s d", p=128)),
                f"b_x{j}", writes=(xinb[j],))

        load(0)
        na = 0
        no = 0
        for i, (u, t) in enumerate(tiles):
            j = i % 2
            if i + 1 < len(tiles):
                load(i + 1)
            for s in range(NS):
                def tr():
                    for k in range(8):
                        ins = nc.tensor.transpose(pT[:, k * 128:(k + 1) * 128], xin[j][:, s, k * 128:(k + 1) * 128], C.ident_f[:])
                    return ins
                S.op("pe", tr, reads=(xinb[j],), writes=(pTb,))
                S.op("dve", lambda: nc.vector.tensor_copy(xT[:, :, s * 128:(s + 1) * 128], pT[:].rearrange("p (k t) -> p k t", k=8)),
                     reads=(pTb,), writes=(xTb,))

            def fm(col, fn):
                nonlocal na
                q = na % 2
                na += 1

                def mm():
                    for k in range(8):
                        ins = nc.tensor.matmul(pA[q][:, 0:TT], w[:, k, col:col + 128], xT[:, k, :], start=(k == 0), stop=(k == 7))
                    return ins
                S.op("pe", mm, reads=(xTb,), writes=(pAb[q],))
                fn(pA[q][:, 0:TT], pAb[q])
            for c in range(6):
                fm(c * 128, lambda pv, pb: S.op("act", lambda: nc.scalar.activation(out=sq[:, c, :], in_=pv, func=AF.Silu),
                                                reads=(pb,), writes=(sqb[c],)))
            for s in range(NS):
                for which in (1, 0):
                    base = 2304 + which * 768

                    def mm():
                        for (c0, c1) in ((0, 512), (512, 768)):
                            for k in range(8):
                                ins = nc.tensor.matmul(pV[:, c0:c1], xT[:, k, s * 128:(s + 1) * 128], w[:, k, base + c0:base + c1],
                                                       start=(k == 0), stop=(k == 7))
                        return ins
                    S.op("pe", mm, reads=(xTb,), writes=(pVb,))
                    if which == 0:
                        S.op("dve", lambda: nc.vector.tensor_copy(vo[:, s, :], pV[:, 0:768]), reads=(pVb,), writes=(vob,))
                    else:
                        S.op("act", lambda: nc.scalar.activation(out=gs[:], in_=pV[:, 0:768], func=AF.Silu), reads=(pVb,), writes=(gsb,))
                        S.op("pool", lambda: nc.gpsimd.tensor_tensor(out=go[:, s, :], in0=gs[:], in1=nwb[:], op=ALU.mult),
                             reads=(gsb,), writes=(gob,))
            for c in range(6):
                for d in range(2):
                    fm(768 * (d + 1) + c * 128, lambda pv, pb: S.op(
                        "act", lambda: nc.scalar.activation(out=sg[:, d * 6 + c, :], in_=pv, func=AF.Sigmoid),
                        reads=(pb,), writes=(sgb[d * 6 + c],)))
            for c in range(2):
                fm(3840 + c * 128, lambda pv, pb: S.op(
                    "act", lambda: nc.scalar.activation(out=qm[:, c, :], in_=pv, func=AF.Copy, scale=0.125),
                    reads=(pb,), writes=(qmb,)))
            r0 = u * UT + t * TT
            S.dma("sp", lambda: nc.sync.dma_start(out=C.V1[r0:r0 + TT, :].rearrange("(s p) f -> p s f", p=128), in_=vo[:]),
                  "b_vo", reads=(vob,))
            S.dma("sp", lambda: nc.sync.dma_start(out=C.G1[r0:r0 + TT, :].rearrange("(s p) f -> p s f", p=128), in_=go[:]),
                  "b_go", reads=(gob,))
            b3 = wk["b"][:].rearrange("p (c t) -> p c t", t=HC)
            import os as _os
            for c in range(0 if not _os.environ.get("SKIP_P2") else 6, 6):
                for d in range(2):
                    sgi = sg[:, d * 6 + c, :]
                    sgib = sgb[d * 6 + c]
                    S.op("act", lambda: nc.scalar.activation(out=wk["g"][:], in_=sgi, func=AF.Ln, bias=lb[:, c:c + 1], scale=oml[:, c:c + 1]),
                         reads=(sgib,), writes=(wkb["g"],))
                    S.op("dve", lambda: nc.vector.tensor_scalar(wk["k"][:], sgi, noml[:, c:c + 1], oml[:, c:c + 1], op0=ALU.mult, op1=ALU.add),
                         reads=(sgib,), writes=(wkb["k"],))
                    S.op("dve", lambda: nc.vector.tensor_tensor_scan(wk["b"][:], mask[:], wk["g"][:], 0.0, ALU.mult, ALU.add),
                         reads=(wkb["g"],), writes=(wkb["b"],))
                    S.op("dve", lambda: nc.vector.tensor_copy(ll[:, d, c, :], b3[:, :, HC - 1]), reads=(wkb["b"],), writes=(llb,))
                    S.op("dve", lambda: nc.vector.tensor_tensor(
                        out=wk["t3"][:].rearrange("p (c t) -> p c t", t=HC), in0=b3[:, :, HC - 1:HC].to_broadcast([128, NC4, HC]),
                        in1=b3, op=ALU.subtract), reads=(wkb["b"],), writes=(wkb["t3"],))
                    if d == 0:
                        a1, a2, a3 = wk["b"], wk["b"], wk["t3"]
                        a1b, a2b, a3b = wkb["b"], wkb["b"], wkb["t3"]
                        names = ("QF", "KF", "KHF")
                    else:
                        S.op("dve", lambda: nc.vector.tensor_tensor(out=wk["t4"][:], in0=wk["b"][:], in1=wk["g"][:], op=ALU.subtract),
                             reads=(wkb["b"], wkb["g"]), writes=(wkb["t4"],))
                        S.op("dve", lambda: nc.vector.tensor_tensor(out=wk["t3"][:], in0=wk["t3"][:], in1=wk["g"][:], op=ALU.add),
                             reads=(wkb["t3"], wkb["g"]), writes=(wkb["t3"],))
                        a1, a2, a3 = wk["t3"], wk["t3"], wk["t4"]
                        a1b, a2b, a3b = wkb["t3"], wkb["t3"], wkb["t4"]
                        names = ("QB", "KB", "KHB")
                    S.op("act", lambda: nc.scalar.activation(out=wk["e1"][:], in_=a1[:], func=AF.Exp), reads=(a1b,), writes=(wkb["e1"],))
                    S.op("act", lambda: nc.scalar.activation(out=wk["e2"][:], in_=a2[:], func=AF.Exp, scale=-1.0), reads=(a2b,), writes=(wkb["e2"],))
                    S.op("act", lambda: nc.scalar.activation(out=wk["e3"][:], in_=a3[:], func=AF.Exp), reads=(a3b,), writes=(wkb["e3"],))
                    oj = no % 2
                    no += 1
                    for nm, e_, other, ob_ in ((names[0], "e1", sq[:, c, :], sqb[c]), (names[1], "e2", wk["k"][:], wkb["k"]),
                                               (names[2], "e3", wk["k"][:], wkb["k"])):
                        S.op("pool", lambda: nc.gpsimd.tensor_tensor(out=outs[nm][oj][:], in0=other, in1=wk[e_][:], op=ALU.mult),
                             reads=(ob_, wkb[e_]), writes=(outb[nm][oj],))
                        S.dma("sp", lambda: nc.sync.dma_start(out=dst[nm][u, c * 128:(c + 1) * 128, t * TT:(t + 1) * TT], in_=outs[nm][oj][:]),
                              f"b_o{nm}{oj}", reads=(outb[nm][oj],))
            for d, LD in enumerate((C.LF, C.LB)):
                S.dma("sp", lambda: nc.sync.dma_start(
                    out=LD[u].rearrange("(c p) n -> p c n", p=128)[:, :, t * NC4:(t + 1) * NC4], in_=ll[:, d, :, :]),
                    "b_ll", reads=(llb,))
            if not _os.environ.get("SKIP_MA"):
                mem_attn_tile(C, MA, u, qm, qmb, t * TT, ntok=TT)
        S.barrier()


def hg_tiles(C, es):
    nc, S = C.nc, C.S
    H = Ctx()
    H.Sf = sb(es, nc, "h_Sf", [128, 6, 128], F32); H.Sfb = S.buf()
    H.Sb = sb(es, nc, "h_Sb", [128, 6, 128], F32); H.Sbb = S.buf()
    H.Sf16 = sb(es, nc, "h_Sf16", [128, 6, 128], BF16); H.Sf16b = S.buf()
    H.Sb16 = sb(es, nc, "h_Sb16", [128, 6, 128], BF16); H.Sb16b = S.buf()
    H.Sb16x = [H.Sb16, sb(es, nc, "h_Sb16x", [128, 6, 128], BF16)]; H.Sb16xb = [H.Sb16b, S.buf()]
    H.LF = sb(es, nc, "h_LF", [128, 6, NCH], F32); H.LB = sb(es, nc, "h_LB", [128, 6, NCH], F32)
    H.DF = sb(es, nc, "h_DF", [128, 6, NCH], F32); H.DB = sb(es, nc, "h_DB", [128, 6, NCH], F32)
    H.Lb = S.buf()
    H.khT = sb(es, nc, "h_khT", [64, 768], BF16); H.khTb = S.buf()
    H.pB = ps(es, nc, "h_pB", [128, 1024], BF16); H.pBb = S.buf()
    H.pD = ps(es, nc, "h_pD", [128, 1024]); H.pDb = S.buf()
    H.sbsd = S.buf()
    return H


def hg_load_decays(C, H, u):
    nc, S = C.nc, C.S
    nv = (getattr(C, "hg_ng", None) or (UT // 256)) * (256 // HC)
    S.dma("sp", [lambda: nc.sync.dma_start(out=H.LF[:, :, 0:nv], in_=C.LF[u].rearrange("(c p) n -> p c n", p=128)[:, :, 0:nv]),
                 lambda: nc.sync.dma_start(out=H.LB[:, :, 0:nv], in_=C.LB[u].rearrange("(c p) n -> p c n", p=128)[:, :, 0:nv])], "h_L", writes=(H.Lb,))
    S.op("act", lambda: nc.scalar.activation(out=H.DF[:, :, 0:nv], in_=H.LF[:, :, 0:nv], func=AF.Exp), reads=(H.Lb,), writes=(H.Lb,))
    S.op("act", lambda: nc.scalar.activation(out=H.DB[:, :, 0:nv], in_=H.LB[:, :, 0:nv], func=AF.Exp), reads=(H.Lb,), writes=(H.Lb,))


def hg_state_step(C, H, khsrc, khsrcb, cols, vch, vb, St, Stb, S16, S16b, Dt, n):
    nc, S = C.nc, C.S

    def tr():
        for h in range(6):
            ins = nc.tensor.transpose(H.pB[0:64, h * 128:(h + 1) * 128], khsrc[:, h, cols], C.ident_b[:])
        return ins
    S.op("pe", tr, reads=(khsrcb,), writes=(H.pBb,))
    S.op("act", lambda: nc.scalar.copy(H.khT[:], H.pB[0:64, 0:768]), reads=(H.pBb,), writes=(H.khTb,))

    def ds():
        for h in range(6):
            ins = nc.tensor.matmul(H.pD[:, h * 128:(h + 1) * 128], H.khT[:, h * 128:(h + 1) * 128], vch[:, h * 128:(h + 1) * 128],
                                   start=True, stop=True)
        return ins
    S.op("pe", ds, reads=(H.khTb, vb), writes=(H.pDb,))

    def upd():
        for h in range(6):
            ins = nc.vector.scalar_tensor_tensor(out=St[:, h, :], in0=St[:, h, :], scalar=Dt[:, h, n:n + 1],
                                                 in1=H.pD[:, h * 128:(h + 1) * 128], op0=ALU.mult, op1=ALU.add)
        return ins
    S.op("dve", upd, reads=(H.pDb, H.Lb), writes=(Stb,))
    S.op("act", lambda: nc.scalar.copy(S16[:], St[:]), reads=(Stb,), writes=(S16b,))


def stage_hgrn(C):
    nc, S = C.nc, C.S
    units = range(UNITS) if not getattr(C, "units", None) else C.units
    GT = 256
    GC = GT // HC
    NG = getattr(C, "hg_ng", None) or UT // GT
    with ExitStack() as es:
        wout, _ = load_weight_bf16(C, es, C.w_out[1], 8, D, "wo1", piece_cols=1024)
        T = dict(st=sb(es, nc, "h_st", [128, 2, 6], F32), stb=S.buf(),
                 mv=sb(es, nc, "h_mv", [128, 2], F32), mvb=S.buf(),
                 rs=sb(es, nc, "h_rs", [128, 2], F32), rsb=S.buf(),
                 gb=load_gb(C, es, 4, "h_gb"))
        H = hg_tiles(C, es)
        tri = sb(es, nc, "h_tri", [64, 768], F32)
        S.dma("sp", lambda: nc.sync.dma_start(out=tri[:], in_=C.d_tri[:, :]), "h_tri", writes=(S.buf(),))
        S.barrier()
        names = ("QF", "KF", "QB", "KB", "KHF")
        G = []
        for i in range(2):
            g = Ctx()
            g.t = {n: sb(es, nc, f"h_{n}{i}", [128, 6, GT], BF16) for n in names}
            g.v = sb(es, nc, f"h_v{i}", [64, GC, 768], BF16)
            g.g1 = sb(es, nc, f"h_g1{i}", [64, GC, 768], F32)
            g.sbs = sb(es, nc, f"h_sbs{i}", [128, GC, 768], BF16)
            g.x = sb(es, nc, f"h_x{i}", [128, GT // 128, D], F32)
            g.b = S.buf()
            G.append(g)
        khb_t = [sb(es, nc, f"h_khb{i}", [128, 6, GT], BF16) for i in range(2)]; khb_b = [S.buf() for _ in range(2)]
        vb_t = [sb(es, nc, f"h_vb{i}", [64, GC, 768], BF16) for i in range(2)]; vb_b = [S.buf() for _ in range(2)]
        at = sb(es, nc, "h_at", [64, 768], BF16); atb = S.buf()
        osb = sb(es, nc, "h_osb", [64, 768], F32); osbb = S.buf()
        sqr = sb(es, nc, "h_sqr", [64, 768], F32); sqrb = S.buf()
        ss = sb(es, nc, "h_ss", [64, 3, 6], F32); ssb = S.buf()
        mixc = [sb(es, nc, f"h_mixc{i}", [64, D], BF16) for i in range(2)]; mixcb = [S.buf() for _ in range(2)]
        mixT = sb(es, nc, "h_mixT", [128, 8, 128], BF16); mixTb = S.buf()
        yt = [sb(es, nc, f"h_y{i}", [128, D], F32) for i in range(2)]; ytb = [S.buf() for _ in range(2)]
        eps_r = sb(es, nc, "h_epsr", [64, 1], F32)
        S.op("dve", lambda: nc.vector.memset(eps_r[:], RMS_EPS), writes=(S.buf(),))
        pAT = ps(es, nc, "h_pAT", [64, 1024]); pATb = S.buf()
        pOo = ps(es, nc, "h_pOo", [64, 1024]); pOob = S.buf()
        S.barrier()
        ny = 0
        cfl = sb(es, nc, "h_cfl", [128, 1], F32); cflb = S.buf()
        S.dma("sp", lambda: nc.sync.dma_start(out=cfl[:], in_=C.d_cflag[:, :]), "h_cfl", writes=(cflb,))
        ulist = list(units)
        for u in reversed(ulist):
            hg_load_decays(C, H, u)
            if u == ulist[-1]:
                S.op("dve", lambda: nc.vector.memset(H.Sb[:], 0.0), writes=(H.Sbb,))
            else:
                S.op("dve", lambda: nc.vector.tensor_scalar(H.Sb[:], H.Sb[:], cfl[:, 0:1], None, op0=ALU.mult),
                     reads=(H.Sbb, cflb), writes=(H.Sbb,))
            cur = 0
            S.op("act", lambda: nc.scalar.copy(H.Sb16x[0][:], H.Sb[:]), reads=(H.Sbb,), writes=(H.Sb16xb[0],))

            def loadb(gi):
                j = gi % 2
                r0 = u * UT + gi * GT
                S.dma("sp", lambda: nc.sync.dma_start(out=khb_t[j][:], in_=C.KHB[u].rearrange("(c p) t -> p c t", p=128)[:, :, gi * GT:(gi + 1) * GT]),
                      f"h_khb{j}", writes=(khb_b[j],))
                S.dma("sp", lambda: nc.sync.dma_start(out=vb_t[j][:], in_=C.V1[r0:r0 + GT, :].rearrange("(c p) f -> p c f", p=64)),
                      f"h_vb{j}", writes=(vb_b[j],))
            glist = list(range(NG - 1, -1, -1))
            loadb(glist[0])
            for ii, gi in enumerate(glist):
                j = gi % 2
                if ii + 1 < len(glist):
                    loadb(glist[ii + 1])
                for cc in range(GC - 1, -1, -1):
                    n = gi * GC + cc
                    S.dma("sp", lambda: nc.sync.dma_start(out=C.SBs[u, n], in_=H.Sb16x[cur][:].rearrange("p h d -> p (h d)")), f"h_sbst{cur}",
                          reads=(H.Sb16xb[cur],), writes=(H.sbsd,))
                    hg_state_step(C, H, khb_t[j], khb_b[j], slice(cc * HC, (cc + 1) * HC), vb_t[j][:, cc, :], vb_b[j],
                                  H.Sb, H.Sbb, H.Sb16x[1 - cur], H.Sb16xb[1 - cur], H.DB, n)
                    cur = 1 - cur
        for u in ulist:
            hg_load_decays(C, H, u)
            if u == ulist[0]:
                S.op("dve", lambda: nc.vector.memset(H.Sf[:], 0.0), writes=(H.Sfb,))
            else:
                S.op("dve", lambda: nc.vector.tensor_scalar(H.Sf[:], H.Sf[:], cfl[:, 0:1], None, op0=ALU.mult),
                     reads=(H.Sfb, cflb), writes=(H.Sfb,))
            S.op("act", lambda: nc.scalar.copy(H.Sf16[:], H.Sf[:]), reads=(H.Sfb,), writes=(H.Sf16b,))

            def loadf(gi):
                g = G[gi % 2]
                r0 = u * UT + gi * GT
                fns = [(lambda nm=nm: nc.sync.dma_start(out=g.t[nm][:], in_=getattr(C, nm)[u].rearrange("(c p) t -> p c t", p=128)[:, :, gi * GT:(gi + 1) * GT]))
                       for nm in names]
                fns.append(lambda: nc.sync.dma_start(out=g.v[:], in_=C.V1[r0:r0 + GT, :].rearrange("(c p) f -> p c f", p=64)))
                fns.append(lambda: nc.sync.dma_start(out=g.g1[:], in_=C.G1[r0:r0 + GT, :].rearrange("(c p) f -> p c f", p=64)))
                fns.append(lambda: nc.sync.dma_start(out=g.sbs[:], in_=C.SBs[u, gi * GC:(gi + 1) * GC].rearrange("n p f -> p n f")))
                fns.append(lambda: nc.sync.dma_start(out=g.x[:], in_=C.x2_s[r0:r0 + GT, :].rearrange("(s p) d -> p s d", p=128)))
                S.dma("sp", fns, f"h_g{gi % 2}", reads=(H.sbsd,), writes=(g.b,))
            loadf(0)
            for gi in range(NG):
                g = G[gi % 2]
                if gi + 1 < NG:
                    loadf(gi + 1)
                for cc in range(GC):
                    n = gi * GC + cc
                    cols = slice(cc * HC, (cc + 1) * HC)
                    mj = n % 2
                    S.dma("sp", lambda: nc.sync.dma_start(out=mixc[mj][:, 768:1024], in_=C.mix_s[u, n * HC:(n + 1) * HC, 768:1024]),
                          f"h_mixc{mj}", writes=(mixcb[mj],))

                    def amm():
                        for d, (kn, qn) in enumerate((("KF", "QF"), ("KB", "QB"))):
                            for h in range(6):
                                o = (d * 6 + h) * 64
                                ins = nc.tensor.matmul(pAT[:, o:o + 64], g.t[kn][:, h, cols], g.t[qn][:, h, cols], start=True, stop=True)
                        return ins
                    S.op("pe", amm, reads=(g.b,), writes=(pATb,))
                    S.op("dve", lambda: nc.vector.tensor_tensor(out=at[:], in0=pAT[:, 0:768], in1=tri[:], op=ALU.mult),
                         reads=(pATb,), writes=(atb,))

                    def omm():
                        for h in range(6):
                            o = pOo[:, h * 128:(h + 1) * 128]
                            vv = g.v[:, cc, h * 128:(h + 1) * 128]
                            nc.tensor.matmul(o, at[:, h * 64:(h + 1) * 64], vv, start=True, stop=False)
                            nc.tensor.matmul(o, at[:, (6 + h) * 64:(7 + h) * 64], vv, start=False, stop=False)
                            nc.tensor.matmul(o, g.t["QF"][:, h, cols], H.Sf16[:, h, :], start=False, stop=False)
                            ins = nc.tensor.matmul(o, g.t["QB"][:, h, cols], g.sbs[:, cc, h * 128:(h + 1) * 128], start=False, stop=True)
                        return ins
                    S.op("pe", omm, reads=(atb, g.b, H.Sf16b), writes=(pOob,))
                    hg_state_step(C, H, g.t["KHF"], g.b, cols, g.v[:, cc, :], g.b, H.Sf, H.Sfb, H.Sf16, H.Sf16b, H.DF, n)
                    S.op("act", lambda: nc.scalar.copy(osb[:], pOo[:, 0:768]), reads=(pOob,), writes=(osbb,))
                    S.op("pool", lambda: nc.gpsimd.tensor_tensor(out=sqr[:], in0=osb[:], in1=osb[:], op=ALU.mult), reads=(osbb,), writes=(sqrb,))
                    S.op("dve", lambda: nc.vector.tensor_reduce(out=ss[:, 0, :], in_=sqr[:].rearrange("p (h d) -> p h d", h=6),
                                                                axis=mybir.AxisListType.X, op=ALU.add), reads=(sqrb,), writes=(ssb,))
                    S.op("act", lambda: nc.scalar.activation(out=ss[:, 1, :], in_=ss[:, 0, :], func=AF.Sqrt, bias=eps_r[:, 0:1], scale=1.0 / 128),
                         reads=(ssb,), writes=(ssb,))
                    S.op("dve", lambda: nc.vector.reciprocal(ss[:, 2, :], ss[:, 1, :]), reads=(ssb,), writes=(ssb,))
                    S.op("dve", lambda: nc.vector.tensor_tensor(out=sqr[:].rearrange("p (h d) -> p h d", h=6), in0=osb[:].rearrange("p (h d) -> p h d", h=6),
                                                                in1=ss[:, 2, :].unsqueeze(2).to_broadcast([64, 6, 128]), op=ALU.mult),
                         reads=(osbb, ssb), writes=(sqrb,))
                    S.op("pool", lambda: nc.gpsimd.tensor_tensor(out=mixc[mj][:, 0:768], in0=sqr[:], in1=g.g1[:, cc, :], op=ALU.mult),
                         reads=(sqrb, g.b), writes=(mixcb[mj],))
                    hh = n % 2

                    def tr():
                        for k in range(8):
                            ins = nc.tensor.transpose(H.pB[:, k * 64:(k + 1) * 64], mixc[mj][:, k * 128:(k + 1) * 128], C.ident_b[0:64, 0:64])
                        return ins
                    S.op("pe", tr, reads=(mixcb[mj],), writes=(H.pBb,))
                    S.op("act", lambda: nc.scalar.copy(mixT[:, :, hh * 64:(hh + 1) * 64], H.pB[:, 0:512].rearrange("p (k t) -> p k t", k=8)),
                         reads=(H.pBb,), writes=(mixTb,))
                    if hh == 1:
                        yj = ny % 2
                        ny += 1
                        sx = (n // 2) % (GT // 128)

                        def mm():
                            for hf in range(2):
                                for k in range(8):
                                    ins = nc.tensor.matmul(H.pD[:, hf * 512:(hf + 1) * 512], mixT[:, k, :], wout[:, k, hf * 512:(hf + 1) * 512],
                                                           start=(k == 0), stop=(k == 7))
                            return ins
                        S.op("pe", mm, reads=(mixTb,), writes=(H.pDb,))
                        S.op("dve", lambda: nc.vector.scalar_tensor_tensor(out=yt[yj][:], in0=g.x[:, sx, :], scalar=ALPHA, in1=H.pD[:, :],
                                                                          op0=ALU.mult, op1=ALU.add), reads=(g.b, H.pDb), writes=(ytb[yj],))
                        layernorm_store(C, T, yt[yj], ytb[yj], yt[yj][:], ytb[yj])
                        r = u * UT + (n // 2) * 128
                        S.dma("sp", lambda: nc.sync.dma_start(out=C.x3_s[r:r + 128, :], in_=yt[yj][:]), f"h_y{yj}", reads=(ytb[yj],))
        S.barrier()


def build_program(exchange=True):
    nc = bass.Bass("TRN2", target_bir_lowering=False)
    C = Ctx()
    C.nc = nc
    C.max_tiles = None
    NT = UNITS * UT

    def din(name, shape, dt=F32):
        return nc.dram_tensor(name, list(shape), dt, kind="ExternalInput").ap()

    def dscr(name, shape, dt):
        return nc.dram_tensor(name, list(shape), dt, kind="Internal").ap()

    C.xe = din("xe", [UNITS, UTE, D])
    C.d_mem = din("mem", [UNITS, 256, D])
    C.w_mem_kv = din("w_mem_kv", [D, 512])
    C.w_in_a = din("w_in_a", [D, 2560])
    C.w_in_b = din("w_in_b", [D, 4096])
    C.w_out = din("w_out", [2, D, D])
    C.w_ff1 = din("w_ff1", [2, D, DFF])
    C.w_ff2 = din("w_ff2", [2, DFF, D])
    C.d_ident = din("ident", [128, 128])
    C.d_vecs = din("vecs", [128, 8, D])
    C.natab = din("natab", [1 + 4 * UNITS, 128, 9216])
    C.d_tri = din("tri", [64, 768])
    C.lb_logits = din("lbl", [128, 2, 6])
    C.d_nwb = din("nwb", [128, 768])
    C.y = nc.dram_tensor("y", [NT, D], F32, kind="ExternalOutput").ap()
    C.qT_s = dscr("qT_s", [UNITS, 768, UT], BF16)
    C.kT_s = dscr("kT_s", [UNITS, 768, UTE], BF16)
    C.v_s = dscr("v_s", [UNITS, UTE, 780], BF16)
    C.mix_s = dscr("mix_s", [UNITS, UT, D], BF16)
    C.x1_s = dscr("x1_s", [NT, D], F32)
    C.x2_s = dscr("x2_s", [NT, D], F32)
    C.x3_s = C.x1_s
    for n in ("QF", "KF", "KHF", "QB", "KB", "KHB"):
        setattr(C, n, dscr(n, [UNITS, 768, UT], BF16))
    C.LF = dscr("LF", [UNITS, 768, NCH], F32)
    C.LB = dscr("LB", [UNITS, 768, NCH], F32)
    C.V1 = dscr("V1", [NT, 768], BF16)
    C.G1 = dscr("G1", [NT, 768], F32)
    C.SBs = dscr("SBs", [UNITS, NCH, 128, 768], BF16)
    C.d_cflag = din("cflag", [128, 1])
    with ExitStack() as es:
        C.S = Sched(nc, es)
        setup_persistent(C, es)
        stage_memkv(C)
        stage_inproj_a(C)
        stage_na(C)
        stage_ffn(C, 0, C.x1_s, C.x2_s)
        stage_inproj_b(C)
        stage_hgrn(C)
        stage_ffn(C, 1, C.x3_s, C.y)
        C.S.barrier()
    return nc, C


def tri_mask():
    s = np.arange(64)[:, None]
    t = np.arange(64)[None, :]
    f = (s <= t).astype(np.float32)
    b = (s >= t).astype(np.float32)
    return np.concatenate([np.tile(f, (1, 6)), np.tile(b, (1, 6))], axis=1)


def kernel(x_prompt, x_sample, mem_prompt, mem_sample, w_mem_kv, w_in_a, rpb, w_in_b, lb_logits, hg_norm_w,
           w_out, ln1_g, ln1_b, w_ff1, w_ff2, ln2_g, ln2_b):
    f = lambda a: np.ascontiguousarray(np.asarray(a, dtype=np.float32))
    x_prompt, x_sample, mem_prompt, mem_sample = f(x_prompt), f(x_sample), f(mem_prompt), f(mem_sample)
    rpb0 = f(rpb)[0]
    vec = np.stack([f(ln1_g)[0], f(ln1_b)[0], f(ln2_g)[0], f(ln2_b)[0], f(ln1_g)[1], f(ln1_b)[1], f(ln2_g)[1], f(ln2_b)[1]])
    shared = dict(
        w_mem_kv=f(w_mem_kv), w_in_a=f(w_in_a)[0], w_in_b=f(w_in_b)[0], w_out=f(w_out), w_ff1=f(w_ff1), w_ff2=f(w_ff2),
        ident=np.eye(128, dtype=np.float32), vecs=np.ascontiguousarray(np.broadcast_to(vec[None], (128, 8, D))),
        tri=tri_mask(), lbl=np.ascontiguousarray(f(lb_logits).reshape(2, 6, 128).transpose(2, 0, 1)),
        nwb=np.ascontiguousarray(np.broadcast_to(np.tile(f(hg_norm_w)[0], 6)[None], (128, 768))))
    tab_int = na_table(rpb0, 64, 256, 8)
    tab_true = [na_table(rpb0, 0, 64, p) for p in SPECIAL_PAIRS]
    tab_seg = [[na_table(rpb0, 64 * j, 256, p) for p in SPECIAL_PAIRS] for j in range(4)]
    prompt_of = {}
    pi = 0
    for c in range(2, NCORES):
        for u in range(3 if c < 6 else 2):
            prompt_of[(c, u)] = pi
            pi += 1
    assert pi == x_prompt.shape[0]
    in_maps = []
    for c in range(NCORES):
        xe = np.zeros((UNITS, UTE, D), np.float32)
        mem = np.zeros((UNITS, 256, D), np.float32)
        tabs = [tab_int]
        for u in range(UNITS):
            if c < 2:
                lo, hi = u * UT - HALO, (u + 1) * UT + HALO
                clo, chi = max(lo, 0), min(hi, x_sample.shape[1])
                xe[u, clo - lo:chi - lo] = x_sample[c, clo:chi]
                mem[u] = mem_sample[c]
                tabs += tab_seg[u]
            else:
                if (c, u) in prompt_of:
                    xe[u, HALO:HALO + UT] = x_prompt[prompt_of[(c, u)]]
                    mem[u] = mem_prompt[prompt_of[(c, u)]]
                tabs += tab_true
        natab = np.stack(tabs).reshape(1 + 4 * UNITS, 128, 9216)
        in_maps.append(dict(shared, xe=xe, mem=mem, natab=natab,
                            cflag=np.full((128, 1), 1.0 if c < 2 else 0.0, np.float32)))
    nc, _ = build_program()
    res = run_bass_kernel_spmd(nc, in_maps, core_ids=list(range(NCORES)))
    y_prompt = np.empty((16, UT, D), np.float32)
    y_sample = np.empty((2, 4 * UT, D), np.float32)
    for c in range(NCORES):
        y = np.asarray(res.results[c]["y"]).reshape(UNITS, UT, D)
        for u in range(UNITS):
            if c < 2:
                y_sample[c, u * UT:(u + 1) * UT] = y[u]
            elif (c, u) in prompt_of:
                y_prompt[prompt_of[(c, u)]] = y[u]
    return (y_prompt, y_sample)
```

```python
import numpy as np
from contextlib import ExitStack
import concourse.bass as bass
import concourse.mybir as mybir
from concourse.bass_utils import run_bass_kernel_spmd

F32 = mybir.dt.float32
BF16 = mybir.dt.bfloat16
AF = mybir.ActivationFunctionType
ALU = mybir.AluOpType

D = 1024
DFF = 4096
NCORES = 8
UNITS = 4
UT = 4096
HALO = 256
UTE = UT + 2 * HALO
ALPHA = 4.0 ** 0.25
LN_EPS = 1e-5
RMS_EPS = 1e-6


class Buf:
    __slots__ = ("name", "w", "r")

    def __init__(self, name):
        self.name = name
        self.w = None
        self.r = {}


class Sched:
    def __init__(self, nc, es, same_eng_sync=True):
        self.nc = nc
        self.es = es
        self.eng = {"pe": nc.tensor, "act": nc.scalar, "dve": nc.vector,
                    "pool": nc.gpsimd, "sp": nc.sync}
        self.same = same_eng_sync
        self.semobj = {}
        self.cnt = {}
        self.seen = {e: {} for e in self.eng}
        self.nops = 0

    def sem(self, name):
        if name not in self.semobj:
            self.semobj[name] = self.es.enter_context(self.nc.semaphore("s_" + name))
            self.cnt[name] = 0
        return self.semobj[name]

    def buf(self, name="b"):
        return Buf(name)

    def _waits(self, e, reads, writes, is_dma, extra=()):
        need = {}

        def add(tok, raw):
            s, v, te = tok
            if te == e and not is_dma:
                if e == "pe" or not self.same:
                    return
            if need.get(s, 0) < v:
                need[s] = v
        for b in reads:
            if b.w is not None:
                add(b.w, True)
        for b in writes:
            if b.w is not None:
                add(b.w, False)
            for r in b.r.values():
                add(r, False)
        for tok in extra:
            add(tok, True)
        for s, v in need.items():
            if self.seen[e].get(s, 0) < v:
                self.eng[e].wait_ge(self.semobj[s], v)
                self.seen[e][s] = v

    def _update(self, rk, tok, reads, writes):
        for b in reads:
            b.r[rk] = tok
        for b in writes:
            b.w = tok
            b.r = {}

    def op(self, eng, fn, reads=(), writes=()):
        self._waits(eng, reads, writes, False)
        inst = fn()
        so = self.sem(eng)
        self.cnt[eng] += 1
        inst.then_inc(so, 1)
        tok = (eng, self.cnt[eng], eng)
        self._update(eng, tok, reads, writes)
        self.nops += 1
        return tok

    def dma(self, q, fns, key, reads=(), writes=()):
        if not isinstance(fns, (list, tuple)):
            fns = [fns]
        s = "dma_" + key
        so = self.sem(s)
        prev = ((s, self.cnt[s], None),) if self.cnt[s] > 0 else ()
        self._waits(q, reads, writes, True, extra=prev)
        for f in fns:
            f().then_inc(so, 16)
        self.cnt[s] += 16 * len(fns)
        tok = (s, self.cnt[s], None)
        self._update(s, tok, reads, writes)
        self.nops += 1
        return tok

    def cc(self, fn, key, reads=(), writes=()):
        s = "cc_" + key
        so = self.sem(s)
        prev = ((s, self.cnt[s], None),) if self.cnt[s] > 0 else ()
        self._waits("pool", reads, writes, True, extra=prev)
        fn().then_inc(so, 1)
        self.cnt[s] += 1
        tok = (s, self.cnt[s], None)
        self._update(s, tok, reads, writes)
        return tok

    def barrier(self):
        for e in self.eng:
            for s, v in self.cnt.items():
                if v > 0 and s != e and self.seen[e].get(s, 0) < v:
                    self.eng[e].wait_ge(self.semobj[s], v)
                    self.seen[e][s] = v

    def emit(self, es=None):
        self.nsem = len(self.semobj)


class Ctx:
    pass


_UNIQ = [0]


def sb(es, nc, name, shape, dt):
    _UNIQ[0] += 1
    return es.enter_context(nc.sbuf_tensor(f"sb_{name}_{_UNIQ[0]}", list(shape), dt))


def ps(es, nc, name, shape, dt=F32):
    _UNIQ[0] += 1
    return es.enter_context(nc.psum_tensor(f"ps_{name}_{_UNIQ[0]}", list(shape), dt))


def load_weight_bf16(C, es_outer, wdram, kparts, ncols, name, piece_cols=2048):
    nc, S = C.nc, C.S
    wt = sb(es_outer, nc, name, [128, kparts, ncols], BF16)
    wb = S.buf(name)
    piece_cols = min(piece_cols, ncols)
    npc = ncols // piece_cols
    with ExitStack() as es:
        NSTG = 4
        stg = [sb(es, nc, f"{name}_stg{i}", [128, piece_cols], F32) for i in range(NSTG)]
        stb = [S.buf(f"{name}_stg{i}") for i in range(NSTG)]
        engs = ["dve", "act", "pool"]
        n = 0
        for k in range(kparts):
            for p in range(npc):
                j = n % NSTG
                src = wdram[k * 128:(k + 1) * 128, p * piece_cols:(p + 1) * piece_cols]
                S.dma("sp", (lambda d=stg[j], s=src: nc.sync.dma_start(out=d[:], in_=s)),
                      f"{name}_stg{j}", reads=(), writes=(stb[j],))
                e = engs[n % 3]
                dst = wt[:, k, p * piece_cols:(p + 1) * piece_cols]
                if e == "dve":
                    S.op("dve", (lambda d=dst, s=stg[j]: nc.vector.tensor_copy(d, s[:])),
                         reads=(stb[j],), writes=())
                elif e == "act":
                    S.op("act", (lambda d=dst, s=stg[j]: nc.scalar.copy(d, s[:])),
                         reads=(stb[j],), writes=())
                else:
                    S.op("pool", (lambda d=dst, s=stg[j]: nc.gpsimd.tensor_copy(d, s[:])),
                         reads=(stb[j],), writes=())
                n += 1
        S.barrier()
    return wt, wb


def layernorm_store(C, T, y, yb, out_tile, outb):
    nc, S = C.nc, C.S
    st, stb, mv, mvb, rs, rsb = T["st"], T["stb"], T["mv"], T["mvb"], T["rs"], T["rsb"]
    for h in range(2):
        S.op("dve", (lambda h=h: nc.vector.bn_stats(st[:, h, :], y[:, h * 512:(h + 1) * 512])),
             reads=(yb,), writes=(stb,))
    S.op("dve", (lambda: nc.vector.bn_aggr(mv[:], st[:].rearrange("p a b -> p (a b)"))),
         reads=(stb,), writes=(mvb,))
    S.op("act", (lambda: nc.scalar.activation(out=rs[:, 0:1], in_=mv[:, 1:2], func=AF.Sqrt,
                                              bias=C.eps_ln[:, 0:1], scale=1.0)),
         reads=(mvb,), writes=(rsb,))
    S.op("dve", (lambda: nc.vector.reciprocal(rs[:, 1:2], rs[:, 0:1])),
         reads=(rsb,), writes=(rsb,))
    S.op("dve", (lambda: nc.vector.tensor_scalar(y[:], y[:], mv[:, 0:1], rs[:, 1:2],
                                                 op0=ALU.subtract, op1=ALU.mult)),
         reads=(yb, mvb, rsb), writes=(yb,))
    S.op("pool", (lambda: nc.gpsimd.tensor_tensor(out=y[:], in0=y[:], in1=T["gb"][:, 0, :], op=ALU.mult)),
         reads=(yb,), writes=(yb,))
    S.op("pool", (lambda: nc.gpsimd.tensor_tensor(out=out_tile, in0=y[:], in1=T["gb"][:, 1, :], op=ALU.add)),
         reads=(yb,), writes=(outb,))


def load_gb(C, es, gi, name):
    nc, S = C.nc, C.S
    t = sb(es, nc, name, [128, 2, D], F32)
    S.dma("sp", (lambda: nc.sync.dma_start(out=t[:], in_=C.d_vecs[:, gi:gi + 2, :])), name, writes=(S.buf(),))
    S.barrier()
    return t


def stage_ffn(C, layer, src, dst):
    nc, S = C.nc, C.S
    TT = 256
    ntiles = (UNITS * UT) // TT
    ntiles = min(ntiles, C.max_tiles) if C.max_tiles else ntiles
    with ExitStack() as es:
        w1, _ = load_weight_bf16(C, es, C.w_ff1[layer], 8, DFF, "w1")
        w2, _ = load_weight_bf16(C, es, C.w_ff2[layer], 32, D, "w2", piece_cols=1024)
        xin = [sb(es, nc, f"f_x{i}", [128, 2, D], F32) for i in range(2)]
        xinb = [S.buf() for _ in range(2)]
        xT = sb(es, nc, "f_xT", [128, 8, TT], BF16)
        xTb = S.buf()
        h1 = sb(es, nc, "f_h1", [128, 32, TT], BF16)
        h1b = [S.buf() for _ in range(32)]
        rl = [sb(es, nc, f"f_rl{i}", [128, TT], F32) for i in range(2)]
        rlb = [S.buf() for _ in range(2)]
        yt = [sb(es, nc, f"f_y{i}", [128, D], F32) for i in range(2)]
        ytb = [S.buf() for _ in range(2)]
        T = dict(st=sb(es, nc, "f_st", [128, 2, 6], F32), stb=S.buf(),
                 mv=sb(es, nc, "f_mv", [128, 2], F32), mvb=S.buf(),
                 rs=sb(es, nc, "f_rs", [128, 2], F32), rsb=S.buf(),
                 gb=load_gb(C, es, 4 * layer + 2, "f_gb"))
        pT = ps(es, nc, "f_pT", [128, 1024])
        pTb = S.buf()
        pH = [ps(es, nc, f"f_pH{i}", [128, 512]) for i in range(2)]
        pHb = [S.buf() for _ in range(2)]
        pY = [ps(es, nc, f"f_pY{i}", [128, 1024]) for i in range(2)]
        pYb = [S.buf() for _ in range(2)]

        def load(t):
            j = t % 2
            s = src[t * TT:(t + 1) * TT, :].rearrange("(s p) d -> p s d", p=128)
            S.dma("sp", (lambda: nc.sync.dma_start(out=xin[j][:], in_=s)), f"f_x{j}",
                  writes=(xinb[j],))

        stop = getattr(C, "stop", 99)
        if stop <= 0:
            S.barrier(); return
        load(0)
        for t in range(ntiles):
            j = t % 2
            if t + 1 < ntiles:
                load(t + 1)
            for s in range(2):
                for k in range(8):
                    S.op("pe", (lambda s=s, k=k: nc.tensor.transpose(
                        pT[:, k * 128:(k + 1) * 128], xin[j][:, s, k * 128:(k + 1) * 128], C.ident_f[:])),
                        reads=(xinb[j],), writes=(pTb,))
                S.op("act" if s == 0 else "dve",
                     (lambda s=s: (nc.scalar.copy if s == 0 else nc.vector.tensor_copy)(
                         xT[:, :, s * 128:(s + 1) * 128],
                         pT[:].rearrange("p (k t) -> p k t", k=8))),
                     reads=(pTb,), writes=(xTb,))
            if stop <= 1:
                continue
            for c in range(32):
                q = c % 2
                pv = pH[q][:, 0:256]
                for k in range(8):
                    S.op("pe", (lambda c=c, k=k, pv=pv: nc.tensor.matmul(
                        pv, w1[:, k, c * 128:(c + 1) * 128], xT[:, k, :], start=(k == 0), stop=(k == 7))),
                        reads=(xTb,), writes=(pHb[q],))
                S.op("act", (lambda pv=pv, q=q: nc.scalar.activation(out=rl[q][:], in_=pv, func=AF.Relu)),
                     reads=(pHb[q],), writes=(rlb[q],))
                S.op("pool", (lambda c=c, q=q: nc.gpsimd.tensor_tensor(
                    out=h1[:, c, :], in0=rl[q][:], in1=rl[q][:], op=ALU.mult)),
                    reads=(rlb[q],), writes=(h1b[c],))
            if stop <= 2:
                continue
            for s in range(2):
                for hf in range(2):
                    for c in range(32):
                        S.op("pe", (lambda s=s, hf=hf, c=c: nc.tensor.matmul(
                            pY[s][:, hf * 512:(hf + 1) * 512], h1[:, c, s * 128:(s + 1) * 128],
                            w2[:, c, hf * 512:(hf + 1) * 512], start=(c == 0), stop=(c == 31))),
                            reads=(h1b[c],), writes=(pYb[s],))
                if stop <= 3:
                    continue
                S.op("dve", (lambda s=s: nc.vector.scalar_tensor_tensor(
                    out=yt[s][:], in0=xin[j][:, s, :], scalar=ALPHA, in1=pY[s][:],
                    op0=ALU.mult, op1=ALU.add)),
                    reads=(xinb[j], pYb[s]), writes=(ytb[s],))
                if stop <= 4:
                    continue
                layernorm_store(C, T, yt[s], ytb[s], yt[s][:], ytb[s])
                if stop <= 5:
                    continue
                d = dst[t * TT + s * 128: t * TT + (s + 1) * 128, :]
                S.dma("sp", (lambda s=s, d=d: nc.sync.dma_start(out=d, in_=yt[s][:])), f"f_o{s}",
                      reads=(ytb[s],))
        S.barrier()


def setup_consts(C, es):
    nc, S = C.nc, C.S
    C.ident_f = sb(es, nc, "ident_f", [128, 128], F32)
    C.ident_b = sb(es, nc, "ident_b", [128, 128], BF16)
    C.eps_ln = sb(es, nc, "eps_ln", [128, 1], F32)
    b = S.buf()
    S.dma("sp", [lambda: nc.sync.dma_start(out=C.ident_f[:], in_=C.d_ident[:, :]),
                 ], "consts", writes=(b,))
    S.op("dve", lambda: nc.vector.tensor_copy(C.ident_b[:], C.ident_f[:]), reads=(b,), writes=(b,))
    S.op("dve", lambda: nc.vector.memset(C.eps_ln[:], LN_EPS), writes=(b,))
    S.barrier()


def setup_persistent(C, es):
    nc, S = C.nc, C.S
    setup_consts(C, es)
    C.memk = sb(es, nc, "memk", [128, UNITS, 2, 256], BF16)
    C.memv = sb(es, nc, "memv", [128, UNITS, 2, 4, 65], BF16)
    C.ones_b = sb(es, nc, "ones_b", [128, 1], BF16)
    C.memb = S.buf("mem")


def stage_memkv(C):
    nc, S = C.nc, C.S
    with ExitStack() as es:
        w, _ = load_weight_bf16(C, es, C.w_mem_kv, 8, 512, "wkv", piece_cols=512)
        mx = sb(es, nc, "m_x", [128, 2, D], F32)
        mxb = S.buf()
        mT = sb(es, nc, "m_T", [128, 8, 256], BF16)
        mTb = S.buf()
        pT = ps(es, nc, "m_pT", [128, 1024])
        pTb = S.buf()
        pA = ps(es, nc, "m_pA", [128, 512])
        pAb = S.buf()
        S.op("dve", lambda: nc.vector.memset(C.memv[:], 1.0), writes=(C.memb,))
        S.op("dve", lambda: nc.vector.memset(C.ones_b[:], 1.0), writes=(C.memb,))
        for u in range(UNITS):
            S.dma("sp", lambda: nc.sync.dma_start(out=mx[:], in_=C.d_mem[u].rearrange("(s p) d -> p s d", p=128)),
                  "m_x", writes=(mxb,))
            for s in range(2):
                def tr():
                    for k in range(8):
                        i = nc.tensor.transpose(pT[:, k * 128:(k + 1) * 128], mx[:, s, k * 128:(k + 1) * 128], C.ident_f[:])
                    return i
                S.op("pe", tr, reads=(mxb,), writes=(pTb,))
                S.op("act", lambda: nc.scalar.copy(mT[:, :, s * 128:(s + 1) * 128], pT[:].rearrange("p (k t) -> p k t", k=8)),
                     reads=(pTb,), writes=(mTb,))
            for c in range(2):
                def mm():
                    for k in range(8):
                        i = nc.tensor.matmul(pA[:, 0:256], w[:, k, c * 128:(c + 1) * 128], mT[:, k, :],
                                             start=(k == 0), stop=(k == 7))
                    return i
                S.op("pe", mm, reads=(mTb,), writes=(pAb,))
                S.op("act", lambda: nc.scalar.copy(C.memk[:, u, c, :], pA[:, 0:256]), reads=(pAb,), writes=(C.memb,))
            for mc in range(2):
                def mm():
                    for k in range(8):
                        i = nc.tensor.matmul(pA[:, 0:256], mT[:, k, mc * 128:(mc + 1) * 128], w[:, k, 256:512],
                                             start=(k == 0), stop=(k == 7))
                    return i
                S.op("pe", mm, reads=(mTb,), writes=(pAb,))
                S.op("dve", lambda: nc.vector.tensor_copy(C.memv[:, u, mc, :, 0:64],
                                                          pA[:, 0:256].rearrange("p (h d) -> p h d", h=4)),
                     reads=(pAb,), writes=(C.memb,))
        S.barrier()


def mem_attn_tile(C, T, u, qm, qmb, tok0, ntok=512):
    nc, S = C.nc, C.S
    PT, PTb, pS, pSb, pO, pOb = T["PT"], T["PTb"], T["pS"], T["pSb"], T["pO"], T["pOb"]
    rd, rdb, cr, crb = T["rd"], T["rdb"], T["cr"], T["crb"]
    n = 0
    for h in range(4):
        p0 = (h % 2) * 64
        for mc in range(2):
            q = n % 2
            S.op("pe", lambda: nc.tensor.matmul(pS[q][:, 0:ntok], C.memk[p0:p0 + 64, u, h // 2, mc * 128:(mc + 1) * 128],
                                                qm[p0:p0 + 64, h // 2, 0:ntok], start=True, stop=True),
                 reads=(qmb,), writes=(pSb[q],))
            S.op("act", lambda: nc.scalar.activation(out=PT[:, h, mc, 0:ntok], in_=pS[q][:, 0:ntok], func=AF.Exp),
                 reads=(pSb[q],), writes=(PTb[h],))
            n += 1
    for s in range(ntok // 128):
        def pv():
            for h in range(4):
                for mc in range(2):
                    i = nc.tensor.matmul(pO[:, h * 65:(h + 1) * 65], PT[:, h, mc, s * 128:(s + 1) * 128],
                                         C.memv[:, u, mc, h, :], start=(mc == 0), stop=(mc == 1))
            return i
        S.op("pe", pv, reads=tuple(PTb), writes=(pOb,))
        pov = pO[:, 0:260].rearrange("p (h d) -> p h d", h=4)
        S.op("dve", lambda: nc.vector.reciprocal(rd[:], pov[:, :, 64]), reads=(pOb,), writes=(rdb,))
        j = s % 2
        S.op("dve", lambda: nc.vector.tensor_tensor(out=cr[j][:], in0=pov[:, :, 0:64],
                                                    in1=rd[:].unsqueeze(2).to_broadcast([128, 4, 64]), op=ALU.mult),
             reads=(pOb, rdb), writes=(crb[j],))
        d = C.mix_s[u, tok0 + s * 128: tok0 + (s + 1) * 128, 768:1024]
        S.dma("sp", lambda: nc.sync.dma_start(out=d, in_=cr[j][:].rearrange("p h d -> p (h d)")), f"cr{j}",
              reads=(crb[j],))


def mem_attn_tiles(C, es, pS, pSb):
    nc, S = C.nc, C.S
    return dict(PT=sb(es, nc, "ma_PT", [128, 4, 2, 512], BF16), PTb=[S.buf() for _ in range(4)],
                pS=pS, pSb=pSb,
                pO=ps(es, nc, "ma_pO", [128, 512]), pOb=S.buf(),
                rd=sb(es, nc, "ma_rd", [128, 4], F32), rdb=S.buf(),
                cr=[sb(es, nc, f"ma_cr{i}", [128, 4, 64], BF16) for i in range(2)], crb=[S.buf() for _ in range(2)])


def stage_inproj_a(C):
    nc, S = C.nc, C.S
    units = range(UNITS) if not getattr(C, "units", None) else C.units
    with ExitStack() as es:
        w, _ = load_weight_bf16(C, es, C.w_in_a, 8, 2560, "wa", piece_cols=1280)
        xin = [sb(es, nc, f"a_x{i}", [128, 4, D], F32) for i in range(2)]
        xinb = [S.buf() for _ in range(2)]
        xT = sb(es, nc, "a_xT", [128, 8, 512], BF16)
        xTb = S.buf()
        qf = sb(es, nc, "a_qf", [128, 6, 512], BF16); qfb = S.buf()
        kf = sb(es, nc, "a_kf", [128, 6, 512], BF16); kfb = S.buf()
        qm = sb(es, nc, "a_qm", [128, 2, 512], BF16); qmb = S.buf()
        va = sb(es, nc, "a_va", [128, 4, 12, 65], BF16); vab = S.buf()
        pT = ps(es, nc, "a_pT", [128, 1024]); pTb = S.buf()
        pA = [ps(es, nc, f"a_pA{i}", [128, 512]) for i in range(2)]; pAb = [S.buf() for _ in range(2)]
        pV = ps(es, nc, "a_pV", [128, 1024]); pVb = S.buf()
        MA = mem_attn_tiles(C, es, pA, pAb)
        S.op("dve", lambda: nc.vector.memset(va[:], 1.0), writes=(vab,))
        tiles = [(u, t) for u in units for t in range(9)]

        def load(i):
            u, t = tiles[i]
            j = i % 2
            if t < 8:
                srcs = [(xin[j][:], C.xe[u, HALO + t * 512: HALO + (t + 1) * 512, :].rearrange("(s p) d -> p s d", p=128))]
            else:
                srcs = [(xin[j][:, 0:2, :], C.xe[u, 0:HALO, :].rearrange("(s p) d -> p s d", p=128)),
                        (xin[j][:, 2:4, :], C.xe[u, HALO + UT:UTE, :].rearrange("(s p) d -> p s d", p=128))]
            S.dma("sp", [(lambda o=o, i_=i_: nc.sync.dma_start(out=o, in_=i_)) for o, i_ in srcs], f"a_x{j}",
                  writes=(xinb[j],))

        load(0)
        na = 0
        for i, (u, t) in enumerate(tiles):
            j = i % 2
            if i + 1 < len(tiles):
                load(i + 1)
            real = t < 8
            for s in range(4):
                def tr():
                    for k in range(8):
                        ins = nc.tensor.transpose(pT[:, k * 128:(k + 1) * 128], xin[j][:, s, k * 128:(k + 1) * 128], C.ident_f[:])
                    return ins
                S.op("pe", tr, reads=(xinb[j],), writes=(pTb,))
                if s % 2 == 0:
                    S.op("act", lambda: nc.scalar.copy(xT[:, :, s * 128:(s + 1) * 128], pT[:].rearrange("p (k t) -> p k t", k=8)),
                         reads=(pTb,), writes=(xTb,))
                else:
                    S.op("dve", lambda: nc.vector.tensor_copy(xT[:, :, s * 128:(s + 1) * 128], pT[:].rearrange("p (k t) -> p k t", k=8)),
                         reads=(pTb,), writes=(xTb,))
            jobs = []
            if real:
                jobs += [(c * 128, qf, qfb, c, 0.125) for c in range(6)]
                jobs += [(2304 + c * 128, qm, qmb, c, 0.125) for c in range(2)]
            jobs += [(768 + c * 128, kf, kfb, c, 1.0) for c in range(6)]
            for (col, dt_, db, c, sc) in jobs:
                q = na % 2
                na += 1

                def mm():
                    for k in range(8):
                        ins = nc.tensor.matmul(pA[q][:, :], w[:, k, col:col + 128], xT[:, k, :], start=(k == 0), stop=(k == 7))
                    return ins
                S.op("pe", mm, reads=(xTb,), writes=(pAb[q],))
                S.op("act", lambda: nc.scalar.activation(out=dt_[:, c, :], in_=pA[q][:, :], func=AF.Copy, scale=sc),
                     reads=(pAb[q],), writes=(db,))
            for s in range(4):
                def mm():
                    for hf, (c0, c1) in enumerate(((0, 512), (512, 768))):
                        for k in range(8):
                            ins = nc.tensor.matmul(pV[:, c0:c1], xT[:, k, s * 128:(s + 1) * 128], w[:, k, 1536 + c0:1536 + c1],
                                                   start=(k == 0), stop=(k == 7))
                    return ins
                S.op("pe", mm, reads=(xTb,), writes=(pVb,))
                S.op("dve", lambda: nc.vector.tensor_copy(va[:, s, :, 0:64], pV[:, 0:768].rearrange("p (h d) -> p h d", h=12)),
                     reads=(pVb,), writes=(vab,))
            if real:
                k0 = HALO + t * 512
                S.dma("sp", lambda: nc.sync.dma_start(
                    out=C.qT_s[u].rearrange("(c p) t -> p c t", p=128)[:, :, t * 512:(t + 1) * 512], in_=qf[:]), "a_qf", reads=(qfb,))
                S.dma("sp", lambda: nc.sync.dma_start(
                    out=C.kT_s[u].rearrange("(c p) t -> p c t", p=128)[:, :, k0:k0 + 512], in_=kf[:]), "a_kf", reads=(kfb,))
                S.dma("sp", lambda: nc.sync.dma_start(
                    out=C.v_s[u, k0:k0 + 512, :].rearrange("(s p) f -> p s f", p=128), in_=va[:].rearrange("p s h d -> p s (h d)")),
                    "a_va", reads=(vab,))
                mem_attn_tile(C, MA, u, qm, qmb, t * 512)
            else:
                kv = C.kT_s[u].rearrange("(c p) t -> p c t", p=128)
                S.dma("sp", [lambda: nc.sync.dma_start(out=kv[:, :, 0:HALO], in_=kf[:, :, 0:256]),
                               lambda: nc.sync.dma_start(out=kv[:, :, HALO + UT:UTE], in_=kf[:, :, 256:512])], "a_kf", reads=(kfb,))
                vv = va[:].rearrange("p s h d -> p s (h d)")
                S.dma("sp", [lambda: nc.sync.dma_start(out=C.v_s[u, 0:HALO, :].rearrange("(s p) f -> p s f", p=128), in_=vv[:, 0:2, :]),
                               lambda: nc.sync.dma_start(out=C.v_s[u, HALO + UT:UTE, :].rearrange("(s p) f -> p s f", p=128), in_=vv[:, 2:4, :])],
                      "a_va", reads=(vab,))
        S.barrier()


MASKV = -30000.0
SPECIAL_PAIRS = (0, 1, 30, 31)


def pair_chunks(p):
    if p == 0:
        return list(range(0, 6))
    if p == 31:
        return list(range(30, 36))
    return list(range(p, p + 5))


def na_table(rpb0, row_offset, rows_total, p):
    tbl = np.full((2, 64, 12, 6, 2, 64), MASKV, np.float32)
    qc = np.arange(64)
    cstart = np.clip(qc - 8, 0, 48)
    kc = np.arange(64)
    colok = (kc[:, None] >= cstart[None, :]) & (kc[:, None] < cstart[None, :] + 16)
    dcol = np.clip(kc[:, None] - qc[None, :] + 15, 0, 30)
    for j, m in enumerate(pair_chunks(p)):
        for e in range(2):
            KR = row_offset + 2 * m + e - 4
            if KR < 0 or KR >= rows_total:
                continue
            for a in range(2):
                R = row_offset + 2 * p + a
                rs = min(max(R - 4, 0), rows_total - 8)
                if not (rs <= KR < rs + 8):
                    continue
                drow = KR - R + 7
                blk = rpb0[:, drow, :][:, dcol]
                blk = np.where(colok[None], blk, np.float32(MASKV))
                tbl[e, :, :, j, a, :] = np.transpose(blk, (1, 0, 2))
    return tbl.reshape(128, 12, 6, 128)


def stage_na(C):
    nc, S = C.nc, C.S
    units = range(UNITS) if not getattr(C, "units", None) else C.units
    pairs = getattr(C, "pairs", None)
    with ExitStack() as es:
        wout, _ = load_weight_bf16(C, es, C.w_out[0], 8, D, "wo", piece_cols=1024)
        T = dict(st=sb(es, nc, "n_st", [128, 2, 6], F32), stb=S.buf(),
                 mv=sb(es, nc, "n_mv", [128, 2], F32), mvb=S.buf(),
                 rs=sb(es, nc, "n_rs", [128, 2], F32), rsb=S.buf(),
                 gb=load_gb(C, es, 0, "n_gb"))
        tbI = sb(es, nc, "n_tbI", [128, 12, 6, 128], BF16); tbIb = S.buf()
        tbS = sb(es, nc, "n_tbS", [128, 12, 6, 128], BF16); tbSb = S.buf()
        stg = [sb(es, nc, f"n_stg{i}", [128, 768], F32) for i in range(2)]; stgb = [S.buf() for _ in range(2)]
        kh = sb(es, nc, "n_kh", [128, 6, 2560], BF16); khb = S.buf()
        vh = sb(es, nc, "n_vh", [128, 20, 780], BF16); vhb = S.buf()
        qh = [sb(es, nc, f"n_qh{i}", [128, 6, 512], BF16) for i in range(2)]; qhb = [S.buf() for _ in range(2)]
        PT = sb(es, nc, "n_PT", [128, 12, 768], BF16); PTb = [S.buf() for _ in range(12)]
        x0 = [sb(es, nc, f"n_x0{i}", [128, D], F32) for i in range(2)]; x0b = [S.buf() for _ in range(2)]
        mix = [sb(es, nc, f"n_mix{i}", [128, D], BF16) for i in range(2)]; mixb = [S.buf() for _ in range(2)]
        mixT = sb(es, nc, "n_mixT", [128, 8, 128], BF16); mixTb = S.buf()
        yt = [sb(es, nc, f"n_y{i}", [128, D], F32) for i in range(2)]; ytb = [S.buf() for _ in range(2)]
        rd = sb(es, nc, "n_rd", [128, 12], F32); rdb = S.buf()
        pS = [ps(es, nc, f"n_pS{i}", [128, 1024]) for i in range(2)]; pSb = [S.buf() for _ in range(2)]
        pO = ps(es, nc, "n_pO", [128, 1024]); pOb = S.buf()
        pX = ps(es, nc, "n_pX", [128, 512]); pXb = S.buf()
        pZ = ps(es, nc, "n_pZ", [128, 512]); pZb = S.buf()
        pXh = pX[:].bitcast(BF16)

        ncast = [0]

        def load_table(dst, dstb, idx):
            for h in range(12):
                j = ncast[0] % 2
                ncast[0] += 1
                S.dma("sp", lambda: nc.sync.dma_start(out=stg[j][:], in_=C.natab[idx, :, h * 768:(h + 1) * 768]),
                      f"n_stg{j}", writes=(stgb[j],))
                S.op("pool", lambda: nc.gpsimd.tensor_copy(dst[:, h, :, :], stg[j][:].rearrange("p (j q) -> p j q", j=6)),
                     reads=(stgb[j],), writes=(dstb,))

        load_table(tbI, tbIb, 0)
        pend = [None]
        npair = [0]

        def outproj(u, p, jj):
            def f():
                def tr():
                    for k in range(8):
                        ins = nc.tensor.transpose(pXh[:, k * 128:(k + 1) * 128], mix[jj][:, k * 128:(k + 1) * 128], C.ident_b[:])
                    return ins
                S.op("pe", tr, reads=(mixb[jj],), writes=(pXb,))
                S.op("act", lambda: nc.scalar.copy(mixT[:], pXh.rearrange("p (k t) -> p k t", k=8)), reads=(pXb,), writes=(mixTb,))
                for hf, (pz, pzb) in enumerate(((pZ, pZb), (pX, pXb))):
                    def mm():
                        for k in range(8):
                            ins = nc.tensor.matmul(pz[:, :], mixT[:, k, :], wout[:, k, hf * 512:(hf + 1) * 512],
                                                   start=(k == 0), stop=(k == 7))
                        return ins
                    S.op("pe", mm, reads=(mixTb,), writes=(pzb,))
                    S.op("dve", lambda: nc.vector.scalar_tensor_tensor(
                        out=yt[jj][:, hf * 512:(hf + 1) * 512], in0=x0[jj][:, hf * 512:(hf + 1) * 512], scalar=ALPHA,
                        in1=pz[:, :], op0=ALU.mult, op1=ALU.add), reads=(x0b[jj], pzb), writes=(ytb[jj],))
                layernorm_store(C, T, yt[jj], ytb[jj], yt[jj][:], ytb[jj])
                S.dma("sp", lambda: nc.sync.dma_start(out=C.x1_s[u * UT + p * 128: u * UT + (p + 1) * 128, :], in_=yt[jj][:]),
                      f"n_y{jj}", reads=(ytb[jj],))
            return f

        for u in units:
            for half in range(2):
                plist = [p for p in range(half * 16, half * 16 + 16) if pairs is None or p in pairs]
                if not plist:
                    continue
                tok0 = half * 2048
                kv = C.kT_s[u].rearrange("(c p) t -> p c t", p=128)
                S.dma("sp", [(lambda c=c: nc.sync.dma_start(out=kh[:, c, :], in_=kv[:, c, tok0:tok0 + 2560])) for c in range(6)],
                      "n_kh", writes=(khb,))
                vv = C.v_s[u, tok0:tok0 + 2560, :].rearrange("(s p) f -> p s f", p=128)
                S.dma("sp", [(lambda g=g: nc.sync.dma_start(out=vh[:, g * 5:(g + 1) * 5, :], in_=vv[:, g * 5:(g + 1) * 5, :])) for g in range(4)],
                      "n_vh", writes=(vhb,))
                qcur = None
                for p in plist:
                    jj = npair[0] % 2
                    npair[0] += 1
                    pp = p - half * 16
                    if qcur is None or p // 4 != qcur[0]:
                        qi = (p // 4) % 2
                        qv = C.qT_s[u].rearrange("(c p) t -> p c t", p=128)
                        S.dma("sp", lambda: nc.sync.dma_start(out=qh[qi][:], in_=qv[:, :, (p // 4) * 512:(p // 4) * 512 + 512]),
                              f"n_qh{qi}", writes=(qhb[qi],))
                        qcur = (p // 4, qi)
                    qi = qcur[1]
                    if p in SPECIAL_PAIRS:
                        idx = 1 + 4 * u + SPECIAL_PAIRS.index(p)
                        load_table(tbS, tbSb, idx)
                        tb, tbb = tbS, tbSb
                    else:
                        tb, tbb = tbI, tbIb
                    S.dma("sp", [lambda: nc.sync.dma_start(out=x0[jj][:], in_=C.xe[u, HALO + p * 128:HALO + (p + 1) * 128, :]),
                                 lambda: nc.sync.dma_start(out=mix[jj][:, 768:1024], in_=C.mix_s[u, p * 128:(p + 1) * 128, 768:1024])],
                          f"n_in{jj}", writes=(x0b[jj], mixb[jj]))
                    chunks = pair_chunks(p)
                    nj = len(chunks)
                    qoff = (p % 4) * 128

                    def qk(h):
                        hp = (h % 2) * 64
                        hc = h // 2
                        psS = pS[h % 2]

                        def mm():
                            for j, m in enumerate(chunks):
                                lm = m - half * 16
                                nc.tensor.matmul(psS[:, j * 128:(j + 1) * 128], kh[hp:hp + 64, hc, lm * 128:(lm + 1) * 128],
                                                 qh[qi][hp:hp + 64, hc, qoff:qoff + 128], start=True, stop=False)
                                ins = nc.tensor.matmul(psS[:, j * 128:(j + 1) * 128], C.ident_b[:], tb[:, h, j, :],
                                                       start=False, stop=True)
                            return ins
                        S.op("pe", mm, reads=(khb, qhb[qi], tbb), writes=(pSb[h % 2],))
                        S.op("act", lambda: nc.scalar.activation(out=PT[:, h, 0:nj * 128], in_=psS[:, 0:nj * 128], func=AF.Exp),
                             reads=(pSb[h % 2],), writes=(PTb[h],))

                    def pv(h):
                        def mm():
                            for j, m in enumerate(chunks):
                                lm = m - half * 16
                                oc = (h // 6) * 512 + (h % 6) * 65
                                ins = nc.tensor.matmul(pO[:, oc:oc + 65], PT[:, h, j * 128:(j + 1) * 128],
                                                       vh[:, lm, h * 65:(h + 1) * 65], start=(j == 0), stop=(j == nj - 1))
                            return ins
                        S.op("pe", mm, reads=(PTb[h], vhb), writes=(pOb,))

                    qk(0)
                    qk(1)
                    for h in range(12):
                        if h + 2 < 12:
                            qk(h + 2)
                        pv(h)
                        if h == 4 and pend[0] is not None:
                            pend[0]()
                            pend[0] = None
                    pov = pO[:].rearrange("p (g x) -> p g x", g=2)[:, :, 0:390].rearrange("p g (h d) -> p g h d", h=6)
                    rdv = rd[:].rearrange("p (g h) -> p g h", g=2)
                    S.op("dve", lambda: nc.vector.reciprocal(rdv, pov[:, :, :, 64]), reads=(pOb,), writes=(rdb,))
                    S.op("dve", lambda: nc.vector.tensor_tensor(
                        out=mix[jj][:, 0:768].rearrange("p (g h d) -> p g h d", g=2, h=6), in0=pov[:, :, :, 0:64],
                        in1=rdv.unsqueeze(3).to_broadcast([128, 2, 6, 64]), op=ALU.mult),
                        reads=(pOb, rdb), writes=(mixb[jj],))
                    pend[0] = outproj(u, p, jj)
        if pend[0] is not None:
            pend[0]()
        S.barrier()


HC = 64
NCH = UT // HC


def stage_inproj_b(C):
    nc, S = C.nc, C.S
    units = range(UNITS) if not getattr(C, "units", None) else C.units
    TT = 256
    NS = TT // 128
    NC4 = TT // HC
    src = C.x2_s
    with ExitStack() as es:
        w, _ = load_weight_bf16(C, es, C.w_in_b, 8, 4096, "wb")
        lg = sb(es, nc, "b_lg", [128, 2, 6], F32)
        lb = sb(es, nc, "b_lb", [128, 6], F32)
        oml = sb(es, nc, "b_oml", [128, 6], F32)
        noml = sb(es, nc, "b_noml", [128, 6], F32)
        nwb = sb(es, nc, "b_nwb", [128, 768], F32)
        mask = sb(es, nc, "b_mask", [128, TT], F32)
        cb = S.buf()
        S.dma("sp", [lambda: nc.sync.dma_start(out=lg[:], in_=C.lb_logits[:, :, :]),
                     lambda: nc.sync.dma_start(out=nwb[:], in_=C.d_nwb[:, :])], "b_c", writes=(cb,))
        S.op("dve", lambda: nc.vector.tensor_tensor(out=lb[:], in0=lg[:, 1, :], in1=lg[:, 0, :], op=ALU.subtract), reads=(cb,), writes=(cb,))
        S.op("act", lambda: nc.scalar.activation(out=lb[:], in_=lb[:], func=AF.Sigmoid), reads=(cb,), writes=(cb,))
        S.op("dve", lambda: nc.vector.tensor_scalar(oml[:], lb[:], -1.0, 1.0, op0=ALU.mult, op1=ALU.add), reads=(cb,), writes=(cb,))
        S.op("dve", lambda: nc.vector.tensor_scalar(noml[:], oml[:], -1.0, None, op0=ALU.mult), reads=(cb,), writes=(cb,))
        S.op("dve", lambda: nc.vector.memset(mask[:], 1.0), writes=(cb,))
        S.op("dve", lambda: nc.vector.memset(mask[:].rearrange("p (c t) -> p c t", t=HC)[:, :, 0:1], 0.0), writes=(cb,))
        S.barrier()

        xin = [sb(es, nc, f"b_x{i}", [128, NS, D], F32) for i in range(2)]; xinb = [S.buf() for _ in range(2)]
        xT = sb(es, nc, "b_xT", [128, 8, TT], BF16); xTb = S.buf()
        sq = sb(es, nc, "b_sq", [128, 6, TT], F32); sqb = [S.buf() for _ in range(6)]
        sg = sb(es, nc, "b_sg", [128, 12, TT], F32); sgb = [S.buf() for _ in range(12)]
        wk = {n: sb(es, nc, f"b_{n}", [128, TT], F32) for n in ("g", "k", "b", "t3", "t4", "e1", "e2", "e3")}
        wkb = {n: S.buf() for n in wk}
        outs = {n: [sb(es, nc, f"b_o{n}{i}", [128, TT], BF16) for i in range(2)] for n in ("QF", "KF", "KHF", "QB", "KB", "KHB")}
        outb = {n: [S.buf() for _ in range(2)] for n in outs}
        ll = sb(es, nc, "b_ll", [128, 2, 6, NC4], F32); llb = S.buf()
        qm = sb(es, nc, "b_qm", [128, 2, TT], BF16); qmb = S.buf()
        vo = sb(es, nc, "b_vo", [128, NS, 768], BF16); vob = S.buf()
        go = sb(es, nc, "b_go", [128, NS, 768], F32); gob = S.buf()
        gs = sb(es, nc, "b_gs", [128, 768], F32); gsb = S.buf()
        pT = ps(es, nc, "b_pT", [128, 1024]); pTb = S.buf()
        pA = [ps(es, nc, f"b_pA{i}", [128, 512]) for i in range(2)]; pAb = [S.buf() for _ in range(2)]
        pV = ps(es, nc, "b_pV", [128, 1024]); pVb = S.buf()
        MA = mem_attn_tiles(C, es, pA, pAb)
        ntile = UT // TT
        tiles = [(u, t) for u in units for t in range(ntile)]
        if getattr(C, "max_tiles", None):
            tiles = tiles[:C.max_tiles]
        dst = dict(QF=C.QF, KF=C.KF, KHF=C.KHF, QB=C.QB, KB=C.KB, KHB=C.KHB)

        def load(i):
            u, t = tiles[i]
            j = i % 2
            S.dma("sp", lambda: nc.sync.dma_start(
                out=xin[j][:], in_=src[u * UT + t * TT: u * UT + (t + 1) * TT, :].rearrange("(s p) d -> p s d", p=128)),
                f"b_x{j}", writes=(xinb[j],))

        load(0)
        na = 0
        no = 0
        for i, (u, t) in enumerate(tiles):
            j = i % 2
            if i + 1 < len(tiles):
                load(i + 1)
            for s in range(NS):
                def tr():
                    for k in range(8):
                        ins = nc.tensor.transpose(pT[:, k * 128:(k + 1) * 128], xin[j][:, s, k * 128:(k + 1) * 128], C.ident_f[:])
                    return ins
                S.op("pe", tr, reads=(xinb[j],), writes=(pTb,))
                S.op("dve", lambda: nc.vector.tensor_copy(xT[:, :, s * 128:(s + 1) * 128], pT[:].rearrange("p (k t) -> p k t", k=8)),
                     reads=(pTb,), writes=(xTb,))

            def fm(col, fn):
                nonlocal na
                q = na % 2
                na += 1

                def mm():
                    for k in range(8):
                        ins = nc.tensor.matmul(pA[q][:, 0:TT], w[:, k, col:col + 128], xT[:, k, :], start=(k == 0), stop=(k == 7))
                    return ins
                S.op("pe", mm, reads=(xTb,), writes=(pAb[q],))
                fn(pA[q][:, 0:TT], pAb[q])
            for c in range(6):
                fm(c * 128, lambda pv, pb: S.op("act", lambda: nc.scalar.activation(out=sq[:, c, :], in_=pv, func=AF.Silu),
                                                reads=(pb,), writes=(sqb[c],)))
            for s in range(NS):
                for which in (1, 0):
                    base = 2304 + which * 768

                    def mm():
                        for (c0, c1) in ((0, 512), (512, 768)):
                            for k in range(8):
                                ins = nc.tensor.matmul(pV[:, c0:c1], xT[:, k, s * 128:(s + 1) * 128], w[:, k, base + c0:base + c1],
                                                       start=(k == 0), stop=(k == 7))
                        return ins
                    S.op("pe", mm, reads=(xTb,), writes=(pVb,))
                    if which == 0:
                        S.op("dve", lambda: nc.vector.tensor_copy(vo[:, s, :], pV[:, 0:768]), reads=(pVb,), writes=(vob,))
                    else:
                        S.op("act", lambda: nc.scalar.activation(out=gs[:], in_=pV[:, 0:768], func=AF.Silu), reads=(pVb,), writes=(gsb,))
                        S.op("pool", lambda: nc.gpsimd.tensor_tensor(out=go[:, s, :], in0=gs[:], in1=nwb[:], op=ALU.mult),
                             reads=(gsb,), writes=(gob,))
            for c in range(6):
                for d in range(2):
                    fm(768 * (d + 1) + c * 128, lambda pv, pb: S.op(
                        "act", lambda: nc.scalar.activation(out=sg[:, d * 6 + c, :], in_=pv, func=AF.Sigmoid),
                        reads=(pb,), writes=(sgb[d * 6 + c],)))
            for c in range(2):
                fm(3840 + c * 128, lambda pv, pb: S.op(
                    "act", lambda: nc.scalar.activation(out=qm[:, c, :], in_=pv, func=AF.Copy, scale=0.125),
                    reads=(pb,), writes=(qmb,)))
            r0 = u * UT + t * TT
            S.dma("sp", lambda: nc.sync.dma_start(out=C.V1[r0:r0 + TT, :].rearrange("(s p) f -> p s f", p=128), in_=vo[:]),
                  "b_vo", reads=(vob,))
            S.dma("sp", lambda: nc.sync.dma_start(out=C.G1[r0:r0 + TT, :].rearrange("(s p) f -> p s f", p=128), in_=go[:]),
                  "b_go", reads=(gob,))
            b3 = wk["b"][:].rearrange("p (c t) -> p c t", t=HC)
            import os as _os
            for c in range(0 if not _os.environ.get("SKIP_P2") else 6, 6):
                for d in range(2):
                    sgi = sg[:, d * 6 + c, :]
                    sgib = sgb[d * 6 + c]
                    S.op("act", lambda: nc.scalar.activation(out=wk["g"][:], in_=sgi, func=AF.Ln, bias=lb[:, c:c + 1], scale=oml[:, c:c + 1]),
                         reads=(sgib,), writes=(wkb["g"],))
                    S.op("dve", lambda: nc.vector.tensor_scalar(wk["k"][:], sgi, noml[:, c:c + 1], oml[:, c:c + 1], op0=ALU.mult, op1=ALU.add),
                         reads=(sgib,), writes=(wkb["k"],))
                    S.op("dve", lambda: nc.vector.tensor_tensor_scan(wk["b"][:], mask[:], wk["g"][:], 0.0, ALU.mult, ALU.add),
                         reads=(wkb["g"],), writes=(wkb["b"],))
                    S.op("dve", lambda: nc.vector.tensor_copy(ll[:, d, c, :], b3[:, :, HC - 1]), reads=(wkb["b"],), writes=(llb,))
                    S.op("dve", lambda: nc.vector.tensor_tensor(
                        out=wk["t3"][:].rearrange("p (c t) -> p c t", t=HC), in0=b3[:, :, HC - 1:HC].to_broadcast([128, NC4, HC]),
                        in1=b3, op=ALU.subtract), reads=(wkb["b"],), writes=(wkb["t3"],))
                    if d == 0:
                        a1, a2, a3 = wk["b"], wk["b"], wk["t3"]
                        a1b, a2b, a3b = wkb["b"], wkb["b"], wkb["t3"]
                        names = ("QF", "KF", "KHF")
                    else:
                        S.op("dve", lambda: nc.vector.tensor_tensor(out=wk["t4"][:], in0=wk["b"][:], in1=wk["g"][:], op=ALU.subtract),
                             reads=(wkb["b"], wkb["g"]), writes=(wkb["t4"],))
                        S.op("dve", lambda: nc.vector.tensor_tensor(out=wk["t3"][:], in0=wk["t3"][:], in1=wk["g"][:], op=ALU.add),
                             reads=(wkb["t3"], wkb["g"]), writes=(wkb["t3"],))
                        a1, a2, a3 = wk["t3"], wk["t3"], wk["t4"]
                        a1b, a2b, a3b = wkb["t3"], wkb["t3"], wkb["t4"]
                        names = ("QB", "KB", "KHB")
                    S.op("act", lambda: nc.scalar.activation(out=wk["e1"][:], in_=a1[:], func=AF.Exp), reads=(a1b,), writes=(wkb["e1"],))
                    S.op("act", lambda: nc.scalar.activation(out=wk["e2"][:], in_=a2[:], func=AF.Exp, scale=-1.0), reads=(a2b,), writes=(wkb["e2"],))
                    S.op("act", lambda: nc.scalar.activation(out=wk["e3"][:], in_=a3[:], func=AF.Exp), reads=(a3b,), writes=(wkb["e3"],))
                    oj = no % 2
                    no += 1
                    for nm, e_, other, ob_ in ((names[0], "e1", sq[:, c, :], sqb[c]), (names[1], "e2", wk["k"][:], wkb["k"]),
                                               (names[2], "e3", wk["k"][:], wkb["k"])):
                        S.op("pool", lambda: nc.gpsimd.tensor_tensor(out=outs[nm][oj][:], in0=other, in1=wk[e_][:], op=ALU.mult),
                             reads=(ob_, wkb[e_]), writes=(outb[nm][oj],))
                        S.dma("sp", lambda: nc.sync.dma_start(out=dst[nm][u, c * 128:(c + 1) * 128, t * TT:(t + 1) * TT], in_=outs[nm][oj][:]),
                              f"b_o{nm}{oj}", reads=(outb[nm][oj],))
            for d, LD in enumerate((C.LF, C.LB)):
                S.dma("sp", lambda: nc.sync.dma_start(
                    out=LD[u].rearrange("(c p) n -> p c n", p=128)[:, :, t * NC4:(t + 1) * NC4], in_=ll[:, d, :, :]),
                    "b_ll", reads=(llb,))
            if not _os.environ.get("SKIP_MA"):
                mem_attn_tile(C, MA, u, qm, qmb, t * TT, ntok=TT)
        S.barrier()


def hg_tiles(C, es):
    nc, S = C.nc, C.S
    H = Ctx()
    H.Sf = sb(es, nc, "h_Sf", [128, 6, 128], F32); H.Sfb = S.buf()
    H.Sb = sb(es, nc, "h_Sb", [128, 6, 128], F32); H.Sbb = S.buf()
    H.Sf16 = sb(es, nc, "h_Sf16", [128, 6, 128], BF16); H.Sf16b = S.buf()
    H.Sb16 = sb(es, nc, "h_Sb16", [128, 6, 128], BF16); H.Sb16b = S.buf()
    H.Sb16x = [H.Sb16, sb(es, nc, "h_Sb16x", [128, 6, 128], BF16)]; H.Sb16xb = [H.Sb16b, S.buf()]
    H.LF = sb(es, nc, "h_LF", [128, 6, NCH], F32); H.LB = sb(es, nc, "h_LB", [128, 6, NCH], F32)
    H.DF = sb(es, nc, "h_DF", [128, 6, NCH], F32); H.DB = sb(es, nc, "h_DB", [128, 6, NCH], F32)
    H.Lb = S.buf()
    H.khT = sb(es, nc, "h_khT", [64, 768], BF16); H.khTb = S.buf()
    H.pB = ps(es, nc, "h_pB", [128, 1024], BF16); H.pBb = S.buf()
    H.pD = ps(es, nc, "h_pD", [128, 1024]); H.pDb = S.buf()
    H.sbsd = S.buf()
    return H


def hg_load_decays(C, H, u):
    nc, S = C.nc, C.S
    nv = (getattr(C, "hg_ng", None) or (UT // 256)) * (256 // HC)
    S.dma("sp", [lambda: nc.sync.dma_start(out=H.LF[:, :, 0:nv], in_=C.LF[u].rearrange("(c p) n -> p c n", p=128)[:, :, 0:nv]),
                 lambda: nc.sync.dma_start(out=H.LB[:, :, 0:nv], in_=C.LB[u].rearrange("(c p) n -> p c n", p=128)[:, :, 0:nv])], "h_L", writes=(H.Lb,))
    S.op("act", lambda: nc.scalar.activation(out=H.DF[:, :, 0:nv], in_=H.LF[:, :, 0:nv], func=AF.Exp), reads=(H.Lb,), writes=(H.Lb,))
    S.op("act", lambda: nc.scalar.activation(out=H.DB[:, :, 0:nv], in_=H.LB[:, :, 0:nv], func=AF.Exp), reads=(H.Lb,), writes=(H.Lb,))


def hg_state_step(C, H, khsrc, khsrcb, cols, vch, vb, St, Stb, S16, S16b, Dt, n):
    nc, S = C.nc, C.S

    def tr():
        for h in range(6):
            ins = nc.tensor.transpose(H.pB[0:64, h * 128:(h + 1) * 128], khsrc[:, h, cols], C.ident_b[:])
        return ins
    S.op("pe", tr, reads=(khsrcb,), writes=(H.pBb,))
    S.op("act", lambda: nc.scalar.copy(H.khT[:], H.pB[0:64, 0:768]), reads=(H.pBb,), writes=(H.khTb,))

    def ds():
        for h in range(6):
            ins = nc.tensor.matmul(H.pD[:, h * 128:(h + 1) * 128], H.khT[:, h * 128:(h + 1) * 128], vch[:, h * 128:(h + 1) * 128],
                                   start=True, stop=True)
        return ins
    S.op("pe", ds, reads=(H.khTb, vb), writes=(H.pDb,))

    def upd():
        for h in range(6):
            ins = nc.vector.scalar_tensor_tensor(out=St[:, h, :], in0=St[:, h, :], scalar=Dt[:, h, n:n + 1],
                                                 in1=H.pD[:, h * 128:(h + 1) * 128], op0=ALU.mult, op1=ALU.add)
        return ins
    S.op("dve", upd, reads=(H.pDb, H.Lb), writes=(Stb,))
    S.op("act", lambda: nc.scalar.copy(S16[:], St[:]), reads=(Stb,), writes=(S16b,))


def stage_hgrn(C):
    nc, S = C.nc, C.S
    units = range(UNITS) if not getattr(C, "units", None) else C.units
    GT = 256
    GC = GT // HC
    NG = getattr(C, "hg_ng", None) or UT // GT
    with ExitStack() as es:
        wout, _ = load_weight_bf16(C, es, C.w_out[1], 8, D, "wo1", piece_cols=1024)
        T = dict(st=sb(es, nc, "h_st", [128, 2, 6], F32), stb=S.buf(),
                 mv=sb(es, nc, "h_mv", [128, 2], F32), mvb=S.buf(),
                 rs=sb(es, nc, "h_rs", [128, 2], F32), rsb=S.buf(),
                 gb=load_gb(C, es, 4, "h_gb"))
        H = hg_tiles(C, es)
        tri = sb(es, nc, "h_tri", [64, 768], F32)
        S.dma("sp", lambda: nc.sync.dma_start(out=tri[:], in_=C.d_tri[:, :]), "h_tri", writes=(S.buf(),))
        S.barrier()
        names = ("QF", "KF", "QB", "KB", "KHF")
        G = []
        for i in range(2):
            g = Ctx()
            g.t = {n: sb(es, nc, f"h_{n}{i}", [128, 6, GT], BF16) for n in names}
            g.v = sb(es, nc, f"h_v{i}", [64, GC, 768], BF16)
            g.g1 = sb(es, nc, f"h_g1{i}", [64, GC, 768], F32)
            g.sbs = sb(es, nc, f"h_sbs{i}", [128, GC, 768], BF16)
            g.x = sb(es, nc, f"h_x{i}", [128, GT // 128, D], F32)
            g.b = S.buf()
            G.append(g)
        khb_t = [sb(es, nc, f"h_khb{i}", [128, 6, GT], BF16) for i in range(2)]; khb_b = [S.buf() for _ in range(2)]
        vb_t = [sb(es, nc, f"h_vb{i}", [64, GC, 768], BF16) for i in range(2)]; vb_b = [S.buf() for _ in range(2)]
        at = sb(es, nc, "h_at", [64, 768], BF16); atb = S.buf()
        osb = sb(es, nc, "h_osb", [64, 768], F32); osbb = S.buf()
        sqr = sb(es, nc, "h_sqr", [64, 768], F32); sqrb = S.buf()
        ss = sb(es, nc, "h_ss", [64, 3, 6], F32); ssb = S.buf()
        mixc = [sb(es, nc, f"h_mixc{i}", [64, D], BF16) for i in range(2)]; mixcb = [S.buf() for _ in range(2)]
        mixT = sb(es, nc, "h_mixT", [128, 8, 128], BF16); mixTb = S.buf()
        yt = [sb(es, nc, f"h_y{i}", [128, D], F32) for i in range(2)]; ytb = [S.buf() for _ in range(2)]
        eps_r = sb(es, nc, "h_epsr", [64, 1], F32)
        S.op("dve", lambda: nc.vector.memset(eps_r[:], RMS_EPS), writes=(S.buf(),))
        pAT = ps(es, nc, "h_pAT", [64, 1024]); pATb = S.buf()
        pOo = ps(es, nc, "h_pOo", [64, 1024]); pOob = S.buf()
        S.barrier()
        ny = 0
        cfl = sb(es, nc, "h_cfl", [128, 1], F32); cflb = S.buf()
        S.dma("sp", lambda: nc.sync.dma_start(out=cfl[:], in_=C.d_cflag[:, :]), "h_cfl", writes=(cflb,))
        ulist = list(units)
        khTx = [H.khT, sb(es, nc, "h_khT1", [64, 768], BF16)]; khTxb = [H.khTb, S.buf()]
        Sf16x = [H.Sf16, sb(es, nc, "h_Sf16x", [128, 6, 128], BF16)]; Sf16xb = [H.Sf16b, S.buf()]
        osbx = [osb, sb(es, nc, "h_osb1", [64, 768], F32)]; osbxb = [osbb, S.buf()]
        pE = ps(es, nc, "h_pE", [128, 512]); pEb = S.buf()

        def kprep(khsrc, khsrcb, cols, k):
            def tr():
                for h in range(6):
                    ins = nc.tensor.transpose(H.pB[0:64, h * 128:(h + 1) * 128], khsrc[:, h, cols], C.ident_b[:])
                return ins
            S.op("pe", tr, reads=(khsrcb,), writes=(H.pBb,))
            S.op("act", lambda: nc.scalar.copy(khTx[k][:], H.pB[0:64, 0:768]), reads=(H.pBb,), writes=(khTxb[k],))

        def update(k, vch, vb, St, Stb, S16, S16b, Dt, n):
            def ds():
                for h in range(6):
                    ins = nc.tensor.matmul(H.pD[:, h * 128:(h + 1) * 128], khTx[k][:, h * 128:(h + 1) * 128], vch[:, h * 128:(h + 1) * 128],
                                           start=True, stop=True)
                return ins
            S.op("pe", ds, reads=(khTxb[k], vb), writes=(H.pDb,))

            def upd():
                for h in range(6):
                    ins = nc.vector.scalar_tensor_tensor(out=St[:, h, :], in0=St[:, h, :], scalar=Dt[:, h, n:n + 1],
                                                         in1=H.pD[:, h * 128:(h + 1) * 128], op0=ALU.mult, op1=ALU.add)
                return ins
            S.op("dve", upd, reads=(H.pDb, H.Lb), writes=(Stb,))
            S.op("act", lambda: nc.scalar.copy(S16[:], St[:]), reads=(Stb,), writes=(S16b,))

        for u in reversed(ulist):
            hg_load_decays(C, H, u)
            if u == ulist[-1]:
                S.op("dve", lambda: nc.vector.memset(H.Sb[:], 0.0), writes=(H.Sbb,))
            else:
                S.op("dve", lambda: nc.vector.tensor_scalar(H.Sb[:], H.Sb[:], cfl[:, 0:1], None, op0=ALU.mult),
                     reads=(H.Sbb, cflb), writes=(H.Sbb,))
            cur = 0
            S.op("act", lambda: nc.scalar.copy(H.Sb16x[0][:], H.Sb[:]), reads=(H.Sbb,), writes=(H.Sb16xb[0],))

            def loadb(gi):
                j = gi % 2
                r0 = u * UT + gi * GT
                S.dma("sp", lambda: nc.sync.dma_start(out=khb_t[j][:], in_=C.KHB[u].rearrange("(c p) t -> p c t", p=128)[:, :, gi * GT:(gi + 1) * GT]),
                      f"h_khb{j}", writes=(khb_b[j],))
                S.dma("sp", lambda: nc.sync.dma_start(out=vb_t[j][:], in_=C.V1[r0:r0 + GT, :].rearrange("(c p) f -> p c f", p=64)),
                      f"h_vb{j}", writes=(vb_b[j],))
            seq = [(gi, cc) for gi in range(NG - 1, -1, -1) for cc in range(GC - 1, -1, -1)]
            loadb(seq[0][0])
            if NG > 1:
                loadb(seq[0][0] - 1)
            kprep(khb_t[seq[0][0] % 2], khb_b[seq[0][0] % 2], slice(seq[0][1] * HC, (seq[0][1] + 1) * HC), 0)
            for i, (gi, cc) in enumerate(seq):
                j = gi % 2
                n = gi * GC + cc
                S.dma("sp", lambda: nc.sync.dma_start(out=C.SBs[u, n], in_=H.Sb16x[cur][:].rearrange("p h d -> p (h d)")), f"h_sbst{cur}",
                      reads=(H.Sb16xb[cur],), writes=(H.sbsd,))
                if i + 1 < len(seq):
                    g2, c2 = seq[i + 1]
                    kprep(khb_t[g2 % 2], khb_b[g2 % 2], slice(c2 * HC, (c2 + 1) * HC), (i + 1) % 2)
                update(i % 2, vb_t[j][:, cc, :], vb_b[j], H.Sb, H.Sbb, H.Sb16x[1 - cur], H.Sb16xb[1 - cur], H.DB, n)
                cur = 1 - cur
                if cc == 0 and gi - 2 >= 0:
                    loadb(gi - 2)
        for u in ulist:
            hg_load_decays(C, H, u)
            if u == ulist[0]:
                S.op("dve", lambda: nc.vector.memset(H.Sf[:], 0.0), writes=(H.Sfb,))
            else:
                S.op("dve", lambda: nc.vector.tensor_scalar(H.Sf[:], H.Sf[:], cfl[:, 0:1], None, op0=ALU.mult),
                     reads=(H.Sfb, cflb), writes=(H.Sfb,))
            S.op("act", lambda: nc.scalar.copy(Sf16x[1][:], H.Sf[:]), reads=(H.Sfb,), writes=(Sf16xb[1],))

            def loadf(gi):
                g = G[gi % 2]
                r0 = u * UT + gi * GT
                fns = [(lambda nm=nm: nc.sync.dma_start(out=g.t[nm][:], in_=getattr(C, nm)[u].rearrange("(c p) t -> p c t", p=128)[:, :, gi * GT:(gi + 1) * GT]))
                       for nm in names]
                fns.append(lambda: nc.sync.dma_start(out=g.v[:], in_=C.V1[r0:r0 + GT, :].rearrange("(c p) f -> p c f", p=64)))
                fns.append(lambda: nc.sync.dma_start(out=g.g1[:], in_=C.G1[r0:r0 + GT, :].rearrange("(c p) f -> p c f", p=64)))
                fns.append(lambda: nc.sync.dma_start(out=g.sbs[:], in_=C.SBs[u, gi * GC:(gi + 1) * GC].rearrange("n p f -> p n f")))
                fns.append(lambda: nc.sync.dma_start(out=g.x[:], in_=C.x2_s[r0:r0 + GT, :].rearrange("(s p) d -> p s d", p=128)))
                S.dma("sp", fns, f"h_g{gi % 2}", reads=(H.sbsd,), writes=(g.b,))

            def epilogue2(g, cc, n, mj):
                nonlocal ny
                ob, obb = osbx[n % 2], osbxb[n % 2]
                S.op("pool", lambda: nc.gpsimd.tensor_tensor(out=sqr[:], in0=ob[:], in1=ob[:], op=ALU.mult), reads=(obb,), writes=(sqrb,))
                S.op("dve", lambda: nc.vector.tensor_reduce(out=ss[:, 0, :], in_=sqr[:].rearrange("p (h d) -> p h d", h=6),
                                                            axis=mybir.AxisListType.X, op=ALU.add), reads=(sqrb,), writes=(ssb,))
                S.op("act", lambda: nc.scalar.activation(out=ss[:, 1, :], in_=ss[:, 0, :], func=AF.Sqrt, bias=eps_r[:, 0:1], scale=1.0 / 128),
                     reads=(ssb,), writes=(ssb,))
                S.op("dve", lambda: nc.vector.reciprocal(ss[:, 2, :], ss[:, 1, :]), reads=(ssb,), writes=(ssb,))
                S.op("dve", lambda: nc.vector.tensor_tensor(out=sqr[:].rearrange("p (h d) -> p h d", h=6), in0=ob[:].rearrange("p (h d) -> p h d", h=6),
                                                            in1=ss[:, 2, :].unsqueeze(2).to_broadcast([64, 6, 128]), op=ALU.mult),
                     reads=(obb, ssb), writes=(sqrb,))
                S.op("pool", lambda: nc.gpsimd.tensor_tensor(out=mixc[mj][:, 0:768], in0=sqr[:], in1=g.g1[:, cc, :], op=ALU.mult),
                     reads=(sqrb, g.b), writes=(mixcb[mj],))
                hh = n % 2

                def tr():
                    for k in range(8):
                        ins = nc.tensor.transpose(H.pB[:, k * 64:(k + 1) * 64], mixc[mj][:, k * 128:(k + 1) * 128], C.ident_b[0:64, 0:64])
                    return ins
                S.op("pe", tr, reads=(mixcb[mj],), writes=(H.pBb,))
                S.op("act", lambda: nc.scalar.copy(mixT[:, :, hh * 64:(hh + 1) * 64], H.pB[:, 0:512].rearrange("p (k t) -> p k t", k=8)),
                     reads=(H.pBb,), writes=(mixTb,))
                if hh == 1:
                    yj = ny % 2
                    ny += 1
                    sx = (n // 2) % (GT // 128)
                    for hf in range(2):
                        def mm():
                            for k in range(8):
                                ins = nc.tensor.matmul(pE[:, :], mixT[:, k, :], wout[:, k, hf * 512:(hf + 1) * 512], start=(k == 0), stop=(k == 7))
                            return ins
                        S.op("pe", mm, reads=(mixTb,), writes=(pEb,))
                        S.op("dve", lambda: nc.vector.scalar_tensor_tensor(out=yt[yj][:, hf * 512:(hf + 1) * 512], in0=g.x[:, sx, hf * 512:(hf + 1) * 512],
                                                                          scalar=ALPHA, in1=pE[:, :], op0=ALU.mult, op1=ALU.add),
                             reads=(g.b, pEb), writes=(ytb[yj],))
                    layernorm_store(C, T, yt[yj], ytb[yj], yt[yj][:], ytb[yj])
                    r = u * UT + (n // 2) * 128
                    S.dma("sp", lambda: nc.sync.dma_start(out=C.x3_s[r:r + 128, :], in_=yt[yj][:]), f"h_y{yj}", reads=(ytb[yj],))

            loadf(0)
            pend = None
            for gi in range(NG):
                g = G[gi % 2]
                if pend is not None:
                    epilogue2(*pend)
                    pend = None
                if gi + 1 < NG:
                    loadf(gi + 1)
                for cc in range(GC):
                    n = gi * GC + cc
                    cols = slice(cc * HC, (cc + 1) * HC)
                    mj = n % 2
                    S.dma("sp", lambda: nc.sync.dma_start(out=mixc[mj][:, 768:1024], in_=C.mix_s[u, n * HC:(n + 1) * HC, 768:1024]),
                          f"h_mixc{mj}", writes=(mixcb[mj],))

                    def amm():
                        for d, (kn, qn) in enumerate((("KF", "QF"), ("KB", "QB"))):
                            for h in range(6):
                                o = (d * 6 + h) * 64
                                ins = nc.tensor.matmul(pAT[:, o:o + 64], g.t[kn][:, h, cols], g.t[qn][:, h, cols], start=True, stop=True)
                        return ins
                    S.op("pe", amm, reads=(g.b,), writes=(pATb,))
                    S.op("dve", lambda: nc.vector.tensor_tensor(out=at[:], in0=pAT[:, 0:768], in1=tri[:], op=ALU.mult),
                         reads=(pATb,), writes=(atb,))
                    kprep(g.t["KHF"], g.b, cols, n % 2)
                    sprev, sprevb = Sf16x[(n + 1) % 2], Sf16xb[(n + 1) % 2]
                    update(n % 2, g.v[:, cc, :], g.b, H.Sf, H.Sfb, Sf16x[n % 2], Sf16xb[n % 2], H.DF, n)

                    def omm():
                        for h in range(6):
                            o = pOo[:, h * 128:(h + 1) * 128]
                            vv = g.v[:, cc, h * 128:(h + 1) * 128]
                            nc.tensor.matmul(o, at[:, h * 64:(h + 1) * 64], vv, start=True, stop=False)
                            nc.tensor.matmul(o, at[:, (6 + h) * 64:(7 + h) * 64], vv, start=False, stop=False)
                            nc.tensor.matmul(o, g.t["QF"][:, h, cols], sprev[:, h, :], start=False, stop=False)
                            ins = nc.tensor.matmul(o, g.t["QB"][:, h, cols], g.sbs[:, cc, h * 128:(h + 1) * 128], start=False, stop=True)
                        return ins
                    S.op("pe", omm, reads=(atb, g.b, sprevb), writes=(pOob,))
                    S.op("act", lambda: nc.scalar.copy(osbx[n % 2][:], pOo[:, 0:768]), reads=(pOob,), writes=(osbxb[n % 2],))
                    if pend is not None:
                        epilogue2(*pend)
                    pend = (g, cc, n, mj)
            if pend is not None:
                epilogue2(*pend)
                pend = None
        S.barrier()


def build_program(exchange=True):
    nc = bass.Bass("TRN2", target_bir_lowering=False)
    C = Ctx()
    C.nc = nc
    C.max_tiles = None
    NT = UNITS * UT

    def din(name, shape, dt=F32):
        return nc.dram_tensor(name, list(shape), dt, kind="ExternalInput").ap()

    def dscr(name, shape, dt):
        return nc.dram_tensor(name, list(shape), dt, kind="Internal").ap()

    C.xe = din("xe", [UNITS, UTE, D])
    C.d_mem = din("mem", [UNITS, 256, D])
    C.w_mem_kv = din("w_mem_kv", [D, 512])
    C.w_in_a = din("w_in_a", [D, 2560])
    C.w_in_b = din("w_in_b", [D, 4096])
    C.w_out = din("w_out", [2, D, D])
    C.w_ff1 = din("w_ff1", [2, D, DFF])
    C.w_ff2 = din("w_ff2", [2, DFF, D])
    C.d_ident = din("ident", [128, 128])
    C.d_vecs = din("vecs", [128, 8, D])
    C.natab = din("natab", [1 + 4 * UNITS, 128, 9216])
    C.d_tri = din("tri", [64, 768])
    C.lb_logits = din("lbl", [128, 2, 6])
    C.d_nwb = din("nwb", [128, 768])
    C.y = nc.dram_tensor("y", [NT, D], F32, kind="ExternalOutput").ap()
    C.qT_s = dscr("qT_s", [UNITS, 768, UT], BF16)
    C.kT_s = dscr("kT_s", [UNITS, 768, UTE], BF16)
    C.v_s = dscr("v_s", [UNITS, UTE, 780], BF16)
    C.mix_s = dscr("mix_s", [UNITS, UT, D], BF16)
    C.x1_s = dscr("x1_s", [NT, D], F32)
    C.x2_s = dscr("x2_s", [NT, D], F32)
    C.x3_s = C.x1_s
    for n in ("QF", "KF", "KHF", "QB", "KB", "KHB"):
        setattr(C, n, dscr(n, [UNITS, 768, UT], BF16))
    C.LF = dscr("LF", [UNITS, 768, NCH], F32)
    C.LB = dscr("LB", [UNITS, 768, NCH], F32)
    C.V1 = dscr("V1", [NT, 768], BF16)
    C.G1 = dscr("G1", [NT, 768], F32)
    C.SBs = dscr("SBs", [UNITS, NCH, 128, 768], BF16)
    C.d_cflag = din("cflag", [128, 1])
    with ExitStack() as es:
        C.S = Sched(nc, es)
        setup_persistent(C, es)
        stage_memkv(C)
        stage_inproj_a(C)
        stage_na(C)
        stage_ffn(C, 0, C.x1_s, C.x2_s)
        stage_inproj_b(C)
        stage_hgrn(C)
        stage_ffn(C, 1, C.x3_s, C.y)
        C.S.barrier()
    return nc, C


def tri_mask():
    s = np.arange(64)[:, None]
    t = np.arange(64)[None, :]
    f = (s <= t).astype(np.float32)
    b = (s >= t).astype(np.float32)
    return np.concatenate([np.tile(f, (1, 6)), np.tile(b, (1, 6))], axis=1)


def kernel(x_prompt, x_sample, mem_prompt, mem_sample, w_mem_kv, w_in_a, rpb, w_in_b, lb_logits, hg_norm_w,
           w_out, ln1_g, ln1_b, w_ff1, w_ff2, ln2_g, ln2_b):
    f = lambda a: np.ascontiguousarray(np.asarray(a, dtype=np.float32))
    x_prompt, x_sample, mem_prompt, mem_sample = f(x_prompt), f(x_sample), f(mem_prompt), f(mem_sample)
    rpb0 = f(rpb)[0]
    vec = np.stack([f(ln1_g)[0], f(ln1_b)[0], f(ln2_g)[0], f(ln2_b)[0], f(ln1_g)[1], f(ln1_b)[1], f(ln2_g)[1], f(ln2_b)[1]])
    shared = dict(
        w_mem_kv=f(w_mem_kv), w_in_a=f(w_in_a)[0], w_in_b=f(w_in_b)[0], w_out=f(w_out), w_ff1=f(w_ff1), w_ff2=f(w_ff2),
        ident=np.eye(128, dtype=np.float32), vecs=np.ascontiguousarray(np.broadcast_to(vec[None], (128, 8, D))),
        tri=tri_mask(), lbl=np.ascontiguousarray(f(lb_logits).reshape(2, 6, 128).transpose(2, 0, 1)),
        nwb=np.ascontiguousarray(np.broadcast_to(np.tile(f(hg_norm_w)[0], 6)[None], (128, 768))))
    tab_int = na_table(rpb0, 64, 256, 8)
    tab_true = [na_table(rpb0, 0, 64, p) for p in SPECIAL_PAIRS]
    tab_seg = [[na_table(rpb0, 64 * j, 256, p) for p in SPECIAL_PAIRS] for j in range(4)]
    prompt_of = {}
    pi = 0
    for c in range(2, NCORES):
        for u in range(3 if c < 6 else 2):
            prompt_of[(c, u)] = pi
            pi += 1
    assert pi == x_prompt.shape[0]
    in_maps = []
    for c in range(NCORES):
        xe = np.zeros((UNITS, UTE, D), np.float32)
        mem = np.zeros((UNITS, 256, D), np.float32)
        tabs = [tab_int]
        for u in range(UNITS):
            if c < 2:
                lo, hi = u * UT - HALO, (u + 1) * UT + HALO
                clo, chi = max(lo, 0), min(hi, x_sample.shape[1])
                xe[u, clo - lo:chi - lo] = x_sample[c, clo:chi]
                mem[u] = mem_sample[c]
                tabs += tab_seg[u]
            else:
                if (c, u) in prompt_of:
                    xe[u, HALO:HALO + UT] = x_prompt[prompt_of[(c, u)]]
                    mem[u] = mem_prompt[prompt_of[(c, u)]]
                tabs += tab_true
        natab = np.stack(tabs).reshape(1 + 4 * UNITS, 128, 9216)
        in_maps.append(dict(shared, xe=xe, mem=mem, natab=natab,
                            cflag=np.full((128, 1), 1.0 if c < 2 else 0.0, np.float32)))
    nc, _ = build_program()
    res = run_bass_kernel_spmd(nc, in_maps, core_ids=list(range(NCORES)))
    y_prompt = np.empty((16, UT, D), np.float32)
    y_sample = np.empty((2, 4 * UT, D), np.float32)
    for c in range(NCORES):
        y = np.asarray(res.results[c]["y"]).reshape(UNITS, UT, D)
        for u in range(UNITS):
            if c < 2:
                y_sample[c, u * UT:(u + 1) * UT] = y[u]
            elif (c, u) in prompt_of:
                y_prompt[prompt_of[(c, u)]] = y[u]
    return (y_prompt, y_sample)
```
